# Optimizing a Trainium2 kernel written in Bass

```python
import math
import jax
import jax.numpy as jnp
from jax import lax
import numpy as np

D_MODEL = 1024
BATCH = 4
SEQ = 8192
DEPTH = 1
DEC_BATCH = 8
DEC_SEQ = 16
PAST_LEN = 4096

CHUNK = 64
Q_BLOCK = 128
NORM_EPS = 1e-6
MIX_WIDTH = D_MODEL
RWKV_WIDTH = MIX_WIDTH // 2
RWKV_HEAD = 64
RWKV_HEADS = RWKV_WIDTH // RWKV_HEAD
W_LORA = 64
A_LORA = 64
G_LORA = 128
RWKV_IN = 3 * RWKV_WIDTH + W_LORA + A_LORA + G_LORA
RW_SPLITS = (RWKV_WIDTH, 2 * RWKV_WIDTH, 3 * RWKV_WIDTH, 3 * RWKV_WIDTH + W_LORA, 3 * RWKV_WIDTH + W_LORA + A_LORA)
GN_EPS = 64e-5
DIFF_WIDTH = MIX_WIDTH - RWKV_WIDTH
DIFF_HEADS = 4
DIFF_HALF = DIFF_WIDTH // DIFF_HEADS // 2
DIFF_KDIM = 2 * DIFF_HALF
DIFF_VDIM = 2 * DIFF_HALF
DIFF_IN = 3 * DIFF_WIDTH
IN_DIM = RWKV_IN + DIFF_IN
ROPE_THETA = 500000.0
ROPE_DIM = DIFF_HALF // 4
D_FF = 2816
FF_CONV = 3

kernel_name = 'hymba_rwkv7_diffattn_convffn_stream_step'


def rms_norm(x, g, eps=NORM_EPS):
    xf = x.astype(jnp.float32)
    y = xf * lax.rsqrt(jnp.mean(xf * xf, axis=-1, keepdims=True) + eps)
    return (y * g.astype(jnp.float32)).astype(x.dtype)


def rope_partial(x, pos):
    half = ROPE_DIM // 2
    inv = ROPE_THETA ** (-jnp.arange(0, ROPE_DIM, 2, dtype=jnp.float32) / ROPE_DIM)
    ang = pos.astype(jnp.float32)[:, None] * inv[None, :]
    cos = jnp.cos(ang)[None, :, None, None, :]
    sin = jnp.sin(ang)[None, :, None, None, :]
    xf = x.astype(jnp.float32)
    x1 = xf[..., :half]
    x2 = xf[..., half:ROPE_DIM]
    out = jnp.concatenate([x1 * cos - x2 * sin, x2 * cos + x1 * sin, xf[..., ROPE_DIM:]], axis=-1)
    return out.astype(x.dtype)


def rwkv7_mix(p, shift_prev, wkv0, lw):
    B, T, _ = p.shape
    H, N = RWKV_HEADS, RWKV_HEAD
    f32 = jnp.float32
    pf = p.astype(f32)
    prev = jnp.concatenate([shift_prev.astype(f32), pf[:, :-1]], axis=1)
    z = pf + (prev - pf) * lw['rw_mu'].astype(f32)
    r, k, v, wl, al, gl = jnp.split(z, RW_SPLITS, axis=-1)
    w_log = -jax.nn.softplus(-(lw['rw_w0'].astype(f32) + jnp.tanh(wl) @ lw['rw_w_w2'].astype(f32))) - 0.5
    decay = jnp.exp(-jnp.exp(w_log))
    a = jax.nn.sigmoid(lw['rw_a0'].astype(f32) + al @ lw['rw_a_w2'].astype(f32))
    g = jax.nn.sigmoid(gl) @ lw['rw_g_w2'].astype(f32)
    kk = (k * lw['rw_k_k'].astype(f32)).reshape(B, T, H, N)
    kk = kk / jnp.maximum(jnp.sqrt(jnp.sum(kk * kk, axis=-1, keepdims=True)), 1e-12)
    k = k * (1.0 + (a - 1.0) * lw['rw_k_a'].astype(f32))
    r, decay, k, v, a = [t.reshape(B, T, H, N) for t in (r, decay, k, v, a)]
    avec = -kk
    bvec = kk * a

    def step(S, inp):
        r_t, w_t, k_t, v_t, a_t, b_t = inp
        sa = jnp.einsum('bhij,bhj->bhi', S, a_t)
        S = S * w_t[:, :, None, :] + sa[..., None] * b_t[:, :, None, :] + v_t[..., None] * k_t[:, :, None, :]
        y_t = jnp.einsum('bhij,bhj->bhi', S, r_t)
        return S, y_t

    xs = tuple(t.transpose(1, 0, 2, 3) for t in (r, decay, k, v, avec, bvec))
    S_T, ys = lax.scan(step, wkv0.astype(f32), xs)
    y = ys.transpose(1, 0, 2, 3)
    mu = jnp.mean(y, axis=-1, keepdims=True)
    var = jnp.mean(jnp.square(y - mu), axis=-1, keepdims=True)
    y = ((y - mu) * lax.rsqrt(var + GN_EPS)).reshape(B, T, RWKV_WIDTH)
    y = y * lw['rw_gn_g'].astype(f32) + lw['rw_gn_b'].astype(f32)
    bonus = jnp.sum(r * k * lw['rw_r_k'].astype(f32), axis=-1, keepdims=True) * v
    y = (y + bonus.reshape(B, T, RWKV_WIDTH)) * g
    return y.astype(p.dtype), S_T, p[:, -1:]


def diff_project(p, pos, lw):
    B, T, _ = p.shape
    q, k, v = jnp.split(p, [DIFF_WIDTH, 2 * DIFF_WIDTH], axis=-1)
    q = q.reshape(B, T, DIFF_HEADS, 2, DIFF_HALF)
    k = k.reshape(B, T, DIFF_HEADS, 2, DIFF_HALF)
    v = v.reshape(B, T, DIFF_HEADS, DIFF_VDIM)
    q = rope_partial(rms_norm(q, lw['df_q_g']), pos)
    k = rope_partial(rms_norm(k, lw['df_k_g']), pos)
    return q, k, v


def diff_attend(q, k, v, q_chunk, k_chunk, lam):
    scale = DIFF_HALF ** -0.5
    s = jnp.einsum('bqhcd,bkhcd->bchqk', q.astype(jnp.float32), k.astype(jnp.float32)) * scale
    mask = k_chunk[None, :] <= q_chunk[:, None]
    s = jnp.where(mask, s, -1e30)
    pr = jax.nn.softmax(s, axis=-1)
    att = pr[:, 0] - lam * pr[:, 1]
    return jnp.einsum('bhqk,bkhd->bqhd', att, v.astype(jnp.float32))


def causal_dwconv(u_pad, w, b):
    T = u_pad.shape[1] - (FF_CONV - 1)
    out = b
    for i in range(FF_CONV):
        out = out + u_pad[:, i:i + T] * w[i]
    return out


def layer_forward(x, pos, past_k, past_v, shift_prev, wkv0, conv_prev, lw, layer_idx):
    B, T, _ = x.shape
    h = rms_norm(x, lw['norm1_g'])
    p = h @ lw['w_in']
    y_rw, wkv_new, shift_new = rwkv7_mix(p[..., :RWKV_IN], shift_prev, wkv0, lw)
    q, k, v = diff_project(p[..., RWKV_IN:], pos, lw)
    lam_init = 0.8 - 0.6 * math.exp(-0.3 * layer_idx)
    f32 = jnp.float32
    lam = (jnp.exp(jnp.sum(lw['df_lq1'].astype(f32) * lw['df_lk1'].astype(f32)))
           - jnp.exp(jnp.sum(lw['df_lq2'].astype(f32) * lw['df_lk2'].astype(f32))) + lam_init)
    q_chunk = pos // CHUNK
    if past_k is None:
        nb = T // Q_BLOCK
        q_blocks = q.reshape(B, nb, Q_BLOCK, DIFF_HEADS, 2, DIFF_HALF).swapaxes(0, 1)
        qc_blocks = q_chunk.reshape(nb, Q_BLOCK)

        def attend_block(args):
            qb, qcb = args
            return diff_attend(qb, k, v, qcb, q_chunk, lam)

        o = lax.map(attend_block, (q_blocks, qc_blocks))
        o = o.swapaxes(0, 1).reshape(B, T, DIFF_HEADS, DIFF_VDIM)
    else:
        P = past_k.shape[1]
        k_all = jnp.concatenate([past_k.reshape(B, P, DIFF_HEADS, 2, DIFF_HALF).astype(k.dtype), k], axis=1)
        v_all = jnp.concatenate([past_v.astype(v.dtype), v], axis=1)
        k_chunk = jnp.concatenate([jnp.arange(P, dtype=jnp.int32) // CHUNK, q_chunk])
        o = diff_attend(q, k_all, v_all, q_chunk, k_chunk, lam)
    o = rms_norm(o, lw['df_subln_g']) * (1.0 - lam_init)
    y_df = o.reshape(B, T, DIFF_WIDTH).astype(x.dtype)
    x = x + jnp.concatenate([y_rw, y_df], axis=-1) @ lw['w_out']
    h2 = rms_norm(x, lw['norm2_g'])
    u = h2 @ lw['w_up']
    u_pad = jnp.concatenate([conv_prev.astype(u.dtype), u], axis=1)
    c = causal_dwconv(u_pad, lw['conv_w'], lw['conv_b'])
    gate, up = jnp.split(c, [D_FF], axis=-1)
    x = x + (jax.nn.silu(gate) * up) @ lw['w_down']
    conv_new = u_pad[:, -(FF_CONV - 1):]
    return x, k.reshape(B, T, DIFF_HEADS, DIFF_KDIM), v, shift_new, wkv_new, conv_new


def setup_inputs(seed: int = 0) -> dict:
    key = jax.random.key(seed)
    ks = list(jax.random.split(key, 40))
    f32 = jnp.float32
    L = DEPTH

    def nrm(shape, s):
        return jax.random.normal(ks.pop(), shape, f32) * s

    def uni(shape, lo, hi):
        return jax.random.uniform(ks.pop(), shape, f32, minval=lo, maxval=hi)

    return {
        'x_prompt': nrm((BATCH, SEQ, D_MODEL), 1.0),
        'x_sample': nrm((DEC_BATCH, DEC_SEQ, D_MODEL), 1.0),
        'cache_k': nrm((L, DEC_BATCH, PAST_LEN, DIFF_HEADS, DIFF_KDIM), 1.0),
        'cache_v': nrm((L, DEC_BATCH, PAST_LEN, DIFF_HEADS, DIFF_VDIM), 1.0),
        'state_shift': nrm((L, DEC_BATCH, 1, RWKV_IN), 1.0),
        'state_wkv': nrm((L, DEC_BATCH, RWKV_HEADS, RWKV_HEAD, RWKV_HEAD), 0.5),
        'state_ffn_conv': nrm((L, DEC_BATCH, FF_CONV - 1, 2 * D_FF), 1.0),
        'norm1_g': 1.0 + nrm((L, D_MODEL), 0.02),
        'w_in': nrm((L, D_MODEL, IN_DIM), D_MODEL ** -0.5),
        'rw_mu': uni((L, RWKV_IN), 0.0, 1.0),
        'rw_w0': uni((L, RWKV_WIDTH), -6.0, 0.0),
        'rw_w_w2': nrm((L, W_LORA, RWKV_WIDTH), 0.1),
        'rw_a0': nrm((L, RWKV_WIDTH), 0.1),
        'rw_a_w2': nrm((L, A_LORA, RWKV_WIDTH), 0.1),
        'rw_g_w2': nrm((L, G_LORA, RWKV_WIDTH), G_LORA ** -0.5),
        'rw_k_k': 0.85 + nrm((L, RWKV_WIDTH), 0.02),
        'rw_k_a': 1.0 + nrm((L, RWKV_WIDTH), 0.02),
        'rw_r_k': nrm((L, RWKV_HEADS, RWKV_HEAD), 0.1),
        'rw_gn_g': 1.0 + nrm((L, RWKV_WIDTH), 0.02),
        'rw_gn_b': nrm((L, RWKV_WIDTH), 0.02),
        'df_q_g': 1.0 + nrm((L, DIFF_HALF), 0.02),
        'df_k_g': 1.0 + nrm((L, DIFF_HALF), 0.02),
        'df_lq1': nrm((L, DIFF_HALF), 0.1),
        'df_lk1': nrm((L, DIFF_HALF), 0.1),
        'df_lq2': nrm((L, DIFF_HALF), 0.1),
        'df_lk2': nrm((L, DIFF_HALF), 0.1),
        'df_subln_g': 1.0 + nrm((L, DIFF_VDIM), 0.02),
        'w_out': nrm((L, MIX_WIDTH, D_MODEL), MIX_WIDTH ** -0.5),
        'norm2_g': 1.0 + nrm((L, D_MODEL), 0.02),
        'w_up': nrm((L, D_MODEL, 2 * D_FF), D_MODEL ** -0.5),
        'conv_w': nrm((L, FF_CONV, 2 * D_FF), 0.5),
        'conv_b': nrm((L, 2 * D_FF), 0.02),
        'w_down': nrm((L, D_FF, D_MODEL), D_FF ** -0.5),
    }


def reference(x_prompt, x_sample, cache_k, cache_v, state_shift, state_wkv, state_ffn_conv,
              norm1_g, w_in, rw_mu, rw_w0, rw_w_w2, rw_a0, rw_a_w2, rw_g_w2, rw_k_k, rw_k_a, rw_r_k,
              rw_gn_g, rw_gn_b, df_q_g, df_k_g, df_lq1, df_lk1, df_lq2, df_lk2, df_subln_g, w_out,
              norm2_g, w_up, conv_w, conv_b, w_down):
    Bp, Tp, _ = x_prompt.shape
    Bs, Ts, _ = x_sample.shape
    P = cache_k.shape[2]
    pos_p = jnp.arange(Tp, dtype=jnp.int32)
    pos_s = P + jnp.arange(Ts, dtype=jnp.int32)
    xp, xs = x_prompt, x_sample
    kp_l, vp_l, shp_l, wkp_l, fcp_l = [], [], [], [], []
    ks_l, vs_l, shs_l, wks_l, fcs_l = [], [], [], [], []
    for l in range(DEPTH):
        lw = {
            'norm1_g': norm1_g[l], 'w_in': w_in[l], 'rw_mu': rw_mu[l], 'rw_w0': rw_w0[l],
            'rw_w_w2': rw_w_w2[l], 'rw_a0': rw_a0[l], 'rw_a_w2': rw_a_w2[l], 'rw_g_w2': rw_g_w2[l],
            'rw_k_k': rw_k_k[l], 'rw_k_a': rw_k_a[l], 'rw_r_k': rw_r_k[l], 'rw_gn_g': rw_gn_g[l],
            'rw_gn_b': rw_gn_b[l], 'df_q_g': df_q_g[l], 'df_k_g': df_k_g[l], 'df_lq1': df_lq1[l],
            'df_lk1': df_lk1[l], 'df_lq2': df_lq2[l], 'df_lk2': df_lk2[l], 'df_subln_g': df_subln_g[l],
            'w_out': w_out[l], 'norm2_g': norm2_g[l], 'w_up': w_up[l], 'conv_w': conv_w[l],
            'conv_b': conv_b[l], 'w_down': w_down[l],
        }
        xp, kp, vp, shp, wkp, fcp = layer_forward(
            xp, pos_p, None, None,
            jnp.zeros((Bp, 1, RWKV_IN), x_prompt.dtype),
            jnp.zeros((Bp, RWKV_HEADS, RWKV_HEAD, RWKV_HEAD), jnp.float32),
            jnp.zeros((Bp, FF_CONV - 1, 2 * D_FF), x_prompt.dtype), lw, l)
        xs, ksn, vsn, shs, wks, fcs = layer_forward(
            xs, pos_s, cache_k[l], cache_v[l], state_shift[l], state_wkv[l], state_ffn_conv[l], lw, l)
        kp_l.append(kp); vp_l.append(vp); shp_l.append(shp); wkp_l.append(wkp); fcp_l.append(fcp)
        ks_l.append(ksn); vs_l.append(vsn); shs_l.append(shs); wks_l.append(wks); fcs_l.append(fcs)
    return (xp, xs,
            jnp.stack(kp_l), jnp.stack(vp_l), jnp.stack(shp_l), jnp.stack(wkp_l), jnp.stack(fcp_l),
            jnp.stack(ks_l), jnp.stack(vs_l), jnp.stack(shs_l), jnp.stack(wks_l), jnp.stack(fcs_l))
```

```python
import contextlib
import math
import numpy as np
import concourse.bass as bass
import concourse.mybir as mybir
from concourse.bass_utils import run_bass_kernel_spmd

F32 = mybir.dt.float32
BF16 = mybir.dt.bfloat16
AF = mybir.ActivationFunctionType
ALU = mybir.AluOpType
AX = mybir.AxisListType

PE, ACT, DVE, POOL, SP = "tensor", "scalar", "vector", "gpsimd", "sync"
ENGINES = [PE, ACT, DVE, POOL, SP]

D = 1024
RW = 512
RWKV_IN = 1792
IN_DIM = 3328
DFF = 2816
NORM_EPS = 1e-6
GN_EPS = 64e-5
ROPE_THETA = 500000.0
DEC_C = math.exp(-0.5)
LAM_INIT = 0.8 - 0.6 * math.exp(-0.3 * 0)
NEG = -30000.0


class Buf:
    __slots__ = ("name", "last_w", "readers", "psum")

    def __init__(self, name, psum=False):
        self.name = name
        self.last_w = None
        self.readers = []
        self.psum = psum


class V:
    __slots__ = ("ap", "buf")

    def __init__(self, ap, buf):
        self.ap = ap
        self.buf = buf

    def __getitem__(self, k):
        return V(self.ap[k], self.buf)

    def rr(self, s, **kw):
        return V(self.ap.rearrange(s, **kw), self.buf)

    def bc(self, dt):
        return V(self.ap.bitcast(dt), self.buf)

    def us(self, ax):
        return V(self.ap.unsqueeze(ax), self.buf)

    def tb(self, shape):
        return V(self.ap.to_broadcast(list(shape)), self.buf)

    def alias(self, name):
        return V(self.ap, Buf(name))


class Op:
    __slots__ = ("eng", "fn", "reads", "writes", "dma_sem", "deps", "sig", "idx", "acc", "dma_cnt", "bar")

    def __init__(self, eng, fn, reads, writes, dma_sem, acc):
        self.eng = eng
        self.fn = fn
        self.reads = reads
        self.writes = writes
        self.dma_sem = dma_sem
        self.deps = None
        self.sig = None
        self.acc = acc
        self.dma_cnt = None
        self.bar = None


class Prog:
    def __init__(self):
        self.ops = []
        self.group_sems = set()

    def op(self, eng, fn, reads=(), writes=(), dma_sem=None, acc=False):
        def bufs(vs):
            out = []
            for v in vs:
                if v is None:
                    continue
                if isinstance(v.buf, (list, tuple)):
                    out.extend(v.buf)
                else:
                    out.append(v.buf)
            return out
        o = Op(eng, fn, bufs(reads), bufs(writes), dma_sem, acc)
        o.idx = len(self.ops)
        self.ops.append(o)
        return o

    def barrier(self):
        if self.ops:
            self.ops[-1].bar = True

    def schedule(self):
        last_on = {e: None for e in ENGINES}
        last_dma = {}
        pend = {e: [] for e in ENGINES}
        for o in self.ops:
            deps = set(pend[o.eng])
            pend[o.eng] = []
            for b in o.reads:
                if b.last_w is not None:
                    deps.add(b.last_w)
            for b in o.writes:
                if b.last_w is not None:
                    deps.add(b.last_w)
                deps.update(b.readers)
            deps.discard(o)
            o.deps = deps
            for b in o.reads:
                if b.psum and b.readers and b.readers[0].eng != o.eng:
                    raise AssertionError(f"PSUM bank {b.name} read by two engines ({b.readers[0].eng}, {o.eng})")
                b.readers.append(o)
            for b in o.writes:
                b.last_w = o
                b.readers = []
            if o.dma_sem is not None:
                last_dma[id(o.dma_sem)] = o
            else:
                last_on[o.eng] = o
            if o.bar:
                allp = [x for x in last_on.values() if x is not None] + list(last_dma.values())
                for e in ENGINES:
                    pend[e] = list(allp)
        known = {e: {p: -1 for p in ENGINES} for e in ENGINES}
        dma_known = {e: {} for e in ENGINES}
        waits = []
        for o in self.ops:
            need = {}
            for d in o.deps:
                if d.dma_sem is not None:
                    key = ("dma", id(d.dma_sem))
                    if d.idx > dma_known[o.eng].get(key[1], -1):
                        cur = need.get(key)
                        if cur is None or d.idx > cur.idx:
                            need[key] = d
                else:
                    if d.eng == PE and o.eng == PE and d.acc and o.acc:
                        continue
                    if d.idx > known[o.eng][d.eng]:
                        key = ("eng", d.eng)
                        cur = need.get(key)
                        if cur is None or d.idx > cur.idx:
                            need[key] = d
            wl = []
            for (kind, key), d in need.items():
                wl.append(d)
                if kind == "dma":
                    dma_known[o.eng][key] = d.idx
                else:
                    known[o.eng][d.eng] = d.idx
                    d.sig = True
            waits.append(wl)
        self.waits = waits

    def emit(self, nc, sems):
        self.schedule()
        cnt = {e: 0 for e in ENGINES}
        dcnt = {}
        for o in self.ops:
            if o.dma_sem is not None:
                k = id(o.dma_sem)
                dcnt[k] = dcnt.get(k, 0) + 16
                o.dma_cnt = dcnt[k]
            elif o.sig:
                cnt[o.eng] += 1
                o.sig = cnt[o.eng]
        for o in self.ops:
            if o.dma_sem is not None and id(o.dma_sem) in self.group_sems:
                o.dma_cnt = dcnt[id(o.dma_sem)]
        print('SEMCOUNTS', cnt, 'max dma', max(dcnt.values()) if dcnt else 0, 'nops', len(self.ops), flush=True)
        for e in ENGINES:
            assert cnt[e] < 60000, (e, cnt[e])
        for v in dcnt.values():
            assert v < 60000, v
        per_eng = {e: [o for o in self.ops if o.eng == e] for e in ENGINES}
        last_dma = {}
        for o in self.ops:
            if o.dma_sem is not None:
                last_dma[id(o.dma_sem)] = o
        waits = self.waits

        with nc.Block() as block:
            def body(engname):
                def f(eng):
                    for o in per_eng[engname]:
                        for d in waits[o.idx]:
                            if d.dma_sem is not None:
                                eng.wait_ge(d.dma_sem, d.dma_cnt)
                            else:
                                eng.wait_ge(sems[d.eng], d.sig)
                        ins = o.fn(eng)
                        if o.dma_sem is not None:
                            ins.then_inc(o.dma_sem, 16)
                        elif o.sig:
                            ins.then_inc(sems[o.eng], 1)
                    if engname == SP:
                        for o in last_dma.values():
                            eng.wait_ge(o.dma_sem, o.dma_cnt)
                return f
            block.tensor(body(PE))
            block.scalar(body(ACT))
            block.vector(body(DVE))
            block.gpsimd(body(POOL))
            block.sync(body(SP))


class Ctx:
    def __init__(self, nc, es, arena_words=53000):
        self.nc, self.es = nc, es
        self.P = Prog()
        self.sems = {e: es.enter_context(nc.semaphore("s_" + e)) for e in ENGINES}
        self.arena = es.enter_context(nc.sbuf_tensor("arena", [128, arena_words], F32))
        self.words = arena_words
        self.top = 0
        self.banks = []
        self.ps = es.enter_context(nc.psum_tensor("psall", [128, 4096], F32))
        for i in range(8):
            self.banks.append(V(self.ps[:, i * 512:(i + 1) * 512], Buf(f"bank{i}", psum=True)))
        self.bi = 0
        self.nsem = 0
        self.lsem = {}
        self.ssem = {}
        self.gsem = self.newsem()
        self.P.group_sems.add(id(self.gsem))
        self.rot = {ACT: 0}

    def newsem(self):
        if getattr(self, "free_sems", None):
            return self.free_sems.pop()
        self.nsem += 1
        return self.es.enter_context(self.nc.semaphore(f"d{self.nsem}"))

    def recycle(self):
        if not hasattr(self, "free_sems"):
            self.free_sems = []
        self.free_sems.extend(self.lsem.values())
        self.free_sems.extend(self.ssem.values())
        self.lsem = {}
        self.ssem = {}

    def sb(self, name, shape, dt=F32, at=None):
        n = 1
        for s in shape[1:]:
            n *= s
        words = n if dt == F32 else (n + 1) // 2
        words = (words + 7) // 8 * 8
        if at is None:
            assert self.top + words <= self.words, (name, self.top, words)
            off = self.top
            self.top += words
        else:
            off = at
        self.last_alloc = (off, words)
        ap = self.arena[:, off:off + words]
        if dt == BF16:
            ap = ap.bitcast(BF16)
        ap = ap[:, 0:n]
        if len(shape) == 3:
            ap = ap.rearrange("p (a b) -> p a b", b=shape[2])
        elif len(shape) == 4:
            ap = ap.rearrange("p (a b c) -> p a b c", b=shape[2], c=shape[3])
        if shape[0] != 128:
            ap = ap[0:shape[0]]
        return V(ap, Buf(name))

    def bankpair(self, i):
        ap = self.ps[:, i * 512:(i + 2) * 512].rearrange("p (b n) -> p b n", b=2)
        return V(ap, [self.banks[i].buf, self.banks[i + 1].buf])

    def bank(self, lo=0, hi=8):
        n = hi - lo
        b = self.banks[lo + (self.bi % n)]
        self.bi += 1
        return b

    def mm(self, out, lhsT, rhs, start=True, stop=True):
        self.P.op(PE, lambda e: e.matmul(out.ap, lhsT=lhsT.ap, rhs=rhs.ap, start=start, stop=stop),
                  [lhsT, rhs], [out], acc=True)

    def tr(self, out, in_, ident):
        self.P.op(PE, lambda e: e.transpose(out.ap, in_.ap, ident.ap), [in_, ident], [out], acc=True)

    def act(self, out, in_, func, bias=None, scale=None, accum=None):
        kw = {}
        reads = [in_]
        if bias is not None:
            kw["bias"] = bias.ap if isinstance(bias, V) else float(bias)
            if isinstance(bias, V):
                reads.append(bias)
        if scale is not None:
            kw["scale"] = scale.ap if isinstance(scale, V) else float(scale)
            if isinstance(scale, V):
                reads.append(scale)
        writes = [out]
        if accum is not None:
            kw["accum_out"] = accum.ap
            writes.append(accum)
        self.P.op(ACT, lambda e: e.activation(out=out.ap, in_=in_.ap, func=func, **kw), reads, writes)

    def tt(self, eng, out, a, b, op):
        self.P.op(eng, lambda e: e.tensor_tensor(out=out.ap, in0=a.ap, in1=b.ap, op=op), [a, b], [out])

    def ts(self, eng, out, a, s1, op0, s2=None, op1=None):
        reads = [a] + [s for s in (s1, s2) if isinstance(s, V)]
        v1 = s1.ap if isinstance(s1, V) else float(s1)
        v2 = None if s2 is None else (s2.ap if isinstance(s2, V) else float(s2))
        if op1 is None:
            self.P.op(eng, lambda e: e.tensor_scalar(out=out.ap, in0=a.ap, scalar1=v1, scalar2=None, op0=op0),
                      reads, [out])
        else:
            self.P.op(eng, lambda e: e.tensor_scalar(out=out.ap, in0=a.ap, scalar1=v1, scalar2=v2, op0=op0,
                                                      op1=op1), reads, [out])

    def stt(self, out, a, s, b, op0, op1):
        reads = [a, b] + ([s] if isinstance(s, V) else [])
        sv = s.ap if isinstance(s, V) else float(s)
        self.P.op(DVE, lambda e: e.scalar_tensor_tensor(out=out.ap, in0=a.ap, scalar=sv, in1=b.ap, op0=op0,
                                                         op1=op1), reads, [out])

    def cp(self, eng, out, in_):
        if eng == ACT:
            self.act(out, in_, AF.Copy)
        else:
            self.P.op(eng, lambda e: e.tensor_copy(out=out.ap, in_=in_.ap), [in_], [out])

    def memset(self, eng, out, val):
        self.P.op(eng, lambda e: e.memset(out.ap, float(val)), [], [out])

    def fence(self, dummy, vs):
        self.P.op(POOL, lambda e: e.memset(dummy.ap, 0.0), [], [dummy] + list(vs))

    def recip(self, out, in_):
        self.P.op(DVE, lambda e: e.reciprocal(out=out.ap, in_=in_.ap), [in_], [out])

    def reduce(self, out, in_, op=ALU.add):
        self.P.op(DVE, lambda e: e.tensor_reduce(out=out.ap, in_=in_.ap, axis=AX.X, op=op), [in_], [out])

    def scan(self, out, d0, d1):
        self.P.op(DVE, lambda e: e.tensor_tensor_scan(out=out.ap, data0=d0.ap, data1=d1.ap, initial=0.0,
                                                       op0=ALU.mult, op1=ALU.add), [d0, d1], [out])

    def load(self, dst, src_ap, eng=SP, group=False):
        if group:
            sem = self.gsem
        else:
            k = id(dst.buf)
            if k not in self.lsem:
                self.lsem[k] = self.newsem()
            sem = self.lsem[k]
        self.P.op(eng, lambda e: e.dma_start(out=dst.ap, in_=src_ap), [], [dst], dma_sem=sem)

    def store(self, dst_ap, src, eng=POOL):
        k = id(src.buf)
        if k not in self.ssem:
            self.ssem[k] = self.newsem()
        sem = self.ssem[k]
        self.P.op(eng, lambda e: e.dma_start(out=dst_ap, in_=src.ap), [src], [], dma_sem=sem)


class Ring:
    def __init__(self, K, name, n, shape, dt=F32):
        self.slots = [K.sb(f"{name}{i}", shape, dt) for i in range(n)]
        self.i = 0

    def next(self):
        s = self.slots[self.i % len(self.slots)]
        self.i += 1
        return s


C_G1, C_G2, C_MU, C_W0, C_A0, C_KK, C_KA, C_RK, C_GNG, C_GNB, C_CW, C_CB, C_FLAG, C_NEGK = (
    0, 8, 16, 30, 34, 38, 42, 46, 50, 54, 58, 190, 234, 235)
NCOL = 236
R_GQK, R_GSUB, R_LQ, R_LK = 0, 512, 640, 768
NROW = 896


class NS:
    pass


def blkview(x, c):
    return x.rr("p (f c t) -> p f c t", f=4, c=2, t=64)[:, :, c]


def setup_common(K, dr):
    G = NS()
    G.cols = K.sb("cols", [128, NCOL])
    G.rows = K.sb("rows", [128, NROW])
    G.ident = K.sb("ident", [128, 128])
    G.masks = K.sb("masks", [128, 5, 512])
    G.identb = K.sb("identb", [128, 128], BF16)
    G.bonesb = K.sb("bonesb", [128, 128], BF16)
    G.valid = K.sb("valid", [128, 64])
    G.oka = K.sb("oka", [128, 4])
    G.lam = K.sb("lam", [128, 4])
    G.gsub = K.sb("gsub", [128, 128])
    tmpb = K.sb("tmpb", [128, 128])
    K.load(G.cols, dr["cols"], group=True)
    K.load(G.rows, dr["rows"][0:1, :].partition_broadcast(128), group=True)
    K.load(G.ident, dr["c_ident"], group=True)
    K.load(G.masks, dr["c_masks"], group=True)
    K.load(tmpb, dr["c_bones"], group=True)
    K.load(G.valid, dr["c_valid"], group=True)
    K.cp(DVE, G.identb, G.ident)
    K.cp(DVE, G.bonesb, tmpb)
    K.ts(POOL, G.oka, G.cols[:, C_KA:C_KA + 4], -1.0, ALU.mult, 1.0, ALU.add)
    prod = K.sb("lamprod", [128, 128])
    K.tt(DVE, prod, G.rows[:, R_LQ:R_LQ + 128], G.rows[:, R_LK:R_LK + 128], ALU.mult)
    K.reduce(G.lam[:, 0:2], prod.rr("p (a b) -> p a b", b=64))
    K.act(G.lam[:, 0:2], G.lam[:, 0:2], AF.Exp)
    K.tt(DVE, G.lam[:, 2:3], G.lam[:, 0:1], G.lam[:, 1:2], ALU.subtract)
    K.ts(DVE, G.lam[:, 3:4], G.lam[:, 2:3], LAM_INIT, ALU.add, -1.0, ALU.mult)
    K.ts(POOL, G.gsub, G.rows[:, R_GSUB:R_GSUB + 128], 1.0 - LAM_INIT, ALU.mult)
    K.ts(POOL, G.rows[:, R_GQK:R_GQK + 256], G.rows[:, R_GQK:R_GQK + 256], 0.125, ALU.mult)
    G.junk = K.sb("junk", [128, 1024], BF16)
    G.epsc = K.sb("epsc", [128, 4])
    K.memset(POOL, G.epsc[:, 0:1], NORM_EPS)
    K.memset(POOL, G.epsc[:, 1:2], GN_EPS)
    K.memset(POOL, G.epsc[:, 2:3], 1e-30)
    K.memset(POOL, G.epsc[:, 3:4], 0.0)
    return G


def load_weight(K, dst, src, rows, ncols, stage, scale_cols=None, col0=0):
    engs = [ACT, DVE, POOL] if scale_cols is None else [ACT, DVE]
    i = 0
    for kc in range(rows // 128):
        for c0 in range(0, ncols, 1792):
            cw = min(1792, ncols - c0)
            st = stage.next()
            K.load(st[:, 0:cw], src[kc * 128:(kc + 1) * 128, col0 + c0:col0 + c0 + cw])
            eng = engs[i % len(engs)]
            i += 1
            d = dst[:, kc, c0:c0 + cw]
            if scale_cols is None:
                K.cp(eng, d, st[:, 0:cw])
            elif eng == ACT:
                K.act(d, st[:, 0:cw], AF.Copy, scale=scale_cols[:, kc:kc + 1])
            else:
                K.ts(eng, d, st[:, 0:cw], scale_cols[:, kc:kc + 1], ALU.mult)


def rmsnorm_to_hT(K, G, S, x_rows, TPt, hT, col0):
    xt = S.x.next()
    K.load(xt[0:TPt], x_rows)
    ss = S.ss.next()
    K.act(G.junk.alias("j")[0:TPt], xt[0:TPt], AF.Square, accum=ss[0:TPt, 0:1])
    K.act(ss[0:TPt, 1:2], ss[0:TPt, 0:1], AF.Sqrt, scale=1.0 / D, bias=G.epsc[0:TPt, 0:1])
    K.recip(ss[0:TPt, 2:3], ss[0:TPt, 1:2])
    hb = S.hb.next()
    K.act(hb[0:TPt], xt[0:TPt], AF.Copy, scale=ss[0:TPt, 2:3])
    bk = K.bank()
    bkb = bk.bc(BF16)
    for kc in range(8):
        K.tr(bkb[:, kc * 128:kc * 128 + TPt], hb[0:TPt, kc * 128:(kc + 1) * 128], G.identb[0:TPt, 0:TPt])
    K.cp(DVE, hT[:, :, col0:col0 + TPt], bkb.rr("p (k t) -> p k t", t=128)[:, :, 0:TPt])
    return xt


def blockmm(K, lhs, rhs, fcs_cols, NC, lhs3=None, rhs3=None):
    bk = K.bank()
    for fc in range(4):
        for c in range(NC):
            blk = fc * 2 + c
            for hp in range(2):
                ph = slice(hp * 64, hp * 64 + 64)
                bc = slice(blk * 64, blk * 64 + 64)
                if lhs3 is not None:
                    cs = slice(lhs3[1] + c * 64, lhs3[1] + c * 64 + 64)
                    a = lhs3[0][ph, fc, cs]
                else:
                    a = lhs[ph, bc]
                if rhs3 is not None:
                    cs = slice(rhs3[1] + c * 64, rhs3[1] + c * 64 + 64)
                    b = rhs3[0][ph, fc, cs]
                else:
                    b = rhs[ph, bc]
                K.mm(bk[ph, bc], a, b)
    return bk


def sweepR_alloc(K, G, dr):
    S = NS()
    S.Wr = K.sb("Wr", [128, 8, RWKV_IN], BF16)
    S.Wl = K.sb("Wl", [128, 2, RW], BF16)
    S.Wg = K.sb("Wg", [128, RW], BF16)
    mark = K.top
    stage = Ring(K, "wst", 2, [128, 1792])
    load_weight(K, S.Wr, dr["w_in"], 1024, RWKV_IN, stage, scale_cols=G.cols[:, C_G1:C_G1 + 8])
    st = stage.next()
    K.load(st[0:64, 0:RW], dr["w_w2"])
    K.load(st[64:128, 0:RW], dr["a_w2"])
    K.cp(DVE, S.Wl[0:64, 0], st[0:64, 0:RW])
    K.cp(DVE, S.Wl[64:128, 1], st[64:128, 0:RW])
    st = stage.next()
    K.load(st[:, 0:RW], dr["g_w2"])
    K.cp(DVE, S.Wg, st[:, 0:RW])
    K.P.barrier()
    K.recycle()
    K.top = mark
    S.x = Ring(K, "x", 2, [128, D])
    S.ss = Ring(K, "ss", 4, [128, 4])
    S.hb = Ring(K, "hb", 2, [128, D], BF16)
    S.hT = Ring(K, "hT", 2, [128, 8, 512], BF16)
    S.pS = Ring(K, "pS", 2, [128, 520])
    S.zT = K.sb("zT", [128, 14, 512])
    S.lora = K.sb("lora", [128, 512], BF16)
    S.sgl = K.sb("sgl", [128, 512], BF16)
    for n in ("aT", "rT", "bT", "kT", "vT", "rk", "g_sb", "yout"):
        setattr(S, n, K.sb(n, [128, 4, 512], BF16))
    for n in ("lwr", "alr", "cum", "E", "Einv", "cump", "kk", "nrm", "tfac", "bvec"):
        setattr(S, n, K.sb(n, [128, 512]))
    S.Eprev, S.rn, S.kkn, S.kmod = S.cump, S.nrm, S.kk, S.tfac
    S.kk2 = K.sb("kk2", [128, 512], BF16)
    S.gC = K.sb("gC", [128, 4, 8])
    TILE_BF = ("Vtok", "Ktok", "Btok", "M", "Nm", "AKT", "RBT", "RKT", "Pm", "Qm", "M2a", "N2a")
    S.sets = []
    for si in range(2):
        B = NS()
        for n in TILE_BF:
            setattr(B, n, K.sb(f"{n}_{si}", [128, 512], BF16))
        S.sets.append(B)
    shared = {}
    for n in ("RHSb", "Ub", "ynb"):
        shared[n] = K.sb(n, [128, 512], BF16)
    for n, shp in (("Ysb", [128, 512]), ("Ysq", [128, 512]), ("yst", [128, 8, 8]), ("t1", [128, 4, 128]),
                   ("t2", [128, 4, 128])):
        shared[n] = K.sb(n, shp)
    for B in S.sets:
        for n, v in shared.items():
            setattr(B, n, v)
    S.wkvio = K.sb("wkvio", [128, 256])
    return S


def new_seq(K, name):
    q = NS()
    q.ST = K.sb(name + "ST", [128, 256])
    q.STb = K.sb(name + "STb", [128, 256], BF16)
    q.carry = K.sb(name + "carry", [128, 14])
    return q


def sweepR_mt(K, G, S, dr, seq, x_ap, N, hT_dram, tok0, mine, ym_dram=None, ym0=0, nvalid=None):
    cols = G.cols
    TPt = min(N, 128)
    NTT = max(1, N // 128)
    NC = 2 if N >= 128 else 1
    NCH = N // 64
    valid = G.valid if nvalid is not None else None
    nv = N if nvalid is None else nvalid
    hT = S.hT.next()
    for i in range(NTT):
        rmsnorm_to_hT(K, G, S, x_ap[i * 128:i * 128 + TPt, :], TPt, hT, i * 128)
    K.store(hT_dram[:, :, tok0:tok0 + N], hT[:, :, 0:N])
    if STOP_AT <= 1:
        return
    for fc in range(14):
        bk = K.bank()
        for kc in range(8):
            K.mm(bk[:, 0:N], S.Wr[:, kc, fc * 128:(fc + 1) * 128], hT[:, kc, 0:N], start=(kc == 0), stop=(kc == 7))
        pS = S.pS.next()
        K.cp(POOL, pS[:, 0:1], seq.carry[:, fc:fc + 1])
        K.cp(ACT, pS[:, 1:N + 1], bk[:, 0:N])
        K.cp(POOL, seq.carry[:, fc:fc + 1], pS[:, nv:nv + 1])
        z = S.zT[:, fc, 0:N]
        K.tt(POOL, z, pS[:, 0:N], pS[:, 1:N + 1], ALU.subtract)
        K.stt(z, z, cols[:, C_MU + fc:C_MU + fc + 1], pS[:, 1:N + 1], ALU.mult, ALU.add)
        if valid is not None:
            K.tt(POOL, z, z, valid[:, 0:N], ALU.mult)
    if STOP_AT <= 2:
        return
    K.act(S.lora[0:64, 0:N], S.zT[0:64, 12, 0:N], AF.Tanh)
    K.cp(POOL, S.lora[64:128, 0:N], S.zT[64:128, 12, 0:N])
    K.act(S.sgl[:, 0:N], S.zT[:, 13, 0:N], AF.Sigmoid)
    for fc in range(4):
        fs = slice(fc * 128, (fc + 1) * 128)
        zr, zk, zv = S.zT[:, fc, 0:N], S.zT[:, 4 + fc, 0:N], S.zT[:, 8 + fc, 0:N]
        lwr, alr, cum, E, Einv, cump, Eprev = (x[:, 0:N] for x in (S.lwr, S.alr, S.cum, S.E, S.Einv, S.cump, S.Eprev))
        kk, nrm, rn, kkn, tfac, kmod, bvec = (x[:, 0:N] for x in (S.kk, S.nrm, S.rn, S.kkn, S.tfac, S.kmod, S.bvec))
        K.act(kk, zk, AF.Copy, scale=cols[:, C_KK + fc:C_KK + fc + 1])
        K.tt(POOL, S.kk2[:, 0:N], kk, kk, ALU.mult)
        b4 = K.bank()
        K.mm(b4[:, 0:N], G.bonesb, S.kk2[:, 0:N])
        b1 = K.bank()
        K.mm(b1[:, 0:N], S.Wl[0:64, 0, fs], S.lora[0:64, 0:N])
        b2 = K.bank()
        K.mm(b2[:, 0:N], S.Wl[64:128, 1, fs], S.lora[64:128, 0:N])
        b3 = K.bank()
        K.mm(b3[:, 0:N], S.Wg[:, fs], S.sgl[:, 0:N])
        K.act(lwr, b1[:, 0:N], AF.Sigmoid, bias=cols[:, C_W0 + fc:C_W0 + fc + 1])
        if valid is not None:
            K.tt(POOL, lwr, lwr, valid[:, 0:N], ALU.mult)
        K.scan(cum, G.masks[:, 4, 0:N], lwr)
        K.act(nrm, b4[:, 0:N], AF.Sqrt, bias=G.epsc[:, 2:3])
        K.recip(rn, nrm)
        K.act(alr, b2[:, 0:N], AF.Sigmoid, bias=cols[:, C_A0 + fc:C_A0 + fc + 1])
        K.tt(DVE, kkn, kk, rn, ALU.mult)
        K.tt(POOL, cump, cum, lwr, ALU.subtract)
        K.act(E, cum, AF.Exp, scale=-DEC_C)
        K.act(Eprev, cump, AF.Exp, scale=-DEC_C)
        K.act(Einv, cum, AF.Exp, scale=DEC_C)
        K.tt(POOL, bvec, kkn, alr, ALU.mult)
        K.ts(DVE, tfac, alr, cols[:, C_KA + fc:C_KA + fc + 1], ALU.mult, G.oka[:, fc:fc + 1], ALU.add)
        K.tt(DVE, kmod, zk, tfac, ALU.mult)
        K.stt(S.aT[:, fc, 0:N], kkn, -1.0, Eprev, ALU.mult, ALU.mult)
        K.tt(POOL, S.bT[:, fc, 0:N], bvec, Einv, ALU.mult)
        K.tt(DVE, S.rT[:, fc, 0:N], zr, E, ALU.mult)
        K.tt(DVE, S.kT[:, fc, 0:N], kmod, Einv, ALU.mult)
        K.stt(S.rk[:, fc, 0:N], zr, cols[:, C_RK + fc:C_RK + fc + 1], kmod, ALU.mult, ALU.mult)
        K.cp(POOL, S.gC[:, fc, 0:NCH], E.rr("p (c t) -> p c t", t=64)[:, :, 63])
        K.cp(POOL, S.vT[:, fc, 0:N], zv)
        K.cp(ACT, S.g_sb[:, fc, 0:N], b3[:, 0:N])
    mk = G.masks
    YB = K.banks[7]

    def bankR():
        return K.bank(0, 7)

    def bmm(lhs, rhs, lhs3=None, rhs3=None):
        bk = bankR()
        for fc in range(4):
            for c in range(NC):
                blk = fc * 2 + c
                for hp in range(2):
                    ph = slice(hp * 64, hp * 64 + 64)
                    bc = slice(blk * 64, blk * 64 + 64)
                    if lhs3 is not None:
                        a_ = lhs3[0][ph, fc, lhs3[1] + c * 64:lhs3[1] + c * 64 + 64]
                    else:
                        a_ = lhs[ph, bc]
                    if rhs3 is not None:
                        b_ = rhs3[0][ph, fc, rhs3[1] + c * 64:rhs3[1] + c * 64 + 64]
                    else:
                        b_ = rhs[ph, bc]
                    K.mm(bk[ph, bc], a_, b_)
        return bk

    def part1(tt, B):
        base = tt * 128
        for dst, src in ((B.Vtok, S.vT), (B.Ktok, S.kT), (B.Btok, S.bT)):
            bk = bankR()
            bkb = bk.bc(BF16)
            for fc in range(4):
                for c in range(NC):
                    blk = fc * 2 + c
                    for hp in range(2):
                        ph = slice(hp * 64, hp * 64 + 64)
                        K.tr(bkb[ph, blk * 64:blk * 64 + 64], src[ph, fc, base + c * 64:base + c * 64 + 64],
                             G.identb[ph, ph])
            K.cp(ACT, dst, bkb[:, 0:512])
            yield
        for dst, l3, r3, mi in ((B.M, S.bT, S.aT, 0), (B.Nm, S.aT, S.bT, 1), (B.AKT, S.kT, S.aT, 0),
                                (B.RBT, S.bT, S.rT, 2), (B.RKT, S.kT, S.rT, 2)):
            bk = bmm(None, None, lhs3=(l3, base), rhs3=(r3, base))
            K.tt(DVE, dst, bk, mk[:, mi], ALU.mult)
            yield
        K.tt(DVE, B.Pm, B.M, mk[:, 3], ALU.add)
        K.tt(DVE, B.Qm, B.Nm, mk[:, 3], ALU.add)
        Mc, Nc = B.M, B.Nm
        for lvl in range(5):
            last = lvl == 4
            M2 = B.M2a if lvl % 2 == 0 else B.M
            N2 = B.N2a if lvl % 2 == 0 else B.Nm
            bM2 = bmm(Nc, Mc)
            K.cp(ACT, M2, bM2)
            yield
            if not last:
                bN2 = bmm(Mc, Nc)
                K.cp(ACT, N2, bN2)
                yield
            bPM = bmm(B.Qm, M2)
            K.tt(DVE, B.Pm, B.Pm, bPM, ALU.add)
            yield
            if not last:
                bNQ = bmm(M2, B.Qm)
                K.tt(DVE, B.Qm, B.Qm, bNQ, ALU.add)
                yield
            Mc, Nc = M2, N2

    def part2(tt, B):
        base = tt * 128
        for c in range(NC):
            cg = tt * 2 + c
            cs = slice(base + c * 64, base + c * 64 + 64)
            bR = bankR()
            for fc in range(4):
                bc = slice((fc * 2 + c) * 64, (fc * 2 + c) * 64 + 64)
                for hp in range(2):
                    ph = slice(hp * 64, hp * 64 + 64)
                    K.mm(bR[ph, bc], B.AKT[ph, bc], B.Vtok[ph, bc], start=True, stop=False)
                    K.mm(bR[ph, bc], S.aT[ph, fc, cs], seq.STb[ph, fc * 64:fc * 64 + 64], start=False, stop=True)
            K.cp(DVE, blkview(B.RHSb, c), blkview(bR, c))
            yield
            bU = bankR()
            for fc in range(4):
                bc = slice((fc * 2 + c) * 64, (fc * 2 + c) * 64 + 64)
                for hp in range(2):
                    ph = slice(hp * 64, hp * 64 + 64)
                    K.mm(bU[ph, bc], B.Pm[ph, bc], B.RHSb[ph, bc])
            K.cp(ACT, blkview(B.Ub, c), blkview(bU, c))
            yield
            bS = bankR()
            for fc in range(4):
                bc = slice((fc * 2 + c) * 64, (fc * 2 + c) * 64 + 64)
                for hp in range(2):
                    ph = slice(hp * 64, hp * 64 + 64)
                    if mine:
                        K.mm(YB[ph, bc], S.rT[ph, fc, cs], seq.STb[ph, fc * 64:fc * 64 + 64], start=True, stop=False)
                        K.mm(YB[ph, bc], B.RBT[ph, bc], B.Ub[ph, bc], start=False, stop=False)
                        K.mm(YB[ph, bc], B.RKT[ph, bc], B.Vtok[ph, bc], start=False, stop=True)
                    K.mm(bS[ph, fc * 64:fc * 64 + 64], B.Btok[ph, bc], B.Ub[ph, bc], start=True, stop=False)
                    K.mm(bS[ph, fc * 64:fc * 64 + 64], B.Ktok[ph, bc], B.Vtok[ph, bc], start=False, stop=True)
            K.tt(DVE, seq.ST, seq.ST, bS[:, 0:256], ALU.add)
            ST3 = seq.ST.rr("p (f i) -> p f i", i=64)
            K.tt(DVE, ST3, ST3, S.gC[:, :, cg].us(2).tb([128, 4, 64]), ALU.mult)
            K.cp(ACT, seq.STb, seq.ST)
            yield
        if not mine:
            return
        K.cp(ACT, B.Ysb, YB)
        Y3 = B.Ysb.rr("p (b i) -> p b i", i=64)
        st = B.yst
        K.reduce(st[:, 0], Y3)
        K.tt(POOL, B.Ysq, B.Ysb, B.Ysb, ALU.mult)
        K.reduce(st[:, 1], B.Ysq.rr("p (b i) -> p b i", i=64))
        yield
        K.ts(DVE, st[:, 2], st[:, 0], 1.0 / 64, ALU.mult)
        K.tt(DVE, st[:, 3], st[:, 2], st[:, 2], ALU.mult)
        K.stt(st[:, 4], st[:, 1], 1.0 / 64, st[:, 3], ALU.mult, ALU.subtract)
        K.act(st[:, 5], st[:, 4], AF.Sqrt, bias=G.epsc[:, 1:2])
        K.recip(st[:, 6], st[:, 5])
        K.tt(DVE, Y3, Y3, st[:, 2].us(2).tb([128, 8, 64]), ALU.subtract)
        K.tt(DVE, B.ynb.rr("p (b i) -> p b i", i=64), Y3, st[:, 6].us(2).tb([128, 8, 64]), ALU.mult)
        yield
        bk = bankR()
        bkb = bk.bc(BF16)
        for fc in range(4):
            for c in range(NC):
                bc = slice((fc * 2 + c) * 64, (fc * 2 + c) * 64 + 64)
                for hp in range(2):
                    ph = slice(hp * 64, hp * 64 + 64)
                    K.tr(bkb[ph, bc], B.ynb[ph, bc], G.identb[ph, ph])
        yT = bkb[:, 0:512].rr("p (f t) -> p f t", f=4)
        for fc in range(4):
            K.act(B.t1[:, fc, 0:TPt], yT[:, fc, 0:TPt], AF.Identity, scale=cols[:, C_GNG + fc:C_GNG + fc + 1],
                  bias=cols[:, C_GNB + fc:C_GNB + fc + 1])
        yield
        bB = bankR()
        for fc in range(4):
            K.mm(bB[:, fc * 128:fc * 128 + TPt], G.bonesb, S.rk[:, fc, base:base + TPt])
        K.tt(DVE, B.t2[:, :, 0:TPt], bB.rr("p (f t) -> p f t", f=4)[:, :, 0:TPt], S.vT[:, :, base:base + TPt], ALU.mult)
        K.tt(POOL, B.t1[:, :, 0:TPt], B.t1[:, :, 0:TPt], B.t2[:, :, 0:TPt], ALU.add)
        K.tt(DVE, S.yout[:, :, base:base + TPt], B.t1[:, :, 0:TPt], S.g_sb[:, :, base:base + TPt], ALU.mult)
        yield

    def run_rr(gens):
        active = [g for g in gens if g is not None]
        while active:
            for g in list(active):
                try:
                    next(g)
                except StopIteration:
                    active.remove(g)

    p1 = [part1(tt, S.sets[tt % 2]) for tt in range(NTT)]
    p2 = [part2(tt, S.sets[tt % 2]) for tt in range(NTT)]
    run_rr([p1[0]])
    for tt in range(1, NTT):
        run_rr([p2[tt - 1], p1[tt]])
    run_rr([p2[NTT - 1]])
    if mine and STOP_AT > 7:
        K.store(ym_dram[:, 0:4, ym0:ym0 + nv], S.yout[:, :, 0:nv])


def seq_init_zero(K, seq):
    K.memset(POOL, seq.ST, 0.0)
    K.memset(POOL, seq.STb, 0.0)
    K.memset(POOL, seq.carry, 0.0)


def seq_init_state(K, G, S, seq, wkv_ap, shift_ap):
    K.load(S.wkvio, wkv_ap)
    K.load(seq.carry, shift_ap)
    bk = K.bank()
    for pr in range(2):
        K.mm(bk[:, pr * 128:pr * 128 + 128], S.wkvio[:, pr * 128:pr * 128 + 128], G.ident)
    K.cp(ACT, seq.ST, bk[:, 0:256])
    K.cp(DVE, seq.STb, seq.ST)


def seq_final(K, G, S, seq, wkv_out_ap, shift_out_ap):
    bk = K.bank()
    for pr in range(2):
        K.mm(bk[:, pr * 128:pr * 128 + 128], seq.ST[:, pr * 128:pr * 128 + 128], G.ident)
    K.cp(ACT, S.wkvio, bk[:, 0:256])
    K.store(wkv_out_ap, S.wkvio)
    K.store(shift_out_ap, seq.carry)


DEBUG = False
STOP_AT = 99


def build_program(TP, TM, PAST, stages=("R", "H", "F")):
    nc = bass.Bass("TRN2", target_bir_lowering=False)
    dr = {}

    def inp(name, shape, dt=F32):
        dr[name] = nc.dram_tensor(name, list(shape), dt, kind="ExternalInput").ap()

    def outp(name, shape, dt=F32):
        dr[name] = nc.dram_tensor(name, list(shape), dt, kind="ExternalOutput").ap()

    def scr(name, shape, dt=BF16):
        dr[name] = nc.dram_tensor(name, list(shape), dt, kind="ExternalOutput" if DEBUG else "Internal").ap()

    NTP, NTM = TP // 128, TM // 128
    inp("x_prev", [TP, D]); inp("x_mine", [TM, D]); inp("x_smp", [64, D])
    inp("cache_k", [PAST, 512]); inp("cache_v", [PAST, 512])
    inp("st_shift", [128, 14]); inp("st_wkv", [128, 256]); inp("st_conv", [128, 88])
    inp("w_in", [D, IN_DIM]); inp("w_out", [D, D]); inp("w_up", [D, 2 * DFF]); inp("w_down", [DFF, D])
    inp("w_w2", [64, RW]); inp("a_w2", [64, RW]); inp("g_w2", [128, RW])
    inp("cols", [128, NCOL]); inp("rows", [1, NROW])
    inp("c_ident", [128, 128]); inp("c_masks", [128, 5 * 512]); inp("c_bones", [128, 128]); inp("c_valid", [128, 64])
    inp("cs_prev", [128, NTP * 16]); inp("cs_mine", [128, NTM * 16]); inp("cs_smp", [128, 16])
    outp("y_mine", [TM, D]); outp("y_smp", [16, D])
    outp("nk_mine", [TM, 512]); outp("nv_mine", [TM, 512])
    outp("nshift", [128, 14]); outp("nwkv", [128, 256]); outp("nconv", [88, 128])
    outp("nk_smp", [16, 512]); outp("nv_smp", [16, 512])
    outp("nshift_s", [128, 14]); outp("nwkv_s", [128, 256]); outp("nconv_s", [88, 128])
    scr("hT_p", [128, 8, TP + TM]); scr("hT_s", [128, 8, 64])
    scr("ymT_p", [128, 8, TM]); scr("ymT_s", [128, 8, 64])
    scr("wup_bf", [8, 128, 8 * 768])

    with contextlib.ExitStack() as es:
        K = Ctx(nc, es)
        G = setup_common(K, dr)
        base_top = K.top
        if "R" in stages:
            S = sweepR_alloc(K, G, dr)
            pq = new_seq(K, "p")
            sq = new_seq(K, "s")
            seq_init_zero(K, pq)
            for m in range((TP + TM) // 512 if STOP_AT >= 1 else 0):
                t0 = m * 512
                if t0 < TP:
                    sweepR_mt(K, G, S, dr, pq, dr["x_prev"][t0:t0 + 512, :], 512, dr["hT_p"], t0, False)
                else:
                    sweepR_mt(K, G, S, dr, pq, dr["x_mine"][t0 - TP:t0 - TP + 512, :], 512, dr["hT_p"], t0, True,
                              ym_dram=dr["ymT_p"], ym0=t0 - TP)
            if STOP_AT >= 0.5:
                seq_final(K, G, S, pq, dr["nwkv"], dr["nshift"])
                seq_init_state(K, G, S, sq, dr["st_wkv"], dr["st_shift"])
            if STOP_AT >= 1:
                sweepR_mt(K, G, S, dr, sq, dr["x_smp"], 64, dr["hT_s"], 0, True, ym_dram=dr["ymT_s"], ym0=0, nvalid=16)
            if STOP_AT >= 0.5:
                seq_final(K, G, S, sq, dr["nwkv_s"], dr["nshift_s"])
            K.P.barrier()
            K.recycle()
            K.top = base_top
        wup_done = False
        if "H" in stages:
            for hp2 in range(2):
                import os
                HSTOP = int(os.environ.get("HSTOP", 9))
                S = sweepH_alloc(K, G, dr, hp2, TP, TM, PAST)
                conv = None
                if False and hp2 == 1 and "F" in stages:
                    cst = Ring(K, "cst", 2, [128, 768])
                    cob = Ring(K, "cob", 2, [128, 768], BF16)
                    conv = wup_convert(K, G, dr, cst, cob)
                    wup_done = True
                nmt = (TP + TM) // 512
                per = -(-64 // nmt)
                for m in range(nmt if HSTOP >= 2 else 0):
                    sweepH_mt(K, G, S, dr, hp2, m, TP, TM, m * 512 >= TP)
                    if conv is not None:
                        for _ in range(per):
                            next(conv, None)
                if conv is not None:
                    for _ in conv:
                        pass
                if HSTOP >= 4:
                    sweepH_sample(K, G, S, dr, hp2, PAST)
                K.P.barrier()
                K.recycle()
                K.top = base_top
        if "F" in stages:
            S = phaseF_alloc(K, G, dr, wup_done)
            ucp = K.sb("ucp", [128, 88])
            ucs = K.sb("ucs", [128, 88])
            K.memset(POOL, ucp, 0.0)
            K.load(ucs, dr["st_conv"])
            for m in range(TM // 512):
                phaseF_mt(K, G, S, dr, ucp, dr["ymT_p"][:, :, m * 512:(m + 1) * 512],
                          dr["x_mine"][m * 512:(m + 1) * 512, :], dr["y_mine"][m * 512:(m + 1) * 512, :], 512)
            conv_out(K, G, S, ucp, dr["nconv"])
            phaseF_mt(K, G, S, dr, ucs, dr["ymT_s"][:, :, 0:16], dr["x_smp"][0:16, :], dr["y_smp"], 16)
            conv_out(K, G, S, ucs, dr["nconv_s"])
        K.P.emit(nc, K.sems)
    return nc


def _consts():
    p = np.arange(128)[:, None]
    c = np.arange(512)[None, :]
    s, y = p % 64, c % 64
    masks = np.stack([(s < y), (s > y), (s <= y), (s == y), np.broadcast_to(y != 0, (128, 512))], axis=1)
    masks = masks.astype(np.float32).reshape(128, 5 * 512)
    q = np.arange(128)[None, :]
    bones = ((p // 64) == (q // 64)).astype(np.float32)
    valid = np.broadcast_to((np.arange(64)[None, :] < 16), (128, 64)).astype(np.float32)
    return masks, bones, valid


def _rope_table(pos):
    inv = (np.float32(ROPE_THETA) ** (-np.arange(0, 16, 2, dtype=np.float32) / np.float32(16))).astype(np.float32)
    ang = (pos.astype(np.float32)[:, None] * inv[None, :]).astype(np.float32)
    t = np.concatenate([np.cos(ang), np.sin(ang)], axis=1).astype(np.float32)
    n = pos.shape[0] // 128
    return np.ascontiguousarray(t.reshape(n, 128, 16).transpose(1, 0, 2).reshape(128, n * 16))


def _colpack(a, nchunk):
    return np.asarray(a, np.float32).reshape(nchunk, 128).T


def prepare_inputs(inp):
    xp = np.asarray(inp["x_prompt"], np.float32)
    xs = np.asarray(inp["x_sample"], np.float32)
    B, T, _ = xp.shape
    TH = T // 2
    TP, TM = TH - 512, TH + 512
    PAST = inp["cache_k"].shape[2]
    masks, bones, valid = _consts()
    ident = np.eye(128, dtype=np.float32)
    f = lambda k: np.asarray(inp[k], np.float32)[0]
    cw = f("conv_w").reshape(3, 44, 128).transpose(2, 0, 1).reshape(128, 132)
    shared_cols = [_colpack(f("norm1_g"), 8), _colpack(f("norm2_g"), 8), _colpack(f("rw_mu"), 14),
                   _colpack(f("rw_w0"), 4), _colpack(f("rw_a0"), 4), _colpack(f("rw_k_k"), 4),
                   _colpack(f("rw_k_a"), 4), _colpack(f("rw_r_k").reshape(-1), 4), _colpack(f("rw_gn_g"), 4),
                   _colpack(f("rw_gn_b"), 4), cw, _colpack(f("conv_b"), 44)]
    negk = np.where(np.arange(128) < 16, 0.0, NEG).astype(np.float32)[:, None]
    rows = np.concatenate([np.tile(f("df_q_g"), 4), np.tile(f("df_k_g"), 4), f("df_subln_g"), f("df_lq1"),
                           f("df_lq2"), f("df_lk1"), f("df_lk2")]).astype(np.float32)[None, :]
    assert rows.shape[1] == NROW
    cs_prev = _rope_table(np.arange(TP))
    cs_smp = _rope_table(np.concatenate([PAST + np.arange(16), np.zeros(112, np.int64)]))
    shared = {
        "w_in": f("w_in"), "w_out": f("w_out"), "w_up": f("w_up"), "w_down": f("w_down"),
        "w_w2": f("rw_w_w2"), "a_w2": f("rw_a_w2"), "g_w2": f("rw_g_w2"), "rows": rows,
        "c_ident": ident, "c_masks": masks, "c_bones": bones, "c_valid": valid,
        "cs_prev": cs_prev, "cs_smp": cs_smp,
    }
    maps = []
    for c in range(8):
        b, g = c // 2, c % 2
        flag = np.full((128, 1), 0.0 if g == 1 else NEG, np.float32)
        cols = np.concatenate(shared_cols + [flag, negk], axis=1).astype(np.float32)
        assert cols.shape[1] == NCOL
        xsm = np.zeros((64, D), np.float32)
        xsm[:16] = xs[c]
        W = np.asarray(inp["state_wkv"], np.float32)[0, c].reshape(2, 2, 2, 64, 64)
        st_wkv = W.transpose(1, 3, 0, 2, 4).reshape(128, 256)
        st_conv = np.asarray(inp["state_ffn_conv"], np.float32)[0, c].reshape(2, 44, 128).transpose(2, 1, 0).reshape(128, 88)
        m = dict(shared)
        m.update({
            "x_prev": np.ascontiguousarray(xp[b, 0:TP]) if g == 1 else np.zeros((TP, D), np.float32),
            "x_mine": np.ascontiguousarray(xp[b, g * TP:g * TP + TM]),
            "x_smp": xsm,
            "cache_k": np.ascontiguousarray(np.asarray(inp["cache_k"], np.float32)[0, c].reshape(PAST, 512)),
            "cache_v": np.ascontiguousarray(np.asarray(inp["cache_v"], np.float32)[0, c].reshape(PAST, 512)),
            "st_shift": np.ascontiguousarray(_colpack(np.asarray(inp["state_shift"], np.float32)[0, c, 0], 14)),
            "st_wkv": np.ascontiguousarray(st_wkv), "st_conv": np.ascontiguousarray(st_conv),
            "cols": np.ascontiguousarray(cols),
            "cs_mine": _rope_table(g * TP + np.arange(TM)),
        })
        maps.append({k: np.ascontiguousarray(v, dtype=np.float32) for k, v in m.items()})
    return maps, (B, T, TH, PAST, TP, TM)


def _unwkv(a):
    return a.reshape(2, 64, 2, 2, 64).transpose(2, 0, 3, 1, 4).reshape(8, 64, 64)


def _unconv(a):
    return a.reshape(44, 2, 128).transpose(1, 0, 2).reshape(2, 2 * DFF)


def assemble(res, dims):
    B, T, TH, PAST, TP, TM = dims
    y_p = np.zeros((B, T, D), np.float32)
    y_s = np.zeros((8, 16, D), np.float32)
    nk_p = np.zeros((1, B, T, 4, 128), np.float32)
    nv_p = np.zeros((1, B, T, 4, 128), np.float32)
    nsh_p = np.zeros((1, B, 1, RWKV_IN), np.float32)
    nwkv_p = np.zeros((1, B, 8, 64, 64), np.float32)
    ncv_p = np.zeros((1, B, 2, 2 * DFF), np.float32)
    nk_s = np.zeros((1, 8, 16, 4, 128), np.float32)
    nv_s = np.zeros((1, 8, 16, 4, 128), np.float32)
    nsh_s = np.zeros((1, 8, 1, RWKV_IN), np.float32)
    nwkv_s = np.zeros((1, 8, 8, 64, 64), np.float32)
    ncv_s = np.zeros((1, 8, 2, 2 * DFF), np.float32)
    for c in range(8):
        r = res[c]
        b, g = c // 2, c % 2
        sl = slice(g * TH, (g + 1) * TH)
        ms = slice(0, TH) if g == 0 else slice(TM - TH, TM)
        y_p[b, sl] = r["y_mine"][ms]
        nk_p[0, b, sl] = r["nk_mine"][ms].reshape(TH, 4, 128)
        nv_p[0, b, sl] = r["nv_mine"][ms].reshape(TH, 4, 128)
        if g == 1:
            nsh_p[0, b, 0] = r["nshift"].T.reshape(-1)
            nwkv_p[0, b] = _unwkv(r["nwkv"])
            ncv_p[0, b] = _unconv(r["nconv"])
        y_s[c] = r["y_smp"]
        nk_s[0, c] = r["nk_smp"].reshape(16, 4, 128)
        nv_s[0, c] = r["nv_smp"].reshape(16, 4, 128)
        nsh_s[0, c, 0] = r["nshift_s"].T.reshape(-1)
        nwkv_s[0, c] = _unwkv(r["nwkv_s"])
        ncv_s[0, c] = _unconv(r["nconv_s"])
    return (y_p, y_s, nk_p, nv_p, nsh_p, nwkv_p, ncv_p, nk_s, nv_s, nsh_s, nwkv_s, ncv_s)


_CACHE = {}


def run(inputs, stages=("R", "H", "F")):
    maps, dims = prepare_inputs(inputs)
    B, T, TH, PAST, TP, TM = dims
    key = (TP, TM, PAST, tuple(stages), DEBUG)
    if key not in _CACHE:
        _CACHE[key] = build_program(TP, TM, PAST, stages)
    nc = _CACHE[key]
    res = run_bass_kernel_spmd(nc, maps, core_ids=list(range(8)))
    return res.results, dims


def kernel(**inputs):
    res, dims = run(inputs)
    return assemble(res, dims)


def sweepH_alloc(K, G, dr, hp2, TP, TM, PAST):
    S = NS()
    NT = (TP + TM) // 128
    NPT = PAST // 128
    S.Wq = K.sb("Wq", [128, 8, 768], BF16)
    mark = K.top
    stage = Ring(K, "wstH", 2, [128, 1792])
    for part in range(3):
        col0 = RWKV_IN + part * 512 + hp2 * 256
        load_weight(K, S.Wq[:, :, part * 256:(part + 1) * 256], dr["w_in"], 1024, 256, stage,
                    scale_cols=G.cols[:, C_G1:C_G1 + 8], col0=col0)
    K.P.barrier()
    K.recycle()
    K.top = mark
    S.KT = K.sb("KT", [128, 2, TP + TM], BF16)
    S.Va = K.sb("Va", [128, NT, 2, 130], BF16)
    S.KTs = K.sb("KTs", [128, 2, PAST + 128], BF16)
    S.Vs = K.sb("Vs", [128, NPT + 1, 2, 130], BF16)
    K.memset(POOL, S.Va[:, :, :, 128:129], 1.0)
    K.memset(POOL, S.Vs[:, :, :, 128:129], 1.0)
    K.memset(POOL, S.KTs[:, :, PAST:PAST + 128], 0.0)
    K.memset(POOL, S.Vs[:, NPT, :, 0:128], 0.0)
    S.KTm = [S.KT[:, :, m * 512:(m + 1) * 512].alias(f"KTm{m}") for m in range((TP + TM) // 512)]
    S.Vam = [S.Va[:, m * 4:(m + 1) * 4].alias(f"Vam{m}") for m in range((TP + TM) // 512)]
    S.cs_prev = K.sb("cs_prev", [128, TP // 128, 16])
    S.cs_mine = K.sb("cs_mine", [128, TM // 128, 16])
    S.cs_smp = K.sb("cs_smp", [128, 1, 16])
    K.load(S.cs_prev, dr["cs_prev"])
    K.load(S.cs_mine, dr["cs_mine"])
    K.load(S.cs_smp, dr["cs_smp"])
    S.hT = Ring(K, "hTH", 2, [128, 8, 512], BF16)
    S.qk = Ring(K, "qk", 4, [128, 512])
    S.sq = Ring(K, "sqH", 4, [128, 512])
    S.st = Ring(K, "stH", 4, [128, 32])
    S.rt = Ring(K, "ropet", 4, [128, 4, 8, 8])
    S.qkb = Ring(K, "qkb", 4, [128, 512], BF16)
    S.vf = Ring(K, "vf", 4, [128, 256])
    S.QT = Ring(K, "QT", 2, [128, 2, 512], BF16)
    S.PT2 = Ring(K, "PT2", 3, [128, 2, 512], BF16)
    S.o = Ring(K, "oH", 2, [128, 128])
    S.accs = K.sb("accs", [128, 4, 385])
    S.est = Ring(K, "est", 2, [128, 8])
    S.ydf = Ring(K, "ydf", 2, [128, 128], BF16)
    S.ydfT = Ring(K, "ydfT", 2, [128, 2, 512], BF16)
    S.ck = Ring(K, "ck", 2, [128, 256])
    S.ckb = Ring(K, "ckb", 2, [128, 256], BF16)
    return S


def qkv_tile(K, G, S, hT, c0, TPt, cs, want_q, QT, qcol, KTdst, kcol, Vdst, nk_ap, nv_ap, nrows, blo=4):
    bA = K.bank(blo, 8)
    for kc in range(8):
        K.mm(bA[0:TPt, :], hT[:, kc, c0:c0 + TPt], S.Wq[:, kc, 0:512], start=(kc == 0), stop=(kc == 7))
    bB = K.bank(blo, 8)
    for kc in range(8):
        K.mm(bB[0:TPt, 0:256], hT[:, kc, c0:c0 + TPt], S.Wq[:, kc, 512:768], start=(kc == 0), stop=(kc == 7))
    qk = S.qk.next()[0:TPt]
    st = S.st.next()[0:TPt]
    K.cp(ACT, qk, bA[0:TPt, :])
    vf = S.vf.next()[0:TPt]
    K.cp(ACT, vf, bB[0:TPt, 0:256])
    yield
    sq = S.sq.next()[0:TPt]
    K.tt(POOL, sq, qk, qk, ALU.mult)
    K.reduce(st[:, 0:8], sq.rr("p (a b) -> p a b", b=64))
    K.act(st[:, 8:16], st[:, 0:8], AF.Sqrt, scale=1.0 / 64, bias=G.epsc[0:TPt, 0:1])
    K.recip(st[:, 16:24], st[:, 8:16])
    yield
    qk3 = qk.rr("p (a b) -> p a b", b=64)
    K.tt(DVE, qk3, qk3, st[:, 16:24].us(2).tb([TPt, 8, 64]), ALU.mult)
    K.tt(POOL, qk, qk, G.rows[0:TPt, R_GQK:R_GQK + 512], ALU.mult)
    yield
    x1, x2 = qk3[:, :, 0:8], qk3[:, :, 8:16]
    cosb = cs[0:TPt, 0:8].us(1).tb([TPt, 8, 8])
    sinb = cs[0:TPt, 8:16].us(1).tb([TPt, 8, 8])
    rt = S.rt.next()[0:TPt]
    K.tt(DVE, rt[:, 0], x1, cosb, ALU.mult)
    K.tt(POOL, rt[:, 1], x2, sinb, ALU.mult)
    K.tt(DVE, rt[:, 2], x2, cosb, ALU.mult)
    K.tt(POOL, rt[:, 3], x1, sinb, ALU.mult)
    K.tt(DVE, x1, rt[:, 0], rt[:, 1], ALU.subtract)
    K.tt(POOL, x2, rt[:, 2], rt[:, 3], ALU.add)
    yield
    if nk_ap is not None:
        K.store(nk_ap, qk[0:nrows, 256:512])
        K.store(nv_ap, vf[0:nrows])
    qkb = S.qkb.next()[0:TPt]
    K.cp(ACT, qkb, qk)
    K.cp(POOL, Vdst[0:TPt, :, 0:128], vf.rr("p (h d) -> p h d", d=128))
    yield
    bT = K.bank(blo, 8)
    bTb = bT.bc(BF16)
    blocks = range(4) if want_q else range(2, 4)
    for blk in blocks:
        K.tr(bTb[:, blk * 128:blk * 128 + TPt], qkb[:, blk * 128:(blk + 1) * 128], G.identb[0:TPt, 0:TPt])
    b3 = bTb[:, 0:512].rr("p (a t) -> p a t", t=128)
    if want_q:
        K.cp(DVE, QT[:, :, qcol:qcol + TPt], b3[:, 0:2, 0:TPt])
    K.cp(DVE, KTdst[:, :, kcol:kcol + TPt], b3[:, 2:4, 0:TPt])
    yield


def run_rr(gens):
    active = [g for g in gens if g is not None]
    while active:
        for g in list(active):
            try:
                next(g)
            except StopIteration:
                active.remove(g)


def attn_epilogue(K, G, S, acc, rows, ydfT_dst):
    est = S.est.next()[0:rows]
    o = S.o.next()[0:rows]
    K.recip(est[:, 0:1], acc[:, 128:129])
    K.recip(est[:, 1:2], acc[:, 384:385])
    K.tt(DVE, est[:, 1:2], est[:, 1:2], G.lam[0:rows, 3:4], ALU.mult)
    K.ts(DVE, o, acc[:, 0:128], est[:, 0:1], ALU.mult)
    K.stt(o, acc[:, 256:384], est[:, 1:2], o, ALU.mult, ALU.add)
    K.act(G.junk.alias("j")[0:rows, 0:128], o, AF.Square, accum=est[:, 2:3])
    K.act(est[:, 3:4], est[:, 2:3], AF.Sqrt, scale=1.0 / 128, bias=G.epsc[0:rows, 0:1])
    K.recip(est[:, 4:5], est[:, 3:4])
    ydf = S.ydf.next()[0:rows]
    K.stt(ydf, o, est[:, 4:5], G.gsub[0:rows], ALU.mult, ALU.mult)
    bT = K.bank(4, 8)
    bTb = bT.bc(BF16)
    K.tr(bTb[:, 0:rows], ydf, G.identb[0:rows, 0:rows])
    K.cp(ACT, ydfT_dst, bTb[:, 0:rows])


def sweepH_mt(K, G, S, dr, hp2, m, TP, TM, mine):
    NTP = TP // 128
    t0 = m * 512
    hT = S.hT.next()
    K.load(hT, dr["hT_p"][:, :, t0:t0 + 512])
    QT = S.QT.next()
    gens = []
    for i in range(4):
        tile = m * 4 + i
        if mine:
            cs = S.cs_mine[:, tile - NTP]
            r0 = t0 - TP + i * 128
            nk_ap = dr["nk_mine"][r0:r0 + 128, hp2 * 256:(hp2 + 1) * 256]
            nv_ap = dr["nv_mine"][r0:r0 + 128, hp2 * 256:(hp2 + 1) * 256]
        else:
            cs = S.cs_prev[:, tile]
            nk_ap = nv_ap = None
        gens.append(qkv_tile(K, G, S, hT, i * 128, 128, cs, mine, QT, i * 128, S.KTm[m], i * 128, S.Vam[m][:, i],
                             nk_ap, nv_ap, 128, blo=(4 if mine else 0)))
    run_rr(gens[0:2])
    run_rr(gens[2:4])
    import os
    if not mine or int(os.environ.get("HSTOP", 9)) < 3:
        return
    ml = m - TP // 512
    nk = NTP + (ml + 1) * 4
    ydfT = S.ydfT.next()
    for hl in range(2):
        acc = [K.banks[j] for j in range(4)]
        DEPTH = 1

        def stageA(kt):
            ktl = kt - NTP - ml * 4
            q0 = max(0, ktl) * 128
            kc0 = (kt % 4) * 128
            pb = 4 + 2 * (kt % 2)
            for comp in range(2):
                ph = slice(comp * 64, comp * 64 + 64)
                K.mm(K.banks[pb + comp][:, 0:512 - q0], S.KTm[kt // 4][ph, hl, kc0:kc0 + 128], QT[ph, hl, q0:512])

        pts = {}

        def stageB(kt):
            ktl = kt - NTP - ml * 4
            q0 = max(0, ktl) * 128
            Nq = 512 - q0
            pb = 4 + 2 * (kt % 2)
            PT = S.PT2.next()
            pts[kt] = PT
            K.act(PT[:, :, 0:Nq], K.bankpair(pb)[:, :, 0:Nq], AF.Exp,
                  bias=(G.cols[:, C_FLAG:C_FLAG + 1] if kt < NTP else None))
            if ktl >= 0:
                K.memset(POOL, PT[64:128, :, 0:64], 0.0)

        def stageC(kt):
            ktl = kt - NTP - ml * 4
            q0 = max(0, ktl) * 128
            Vs = S.Vam[kt // 4]
            PT = pts.pop(kt)
            for comp in range(2):
                for j in range(q0 // 128, 4):
                    lastkt = NTP + ml * 4 + j
                    K.mm(acc[j][:, comp * 256:comp * 256 + 129], PT[:, comp, j * 128 - q0:j * 128 - q0 + 128],
                         Vs[:, kt % 4, hl, 0:129], start=(kt == 0 and comp == 0),
                         stop=(kt == lastkt and comp == 1))

        for n in range(nk + 2):
            if n < nk:
                stageA(n)
            if 0 <= n - 1 < nk:
                stageB(n - 1)
            if n - 2 >= 0:
                stageC(n - 2)
        for j in range(4):
            K.cp(DVE, S.accs[:, j, :], acc[j][:, 0:385])
        for j in range(4):
            attn_epilogue(K, G, S, S.accs[:, j], 128, ydfT[:, hl, j * 128:(j + 1) * 128])
    fo = 4 + hp2 * 2
    K.store(dr["ymT_p"][:, fo:fo + 2, t0 - TP:t0 - TP + 512], ydfT)


def sweepH_sample(K, G, S, dr, hp2, PAST):
    NPT = PAST // 128
    hT = S.hT.next()
    K.load(hT[:, :, 0:64], dr["hT_s"])
    QT = S.QT.next()
    run_rr([qkv_tile(K, G, S, hT, 0, 64, S.cs_smp[:, 0], True, QT, 0, S.KTs, PAST, S.Vs[:, NPT],
                     dr["nk_smp"][:, hp2 * 256:(hp2 + 1) * 256], dr["nv_smp"][:, hp2 * 256:(hp2 + 1) * 256], 16)])
    import os
    SSTOP = int(os.environ.get("SSTOP", 9))
    for ct in range(NPT if SSTOP >= 2 else 0):
        ck = S.ck.next()
        K.load(ck, dr["cache_k"][ct * 128:(ct + 1) * 128, hp2 * 256:(hp2 + 1) * 256])
        ckb = S.ckb.next()
        K.cp(POOL, ckb, ck)
        bT = K.bank(4, 8)
        bTb = bT.bc(BF16)
        for hl in range(2):
            K.tr(bTb[:, hl * 128:(hl + 1) * 128], ckb[:, hl * 128:(hl + 1) * 128], G.identb)
        K.cp(DVE, S.KTs[:, :, ct * 128:(ct + 1) * 128], bTb[:, 0:256].rr("p (a t) -> p a t", t=128))
        cv = S.ck.next()
        K.load(cv, dr["cache_v"][ct * 128:(ct + 1) * 128, hp2 * 256:(hp2 + 1) * 256])
        K.cp(POOL, S.Vs[:, ct, :, 0:128], cv.rr("p (h d) -> p h d", d=128))
    ydfT = S.ydfT.next()
    for hl in range(2):
        acc = K.banks[hl]
        nk = NPT + 1
        pts = {}

        def sA(kt):
            pb = 4 + 2 * (kt % 2)
            for comp in range(2):
                ph = slice(comp * 64, comp * 64 + 64)
                K.mm(K.banks[pb + comp][:, 0:128], S.KTs[ph, hl, kt * 128:kt * 128 + 128], QT[ph, hl, 0:128])

        def sB(kt):
            pb = 4 + 2 * (kt % 2)
            PT = S.PT2.next()
            pts[kt] = PT
            K.act(PT[:, :, 0:128], K.bankpair(pb)[:, :, 0:128], AF.Exp,
                  bias=(G.cols[:, C_NEGK:C_NEGK + 1] if kt == NPT else None))

        def sC(kt):
            PT = pts.pop(kt)
            for comp in range(2):
                K.mm(acc[:, comp * 256:comp * 256 + 129], PT[:, comp, 0:128], S.Vs[:, kt, hl, 0:129],
                     start=(kt == 0 and comp == 0), stop=(kt == NPT and comp == 1))

        for n in range(nk + 2):
            if n < nk:
                sA(n)
            if 0 <= n - 1 < nk:
                sB(n - 1)
            if n - 2 >= 0:
                sC(n - 2)
        if SSTOP >= 4:
            attn_epilogue(K, G, S, acc[0:16], 16, ydfT[:, hl, 0:16])
    fo = 4 + hp2 * 2
    if True:
        K.store(dr["ymT_s"][:, fo:fo + 2, 0:16], ydfT[:, :, 0:16])


FF_PIECES = [(0, 6), (6, 6), (12, 6), (18, 4)]


def wup_convert(K, G, dr, stage, ost):
    i = 0
    for pi, (g0, n) in enumerate(FF_PIECES):
        for isup in range(2):
            c0 = (isup * 22 + g0) * 128
            for kc in range(8):
                st = stage.next()
                K.load(st[:, 0:n * 128], dr["w_up"][kc * 128:(kc + 1) * 128, c0:c0 + n * 128])
                ob = ost.next()
                sc = G.cols[:, C_G2 + kc:C_G2 + kc + 1]
                if i % 2 == 0:
                    K.act(ob[:, 0:n * 128], st[:, 0:n * 128], AF.Copy, scale=sc)
                else:
                    K.ts(DVE, ob[:, 0:n * 128], st[:, 0:n * 128], sc, ALU.mult)
                i += 1
                K.store(dr["wup_bf"][pi * 2 + isup, :, kc * 768:kc * 768 + n * 128], ob[:, 0:n * 128])
                yield


def phaseF_alloc(K, G, dr, wup_done=False):
    S = NS()
    S.Wo = K.sb("Wo", [128, 8, D], BF16)
    S.Wd = K.sb("Wd", [128, 22, D], BF16)
    mark = K.top
    stage = Ring(K, "wstF", 2, [128, 1792])
    load_weight(K, S.Wo, dr["w_out"], 1024, D, stage)
    load_weight(K, S.Wd, dr["w_down"], DFF, D, stage)
    if not wup_done:
        ost = Ring(K, "wupo", 2, [128, 768], BF16)
        for _ in wup_convert(K, G, dr, stage, ost):
            pass
    K.P.barrier()
    K.recycle()
    K.top = mark
    S.wb = Ring(K, "wb", 3, [128, 8, 768], BF16)
    S.ymT = K.sb("ymTF", [128, 8, 512], BF16)
    S.x = Ring(K, "xF", 2, [128, D])
    S.xmid = K.sb("xmid", [128, 4, D])
    S.ss = Ring(K, "ssF", 4, [128, 4])
    S.h2b = K.sb("h2b", [128, D], BF16)
    S.h2T = K.sb("h2T", [128, 8, 512], BF16)
    S.uS = Ring(K, "uS", 3, [128, 520])
    S.ct = Ring(K, "ct", 3, [128, 512])
    S.sg = K.sb("sg", [128, 6, 512], BF16)
    S.mT = K.sb("mT", [128, 22, 512], BF16)
    S.cvo = K.sb("cvo", [128, 128])
    return S


def phaseF_mt(K, G, S, dr, ucarry, ym_ap, x_ap, y_ap, N):
    cols = G.cols
    TPt = min(N, 128)
    NTT = max(1, N // 128)
    K.load(S.ymT[:, :, 0:N], ym_ap)
    xm_tiles = []
    for i in range(NTT):
        xt = S.x.next()
        K.load(xt[0:TPt], x_ap[i * 128:i * 128 + TPt, :])
        xm = S.xmid[0:TPt, i].alias(f"xmid{i}")
        xm_tiles.append(xm)
        for half in range(2):
            hs = slice(half * 512, (half + 1) * 512)
            bk = K.bank()
            for kc in range(8):
                K.mm(bk[0:TPt, :], S.ymT[:, kc, i * 128:i * 128 + TPt], S.Wo[:, kc, hs], start=(kc == 0), stop=(kc == 7))
            K.tt(DVE, xm[:, hs], xt[0:TPt, hs], bk[0:TPt, :], ALU.add)
        ss = S.ss.next()
        K.act(G.junk.alias("j")[0:TPt], xm, AF.Square, accum=ss[0:TPt, 0:1])
        K.act(ss[0:TPt, 1:2], ss[0:TPt, 0:1], AF.Sqrt, scale=1.0 / D, bias=G.epsc[0:TPt, 0:1])
        K.recip(ss[0:TPt, 2:3], ss[0:TPt, 1:2])
        K.act(S.h2b[0:TPt], xm, AF.Copy, scale=ss[0:TPt, 2:3])
        bk = K.bank()
        bkb = bk.bc(BF16)
        for kc in range(8):
            K.tr(bkb[:, kc * 128:kc * 128 + TPt], S.h2b[0:TPt, kc * 128:(kc + 1) * 128], G.identb[0:TPt, 0:TPt])
        K.cp(DVE, S.h2T[:, :, i * 128:i * 128 + TPt], bkb.rr("p (k t) -> p k t", t=128)[:, :, 0:TPt])
    for pi, (g0, n) in enumerate(FF_PIECES):
        for isup in range(2):
            wb = S.wb.next()
            pending = None
            K.load(wb[:, :, 0:n * 128],
                   dr["wup_bf"][pi * 2 + isup].rearrange("p (k c) -> p k c", c=768)[:, :, 0:n * 128], eng=SP)
            for f in range(n):
                fc = isup * 22 + g0 + f
                bk = K.bank()
                for kc in range(8):
                    K.mm(bk[:, 0:N], wb[:, kc, f * 128:(f + 1) * 128], S.h2T[:, kc, 0:N], start=(kc == 0), stop=(kc == 7))
                uS = S.uS.next()
                K.cp(POOL, uS[:, 0:2], ucarry[:, 2 * fc:2 * fc + 2])
                K.cp(ACT, uS[:, 2:N + 2], bk[:, 0:N])
                K.cp(POOL, ucarry[:, 2 * fc:2 * fc + 2], uS[:, N:N + 2])
                ct = S.ct.next()[:, 0:N]
                K.act(ct, uS[:, 2:N + 2], AF.Identity, scale=cols[:, C_CW + 88 + fc:C_CW + 88 + fc + 1],
                      bias=cols[:, C_CB + fc:C_CB + fc + 1])
                K.stt(ct, uS[:, 1:N + 1], cols[:, C_CW + 44 + fc:C_CW + 44 + fc + 1], ct, ALU.mult, ALU.add)
                K.stt(ct, uS[:, 0:N], cols[:, C_CW + fc:C_CW + fc + 1], ct, ALU.mult, ALU.add)
                if pending is not None:
                    pending()

                def fin(ct=ct, f=f, isup=isup, g0=g0):
                    if not isup:
                        K.act(S.sg[:, f, 0:N], ct, AF.Silu)
                    else:
                        K.tt(POOL, S.mT[:, g0 + f, 0:N], ct, S.sg[:, f, 0:N], ALU.mult)
                pending = fin
            if pending is not None:
                pending()
                pending = None
    for i in range(NTT):
        xm = xm_tiles[i]
        for half in range(2):
            hs = slice(half * 512, (half + 1) * 512)
            bk = K.bank()
            for f in range(22):
                K.mm(bk[0:TPt, :], S.mT[:, f, i * 128:i * 128 + TPt], S.Wd[:, f, hs], start=(f == 0), stop=(f == 21))
            K.tt(DVE, xm[:, hs], xm[:, hs], bk[0:TPt, :], ALU.add)
        K.store(y_ap[i * 128:i * 128 + TPt, :], xm)


def conv_out(K, G, S, ucarry, out_ap):
    bk = K.bank()
    K.mm(bk[0:88, 0:128], ucarry, G.ident)
    K.cp(ACT, S.cvo[0:88], bk[0:88, 0:128])
    K.store(out_ap, S.cvo[0:88])
```

```python
import contextlib
import math
import numpy as np
import concourse.bass as bass
import concourse.mybir as mybir
from concourse.bass_utils import run_bass_kernel_spmd

F32 = mybir.dt.float32
BF16 = mybir.dt.bfloat16
AF = mybir.ActivationFunctionType
ALU = mybir.AluOpType
AX = mybir.AxisListType

PE, ACT, DVE, POOL, SP = "tensor", "scalar", "vector", "gpsimd", "sync"
ENGINES = [PE, ACT, DVE, POOL, SP]

D = 1024
RW = 512
RWKV_IN = 1792
IN_DIM = 3328
DFF = 2816
NORM_EPS = 1e-6
GN_EPS = 64e-5
ROPE_THETA = 500000.0
DEC_C = math.exp(-0.5)
LAM_INIT = 0.8 - 0.6 * math.exp(-0.3 * 0)
NEG = -30000.0


class Buf:
    __slots__ = ("name", "last_w", "readers", "psum")

    def __init__(self, name, psum=False):
        self.name = name
        self.last_w = None
        self.readers = []
        self.psum = psum


class V:
    __slots__ = ("ap", "buf")

    def __init__(self, ap, buf):
        self.ap = ap
        self.buf = buf

    def __getitem__(self, k):
        return V(self.ap[k], self.buf)

    def rr(self, s, **kw):
        return V(self.ap.rearrange(s, **kw), self.buf)

    def bc(self, dt):
        return V(self.ap.bitcast(dt), self.buf)

    def us(self, ax):
        return V(self.ap.unsqueeze(ax), self.buf)

    def tb(self, shape):
        return V(self.ap.to_broadcast(list(shape)), self.buf)

    def alias(self, name):
        return V(self.ap, Buf(name))


class Op:
    __slots__ = ("eng", "fn", "reads", "writes", "dma_sem", "deps", "sig", "idx", "acc", "dma_cnt", "bar")

    def __init__(self, eng, fn, reads, writes, dma_sem, acc):
        self.eng = eng
        self.fn = fn
        self.reads = reads
        self.writes = writes
        self.dma_sem = dma_sem
        self.deps = None
        self.sig = None
        self.acc = acc
        self.dma_cnt = None
        self.bar = None


class Prog:
    def __init__(self):
        self.ops = []
        self.group_sems = set()

    def op(self, eng, fn, reads=(), writes=(), dma_sem=None, acc=False):
        def bufs(vs):
            out = []
            for v in vs:
                if v is None:
                    continue
                if isinstance(v.buf, (list, tuple)):
                    out.extend(v.buf)
                else:
                    out.append(v.buf)
            return out
        o = Op(eng, fn, bufs(reads), bufs(writes), dma_sem, acc)
        o.idx = len(self.ops)
        self.ops.append(o)
        return o

    def barrier(self):
        if self.ops:
            self.ops[-1].bar = True

    def schedule(self):
        last_on = {e: None for e in ENGINES}
        last_dma = {}
        pend = {e: [] for e in ENGINES}
        for o in self.ops:
            deps = set(pend[o.eng])
            pend[o.eng] = []
            for b in o.reads:
                if b.last_w is not None:
                    deps.add(b.last_w)
            for b in o.writes:
                if b.last_w is not None:
                    deps.add(b.last_w)
                deps.update(b.readers)
            deps.discard(o)
            o.deps = deps
            for b in o.reads:
                if b.psum and b.readers and b.readers[0].eng != o.eng:
                    raise AssertionError(f"PSUM bank {b.name} read by two engines ({b.readers[0].eng}, {o.eng})")
                b.readers.append(o)
            for b in o.writes:
                b.last_w = o
                b.readers = []
            if o.dma_sem is not None:
                last_dma[id(o.dma_sem)] = o
            else:
                last_on[o.eng] = o
            if o.bar:
                allp = [x for x in last_on.values() if x is not None] + list(last_dma.values())
                for e in ENGINES:
                    pend[e] = list(allp)
        known = {e: {p: -1 for p in ENGINES} for e in ENGINES}
        dma_known = {e: {} for e in ENGINES}
        waits = []
        for o in self.ops:
            need = {}
            for d in o.deps:
                if d.dma_sem is not None:
                    key = ("dma", id(d.dma_sem))
                    if d.idx > dma_known[o.eng].get(key[1], -1):
                        cur = need.get(key)
                        if cur is None or d.idx > cur.idx:
                            need[key] = d
                else:
                    if d.eng == PE and o.eng == PE and d.acc and o.acc:
                        continue
                    if d.idx > known[o.eng][d.eng]:
                        key = ("eng", d.eng)
                        cur = need.get(key)
                        if cur is None or d.idx > cur.idx:
                            need[key] = d
            wl = []
            for (kind, key), d in need.items():
                wl.append(d)
                if kind == "dma":
                    dma_known[o.eng][key] = d.idx
                else:
                    known[o.eng][d.eng] = d.idx
                    d.sig = True
            waits.append(wl)
        self.waits = waits

    def emit(self, nc, sems):
        self.schedule()
        cnt = {e: 0 for e in ENGINES}
        dcnt = {}
        for o in self.ops:
            if o.dma_sem is not None:
                k = id(o.dma_sem)
                dcnt[k] = dcnt.get(k, 0) + 16
                o.dma_cnt = dcnt[k]
            elif o.sig:
                cnt[o.eng] += 1
                o.sig = cnt[o.eng]
        for o in self.ops:
            if o.dma_sem is not None and id(o.dma_sem) in self.group_sems:
                o.dma_cnt = dcnt[id(o.dma_sem)]
        print('SEMCOUNTS', cnt, 'max dma', max(dcnt.values()) if dcnt else 0, 'nops', len(self.ops), flush=True)
        for e in ENGINES:
            assert cnt[e] < 60000, (e, cnt[e])
        for v in dcnt.values():
            assert v < 60000, v
        per_eng = {e: [o for o in self.ops if o.eng == e] for e in ENGINES}
        last_dma = {}
        for o in self.ops:
            if o.dma_sem is not None:
                last_dma[id(o.dma_sem)] = o
        waits = self.waits

        with nc.Block() as block:
            def body(engname):
                def f(eng):
                    for o in per_eng[engname]:
                        for d in waits[o.idx]:
                            if d.dma_sem is not None:
                                eng.wait_ge(d.dma_sem, d.dma_cnt)
                            else:
                                eng.wait_ge(sems[d.eng], d.sig)
                        ins = o.fn(eng)
                        if o.dma_sem is not None:
                            ins.then_inc(o.dma_sem, 16)
                        elif o.sig:
                            ins.then_inc(sems[o.eng], 1)
                    if engname == SP:
                        for o in last_dma.values():
                            eng.wait_ge(o.dma_sem, o.dma_cnt)
                return f
            block.tensor(body(PE))
            block.scalar(body(ACT))
            block.vector(body(DVE))
            block.gpsimd(body(POOL))
            block.sync(body(SP))


class Ctx:
    def __init__(self, nc, es, arena_words=53000):
        self.nc, self.es = nc, es
        self.P = Prog()
        self.sems = {e: es.enter_context(nc.semaphore("s_" + e)) for e in ENGINES}
        self.arena = es.enter_context(nc.sbuf_tensor("arena", [128, arena_words], F32))
        self.words = arena_words
        self.top = 0
        self.banks = []
        self.ps = es.enter_context(nc.psum_tensor("psall", [128, 4096], F32))
        for i in range(8):
            self.banks.append(V(self.ps[:, i * 512:(i + 1) * 512], Buf(f"bank{i}", psum=True)))
        self.bi = 0
        self.nsem = 0
        self.lsem = {}
        self.ssem = {}
        self.gsem = self.newsem()
        self.P.group_sems.add(id(self.gsem))
        self.rot = {ACT: 0}

    def newsem(self):
        if getattr(self, "free_sems", None):
            return self.free_sems.pop()
        self.nsem += 1
        return self.es.enter_context(self.nc.semaphore(f"d{self.nsem}"))

    def recycle(self):
        if not hasattr(self, "free_sems"):
            self.free_sems = []
        self.free_sems.extend(self.lsem.values())
        self.free_sems.extend(self.ssem.values())
        self.lsem = {}
        self.ssem = {}

    def sb(self, name, shape, dt=F32, at=None):
        n = 1
        for s in shape[1:]:
            n *= s
        words = n if dt == F32 else (n + 1) // 2
        words = (words + 7) // 8 * 8
        if at is None:
            assert self.top + words <= self.words, (name, self.top, words)
            off = self.top
            self.top += words
        else:
            off = at
        self.last_alloc = (off, words)
        ap = self.arena[:, off:off + words]
        if dt == BF16:
            ap = ap.bitcast(BF16)
        ap = ap[:, 0:n]
        if len(shape) == 3:
            ap = ap.rearrange("p (a b) -> p a b", b=shape[2])
        elif len(shape) == 4:
            ap = ap.rearrange("p (a b c) -> p a b c", b=shape[2], c=shape[3])
        if shape[0] != 128:
            ap = ap[0:shape[0]]
        return V(ap, Buf(name))

    def bankpair(self, i):
        ap = self.ps[:, i * 512:(i + 2) * 512].rearrange("p (b n) -> p b n", b=2)
        return V(ap, [self.banks[i].buf, self.banks[i + 1].buf])

    def bank(self, lo=0, hi=8):
        n = hi - lo
        b = self.banks[lo + (self.bi % n)]
        self.bi += 1
        return b

    def mm(self, out, lhsT, rhs, start=True, stop=True):
        self.P.op(PE, lambda e: e.matmul(out.ap, lhsT=lhsT.ap, rhs=rhs.ap, start=start, stop=stop),
                  [lhsT, rhs], [out], acc=True)

    def tr(self, out, in_, ident):
        self.P.op(PE, lambda e: e.transpose(out.ap, in_.ap, ident.ap), [in_, ident], [out], acc=True)

    def act(self, out, in_, func, bias=None, scale=None, accum=None):
        kw = {}
        reads = [in_]
        if bias is not None:
            kw["bias"] = bias.ap if isinstance(bias, V) else float(bias)
            if isinstance(bias, V):
                reads.append(bias)
        if scale is not None:
            kw["scale"] = scale.ap if isinstance(scale, V) else float(scale)
            if isinstance(scale, V):
                reads.append(scale)
        writes = [out]
        if accum is not None:
            kw["accum_out"] = accum.ap
            writes.append(accum)
        self.P.op(ACT, lambda e: e.activation(out=out.ap, in_=in_.ap, func=func, **kw), reads, writes)

    def tt(self, eng, out, a, b, op):
        self.P.op(eng, lambda e: e.tensor_tensor(out=out.ap, in0=a.ap, in1=b.ap, op=op), [a, b], [out])

    def ts(self, eng, out, a, s1, op0, s2=None, op1=None):
        reads = [a] + [s for s in (s1, s2) if isinstance(s, V)]
        v1 = s1.ap if isinstance(s1, V) else float(s1)
        v2 = None if s2 is None else (s2.ap if isinstance(s2, V) else float(s2))
        if op1 is None:
            self.P.op(eng, lambda e: e.tensor_scalar(out=out.ap, in0=a.ap, scalar1=v1, scalar2=None, op0=op0),
                      reads, [out])
        else:
            self.P.op(eng, lambda e: e.tensor_scalar(out=out.ap, in0=a.ap, scalar1=v1, scalar2=v2, op0=op0,
                                                      op1=op1), reads, [out])

    def stt(self, out, a, s, b, op0, op1):
        reads = [a, b] + ([s] if isinstance(s, V) else [])
        sv = s.ap if isinstance(s, V) else float(s)
        self.P.op(DVE, lambda e: e.scalar_tensor_tensor(out=out.ap, in0=a.ap, scalar=sv, in1=b.ap, op0=op0,
                                                         op1=op1), reads, [out])

    def cp(self, eng, out, in_):
        if eng == ACT:
            self.act(out, in_, AF.Copy)
        else:
            self.P.op(eng, lambda e: e.tensor_copy(out=out.ap, in_=in_.ap), [in_], [out])

    def memset(self, eng, out, val):
        self.P.op(eng, lambda e: e.memset(out.ap, float(val)), [], [out])

    def fence(self, dummy, vs):
        self.P.op(POOL, lambda e: e.memset(dummy.ap, 0.0), [], [dummy] + list(vs))

    def recip(self, out, in_):
        self.P.op(DVE, lambda e: e.reciprocal(out=out.ap, in_=in_.ap), [in_], [out])

    def reduce(self, out, in_, op=ALU.add):
        self.P.op(DVE, lambda e: e.tensor_reduce(out=out.ap, in_=in_.ap, axis=AX.X, op=op), [in_], [out])

    def scan(self, out, d0, d1):
        self.P.op(DVE, lambda e: e.tensor_tensor_scan(out=out.ap, data0=d0.ap, data1=d1.ap, initial=0.0,
                                                       op0=ALU.mult, op1=ALU.add), [d0, d1], [out])

    def load(self, dst, src_ap, eng=SP, group=False):
        if group:
            sem = self.gsem
        else:
            k = id(dst.buf)
            if k not in self.lsem:
                self.lsem[k] = self.newsem()
            sem = self.lsem[k]
        self.P.op(eng, lambda e: e.dma_start(out=dst.ap, in_=src_ap), [], [dst], dma_sem=sem)

    def store(self, dst_ap, src, eng=POOL):
        k = id(src.buf)
        if k not in self.ssem:
            self.ssem[k] = self.newsem()
        sem = self.ssem[k]
        self.P.op(eng, lambda e: e.dma_start(out=dst_ap, in_=src.ap), [src], [], dma_sem=sem)


class Ring:
    def __init__(self, K, name, n, shape, dt=F32):
        self.slots = [K.sb(f"{name}{i}", shape, dt) for i in range(n)]
        self.i = 0

    def next(self):
        s = self.slots[self.i % len(self.slots)]
        self.i += 1
        return s


C_G1, C_G2, C_MU, C_W0, C_A0, C_KK, C_KA, C_RK, C_GNG, C_GNB, C_CW, C_CB, C_FLAG, C_NEGK = (
    0, 8, 16, 30, 34, 38, 42, 46, 50, 54, 58, 190, 234, 235)
NCOL = 236
R_GQK, R_GSUB, R_LQ, R_LK = 0, 512, 640, 768
NROW = 896


class NS:
    pass


def blkview(x, c):
    return x.rr("p (f c t) -> p f c t", f=4, c=2, t=64)[:, :, c]


def setup_common(K, dr):
    G = NS()
    G.cols = K.sb("cols", [128, NCOL])
    G.rows = K.sb("rows", [128, NROW])
    G.ident = K.sb("ident", [128, 128])
    G.masks = K.sb("masks", [128, 5, 512])
    G.identb = K.sb("identb", [128, 128], BF16)
    G.bonesb = K.sb("bonesb", [128, 128], BF16)
    G.valid = K.sb("valid", [128, 64])
    G.oka = K.sb("oka", [128, 4])
    G.lam = K.sb("lam", [128, 4])
    G.gsub = K.sb("gsub", [128, 128])
    tmpb = K.sb("tmpb", [128, 128])
    K.load(G.cols, dr["cols"], group=True)
    K.load(G.rows, dr["rows"][0:1, :].partition_broadcast(128), group=True)
    K.load(G.ident, dr["c_ident"], group=True)
    K.load(G.masks, dr["c_masks"], group=True)
    K.load(tmpb, dr["c_bones"], group=True)
    K.load(G.valid, dr["c_valid"], group=True)
    K.cp(DVE, G.identb, G.ident)
    K.cp(DVE, G.bonesb, tmpb)
    K.ts(POOL, G.oka, G.cols[:, C_KA:C_KA + 4], -1.0, ALU.mult, 1.0, ALU.add)
    prod = K.sb("lamprod", [128, 128])
    K.tt(DVE, prod, G.rows[:, R_LQ:R_LQ + 128], G.rows[:, R_LK:R_LK + 128], ALU.mult)
    K.reduce(G.lam[:, 0:2], prod.rr("p (a b) -> p a b", b=64))
    K.act(G.lam[:, 0:2], G.lam[:, 0:2], AF.Exp)
    K.tt(DVE, G.lam[:, 2:3], G.lam[:, 0:1], G.lam[:, 1:2], ALU.subtract)
    K.ts(DVE, G.lam[:, 3:4], G.lam[:, 2:3], LAM_INIT, ALU.add, -1.0, ALU.mult)
    K.ts(POOL, G.gsub, G.rows[:, R_GSUB:R_GSUB + 128], 1.0 - LAM_INIT, ALU.mult)
    K.ts(POOL, G.rows[:, R_GQK:R_GQK + 256], G.rows[:, R_GQK:R_GQK + 256], 0.125, ALU.mult)
    G.junk = K.sb("junk", [128, 1024], BF16)
    G.epsc = K.sb("epsc", [128, 4])
    K.memset(POOL, G.epsc[:, 0:1], NORM_EPS)
    K.memset(POOL, G.epsc[:, 1:2], GN_EPS)
    K.memset(POOL, G.epsc[:, 2:3], 1e-30)
    K.memset(POOL, G.epsc[:, 3:4], 0.0)
    return G


def load_weight(K, dst, src, rows, ncols, stage, scale_cols=None, col0=0):
    engs = [ACT, DVE, POOL] if scale_cols is None else [ACT, DVE]
    i = 0
    for kc in range(rows // 128):
        for c0 in range(0, ncols, 1792):
            cw = min(1792, ncols - c0)
            st = stage.next()
            K.load(st[:, 0:cw], src[kc * 128:(kc + 1) * 128, col0 + c0:col0 + c0 + cw])
            eng = engs[i % len(engs)]
            i += 1
            d = dst[:, kc, c0:c0 + cw]
            if scale_cols is None:
                K.cp(eng, d, st[:, 0:cw])
            elif eng == ACT:
                K.act(d, st[:, 0:cw], AF.Copy, scale=scale_cols[:, kc:kc + 1])
            else:
                K.ts(eng, d, st[:, 0:cw], scale_cols[:, kc:kc + 1], ALU.mult)


def rmsnorm_to_hT(K, G, S, x_rows, TPt, hT, col0):
    xt = S.x.next()
    K.load(xt[0:TPt], x_rows)
    ss = S.ss.next()
    K.act(G.junk.alias("j")[0:TPt], xt[0:TPt], AF.Square, accum=ss[0:TPt, 0:1])
    K.act(ss[0:TPt, 1:2], ss[0:TPt, 0:1], AF.Sqrt, scale=1.0 / D, bias=G.epsc[0:TPt, 0:1])
    K.recip(ss[0:TPt, 2:3], ss[0:TPt, 1:2])
    hb = S.hb.next()
    K.act(hb[0:TPt], xt[0:TPt], AF.Copy, scale=ss[0:TPt, 2:3])
    bk = K.bank()
    bkb = bk.bc(BF16)
    for kc in range(8):
        K.tr(bkb[:, kc * 128:kc * 128 + TPt], hb[0:TPt, kc * 128:(kc + 1) * 128], G.identb[0:TPt, 0:TPt])
    K.cp(DVE, hT[:, :, col0:col0 + TPt], bkb.rr("p (k t) -> p k t", t=128)[:, :, 0:TPt])
    return xt


def blockmm(K, lhs, rhs, fcs_cols, NC, lhs3=None, rhs3=None):
    bk = K.bank()
    for fc in range(4):
        for c in range(NC):
            blk = fc * 2 + c
            for hp in range(2):
                ph = slice(hp * 64, hp * 64 + 64)
                bc = slice(blk * 64, blk * 64 + 64)
                if lhs3 is not None:
                    cs = slice(lhs3[1] + c * 64, lhs3[1] + c * 64 + 64)
                    a = lhs3[0][ph, fc, cs]
                else:
                    a = lhs[ph, bc]
                if rhs3 is not None:
                    cs = slice(rhs3[1] + c * 64, rhs3[1] + c * 64 + 64)
                    b = rhs3[0][ph, fc, cs]
                else:
                    b = rhs[ph, bc]
                K.mm(bk[ph, bc], a, b)
    return bk


def sweepR_alloc(K, G, dr):
    S = NS()
    S.Wr = K.sb("Wr", [128, 8, RWKV_IN], BF16)
    S.Wl = K.sb("Wl", [128, 2, RW], BF16)
    S.Wg = K.sb("Wg", [128, RW], BF16)
    mark = K.top
    stage = Ring(K, "wst", 2, [128, 1792])
    load_weight(K, S.Wr, dr["w_in"], 1024, RWKV_IN, stage, scale_cols=G.cols[:, C_G1:C_G1 + 8])
    st = stage.next()
    K.load(st[0:64, 0:RW], dr["w_w2"])
    K.load(st[64:128, 0:RW], dr["a_w2"])
    K.cp(DVE, S.Wl[0:64, 0], st[0:64, 0:RW])
    K.cp(DVE, S.Wl[64:128, 1], st[64:128, 0:RW])
    st = stage.next()
    K.load(st[:, 0:RW], dr["g_w2"])
    K.cp(DVE, S.Wg, st[:, 0:RW])
    K.P.barrier()
    K.recycle()
    K.top = mark
    S.x = Ring(K, "x", 2, [128, D])
    S.ss = Ring(K, "ss", 4, [128, 4])
    S.hb = Ring(K, "hb", 2, [128, D], BF16)
    S.hT = Ring(K, "hT", 2, [128, 8, 512], BF16)
    S.pS = Ring(K, "pS", 2, [128, 520])
    S.zT = K.sb("zT", [128, 14, 512])
    S.lora = K.sb("lora", [128, 512], BF16)
    S.sgl = K.sb("sgl", [128, 512], BF16)
    for n in ("aT", "rT", "bT", "kT", "vT", "rk", "g_sb", "yout"):
        setattr(S, n, K.sb(n, [128, 4, 512], BF16))
    for n in ("lwr", "alr", "cum", "E", "Einv", "cump", "kk", "nrm", "tfac", "bvec"):
        setattr(S, n, K.sb(n, [128, 512]))
    S.Eprev, S.rn, S.kkn, S.kmod = S.cump, S.nrm, S.kk, S.tfac
    S.kk2 = K.sb("kk2", [128, 512], BF16)
    S.gC = K.sb("gC", [128, 4, 8])
    TILE_BF = ("Vtok", "Ktok", "Btok", "M", "Nm", "AKT", "RBT", "RKT", "Pm", "Qm", "M2a", "N2a")
    S.sets = []
    for si in range(2):
        B = NS()
        for n in TILE_BF:
            setattr(B, n, K.sb(f"{n}_{si}", [128, 512], BF16))
        S.sets.append(B)
    shared = {}
    for n in ("RHSb", "Ub", "ynb"):
        shared[n] = K.sb(n, [128, 512], BF16)
    for n, shp in (("Ysb", [128, 512]), ("Ysq", [128, 512]), ("yst", [128, 8, 8]), ("t1", [128, 4, 128]),
                   ("t2", [128, 4, 128])):
        shared[n] = K.sb(n, shp)
    for B in S.sets:
        for n, v in shared.items():
            setattr(B, n, v)
    S.wkvio = K.sb("wkvio", [128, 256])
    return S


def new_seq(K, name):
    q = NS()
    q.ST = K.sb(name + "ST", [128, 256])
    q.STb = K.sb(name + "STb", [128, 256], BF16)
    q.carry = K.sb(name + "carry", [128, 14])
    return q


def sweepR_mt(K, G, S, dr, seq, x_ap, N, hT_dram, tok0, mine, ym_dram=None, ym0=0, nvalid=None):
    cols = G.cols
    TPt = min(N, 128)
    NTT = max(1, N // 128)
    NC = 2 if N >= 128 else 1
    NCH = N // 64
    valid = G.valid if nvalid is not None else None
    nv = N if nvalid is None else nvalid
    hT = S.hT.next()
    for i in range(NTT):
        rmsnorm_to_hT(K, G, S, x_ap[i * 128:i * 128 + TPt, :], TPt, hT, i * 128)
    K.store(hT_dram[:, :, tok0:tok0 + N], hT[:, :, 0:N])
    if STOP_AT <= 1:
        return
    for fc in range(14):
        bk = K.bank()
        for kc in range(8):
            K.mm(bk[:, 0:N], S.Wr[:, kc, fc * 128:(fc + 1) * 128], hT[:, kc, 0:N], start=(kc == 0), stop=(kc == 7))
        pS = S.pS.next()
        K.cp(POOL, pS[:, 0:1], seq.carry[:, fc:fc + 1])
        K.cp(ACT, pS[:, 1:N + 1], bk[:, 0:N])
        K.cp(POOL, seq.carry[:, fc:fc + 1], pS[:, nv:nv + 1])
        z = S.zT[:, fc, 0:N]
        K.tt(POOL, z, pS[:, 0:N], pS[:, 1:N + 1], ALU.subtract)
        K.stt(z, z, cols[:, C_MU + fc:C_MU + fc + 1], pS[:, 1:N + 1], ALU.mult, ALU.add)
        if valid is not None:
            K.tt(POOL, z, z, valid[:, 0:N], ALU.mult)
    if STOP_AT <= 2:
        return
    K.act(S.lora[0:64, 0:N], S.zT[0:64, 12, 0:N], AF.Tanh)
    K.cp(POOL, S.lora[64:128, 0:N], S.zT[64:128, 12, 0:N])
    K.act(S.sgl[:, 0:N], S.zT[:, 13, 0:N], AF.Sigmoid)
    for fc in range(4):
        fs = slice(fc * 128, (fc + 1) * 128)
        zr, zk, zv = S.zT[:, fc, 0:N], S.zT[:, 4 + fc, 0:N], S.zT[:, 8 + fc, 0:N]
        lwr, alr, cum, E, Einv, cump, Eprev = (x[:, 0:N] for x in (S.lwr, S.alr, S.cum, S.E, S.Einv, S.cump, S.Eprev))
        kk, nrm, rn, kkn, tfac, kmod, bvec = (x[:, 0:N] for x in (S.kk, S.nrm, S.rn, S.kkn, S.tfac, S.kmod, S.bvec))
        K.act(kk, zk, AF.Copy, scale=cols[:, C_KK + fc:C_KK + fc + 1])
        K.tt(POOL, S.kk2[:, 0:N], kk, kk, ALU.mult)
        b4 = K.bank()
        K.mm(b4[:, 0:N], G.bonesb, S.kk2[:, 0:N])
        b1 = K.bank()
        K.mm(b1[:, 0:N], S.Wl[0:64, 0, fs], S.lora[0:64, 0:N])
        b2 = K.bank()
        K.mm(b2[:, 0:N], S.Wl[64:128, 1, fs], S.lora[64:128, 0:N])
        b3 = K.bank()
        K.mm(b3[:, 0:N], S.Wg[:, fs], S.sgl[:, 0:N])
        K.act(lwr, b1[:, 0:N], AF.Sigmoid, bias=cols[:, C_W0 + fc:C_W0 + fc + 1])
        if valid is not None:
            K.tt(POOL, lwr, lwr, valid[:, 0:N], ALU.mult)
        K.scan(cum, G.masks[:, 4, 0:N], lwr)
        K.act(nrm, b4[:, 0:N], AF.Sqrt, bias=G.epsc[:, 2:3])
        K.recip(rn, nrm)
        K.act(alr, b2[:, 0:N], AF.Sigmoid, bias=cols[:, C_A0 + fc:C_A0 + fc + 1])
        K.tt(DVE, kkn, kk, rn, ALU.mult)
        K.tt(POOL, cump, cum, lwr, ALU.subtract)
        K.act(E, cum, AF.Exp, scale=-DEC_C)
        K.act(Eprev, cump, AF.Exp, scale=-DEC_C)
        K.act(Einv, cum, AF.Exp, scale=DEC_C)
        K.tt(POOL, bvec, kkn, alr, ALU.mult)
        K.ts(DVE, tfac, alr, cols[:, C_KA + fc:C_KA + fc + 1], ALU.mult, G.oka[:, fc:fc + 1], ALU.add)
        K.tt(DVE, kmod, zk, tfac, ALU.mult)
        K.stt(S.aT[:, fc, 0:N], kkn, -1.0, Eprev, ALU.mult, ALU.mult)
        K.tt(POOL, S.bT[:, fc, 0:N], bvec, Einv, ALU.mult)
        K.tt(DVE, S.rT[:, fc, 0:N], zr, E, ALU.mult)
        K.tt(DVE, S.kT[:, fc, 0:N], kmod, Einv, ALU.mult)
        K.stt(S.rk[:, fc, 0:N], zr, cols[:, C_RK + fc:C_RK + fc + 1], kmod, ALU.mult, ALU.mult)
        K.cp(POOL, S.gC[:, fc, 0:NCH], E.rr("p (c t) -> p c t", t=64)[:, :, 63])
        K.cp(POOL, S.vT[:, fc, 0:N], zv)
        K.cp(ACT, S.g_sb[:, fc, 0:N], b3[:, 0:N])
    mk = G.masks
    YB = K.banks[7]

    def bankR():
        return K.bank(0, 7)

    def bmm(lhs, rhs, lhs3=None, rhs3=None):
        bk = bankR()
        for fc in range(4):
            for c in range(NC):
                blk = fc * 2 + c
                for hp in range(2):
                    ph = slice(hp * 64, hp * 64 + 64)
                    bc = slice(blk * 64, blk * 64 + 64)
                    if lhs3 is not None:
                        a_ = lhs3[0][ph, fc, lhs3[1] + c * 64:lhs3[1] + c * 64 + 64]
                    else:
                        a_ = lhs[ph, bc]
                    if rhs3 is not None:
                        b_ = rhs3[0][ph, fc, rhs3[1] + c * 64:rhs3[1] + c * 64 + 64]
                    else:
                        b_ = rhs[ph, bc]
                    K.mm(bk[ph, bc], a_, b_)
        return bk

    def part1(tt, B):
        base = tt * 128
        for dst, src in ((B.Vtok, S.vT), (B.Ktok, S.kT), (B.Btok, S.bT)):
            bk = bankR()
            bkb = bk.bc(BF16)
            for fc in range(4):
                for c in range(NC):
                    blk = fc * 2 + c
                    for hp in range(2):
                        ph = slice(hp * 64, hp * 64 + 64)
                        K.tr(bkb[ph, blk * 64:blk * 64 + 64], src[ph, fc, base + c * 64:base + c * 64 + 64],
                             G.identb[ph, ph])
            K.cp(ACT, dst, bkb[:, 0:512])
            yield
        for dst, l3, r3, mi in ((B.M, S.bT, S.aT, 0), (B.Nm, S.aT, S.bT, 1), (B.AKT, S.kT, S.aT, 0),
                                (B.RBT, S.bT, S.rT, 2), (B.RKT, S.kT, S.rT, 2)):
            bk = bmm(None, None, lhs3=(l3, base), rhs3=(r3, base))
            K.tt(DVE, dst, bk, mk[:, mi], ALU.mult)
            yield
        K.tt(DVE, B.Pm, B.M, mk[:, 3], ALU.add)
        K.tt(DVE, B.Qm, B.Nm, mk[:, 3], ALU.add)
        Mc, Nc = B.M, B.Nm
        for lvl in range(5):
            last = lvl == 4
            M2 = B.M2a if lvl % 2 == 0 else B.M
            N2 = B.N2a if lvl % 2 == 0 else B.Nm
            bM2 = bmm(Nc, Mc)
            K.cp(ACT, M2, bM2)
            yield
            if not last:
                bN2 = bmm(Mc, Nc)
                K.cp(ACT, N2, bN2)
                yield
            bPM = bmm(B.Qm, M2)
            K.tt(DVE, B.Pm, B.Pm, bPM, ALU.add)
            yield
            if not last:
                bNQ = bmm(M2, B.Qm)
                K.tt(DVE, B.Qm, B.Qm, bNQ, ALU.add)
                yield
            Mc, Nc = M2, N2

    def part2(tt, B):
        base = tt * 128
        for c in range(NC):
            cg = tt * 2 + c
            cs = slice(base + c * 64, base + c * 64 + 64)
            bR = bankR()
            for fc in range(4):
                bc = slice((fc * 2 + c) * 64, (fc * 2 + c) * 64 + 64)
                for hp in range(2):
                    ph = slice(hp * 64, hp * 64 + 64)
                    K.mm(bR[ph, bc], B.AKT[ph, bc], B.Vtok[ph, bc], start=True, stop=False)
                    K.mm(bR[ph, bc], S.aT[ph, fc, cs], seq.STb[ph, fc * 64:fc * 64 + 64], start=False, stop=True)
            K.cp(DVE, blkview(B.RHSb, c), blkview(bR, c))
            yield
            bU = bankR()
            for fc in range(4):
                bc = slice((fc * 2 + c) * 64, (fc * 2 + c) * 64 + 64)
                for hp in range(2):
                    ph = slice(hp * 64, hp * 64 + 64)
                    K.mm(bU[ph, bc], B.Pm[ph, bc], B.RHSb[ph, bc])
            K.cp(ACT, blkview(B.Ub, c), blkview(bU, c))
            yield
            bS = bankR()
            for fc in range(4):
                bc = slice((fc * 2 + c) * 64, (fc * 2 + c) * 64 + 64)
                for hp in range(2):
                    ph = slice(hp * 64, hp * 64 + 64)
                    if mine:
                        K.mm(YB[ph, bc], S.rT[ph, fc, cs], seq.STb[ph, fc * 64:fc * 64 + 64], start=True, stop=False)
                        K.mm(YB[ph, bc], B.RBT[ph, bc], B.Ub[ph, bc], start=False, stop=False)
                        K.mm(YB[ph, bc], B.RKT[ph, bc], B.Vtok[ph, bc], start=False, stop=True)
                    K.mm(bS[ph, fc * 64:fc * 64 + 64], B.Btok[ph, bc], B.Ub[ph, bc], start=True, stop=False)
                    K.mm(bS[ph, fc * 64:fc * 64 + 64], B.Ktok[ph, bc], B.Vtok[ph, bc], start=False, stop=True)
            K.tt(DVE, seq.ST, seq.ST, bS[:, 0:256], ALU.add)
            ST3 = seq.ST.rr("p (f i) -> p f i", i=64)
            K.tt(DVE, ST3, ST3, S.gC[:, :, cg].us(2).tb([128, 4, 64]), ALU.mult)
            K.cp(ACT, seq.STb, seq.ST)
            yield
        if not mine:
            return
        K.cp(ACT, B.Ysb, YB)
        Y3 = B.Ysb.rr("p (b i) -> p b i", i=64)
        st = B.yst
        K.reduce(st[:, 0], Y3)
        K.tt(POOL, B.Ysq, B.Ysb, B.Ysb, ALU.mult)
        K.reduce(st[:, 1], B.Ysq.rr("p (b i) -> p b i", i=64))
        yield
        K.ts(DVE, st[:, 2], st[:, 0], 1.0 / 64, ALU.mult)
        K.tt(DVE, st[:, 3], st[:, 2], st[:, 2], ALU.mult)
        K.stt(st[:, 4], st[:, 1], 1.0 / 64, st[:, 3], ALU.mult, ALU.subtract)
        K.act(st[:, 5], st[:, 4], AF.Sqrt, bias=G.epsc[:, 1:2])
        K.recip(st[:, 6], st[:, 5])
        K.tt(DVE, Y3, Y3, st[:, 2].us(2).tb([128, 8, 64]), ALU.subtract)
        K.tt(DVE, B.ynb.rr("p (b i) -> p b i", i=64), Y3, st[:, 6].us(2).tb([128, 8, 64]), ALU.mult)
        yield
        bk = bankR()
        bkb = bk.bc(BF16)
        for fc in range(4):
            for c in range(NC):
                bc = slice((fc * 2 + c) * 64, (fc * 2 + c) * 64 + 64)
                for hp in range(2):
                    ph = slice(hp * 64, hp * 64 + 64)
                    K.tr(bkb[ph, bc], B.ynb[ph, bc], G.identb[ph, ph])
        yT = bkb[:, 0:512].rr("p (f t) -> p f t", f=4)
        for fc in range(4):
            K.act(B.t1[:, fc, 0:TPt], yT[:, fc, 0:TPt], AF.Identity, scale=cols[:, C_GNG + fc:C_GNG + fc + 1],
                  bias=cols[:, C_GNB + fc:C_GNB + fc + 1])
        yield
        bB = bankR()
        for fc in range(4):
            K.mm(bB[:, fc * 128:fc * 128 + TPt], G.bonesb, S.rk[:, fc, base:base + TPt])
        K.tt(DVE, B.t2[:, :, 0:TPt], bB.rr("p (f t) -> p f t", f=4)[:, :, 0:TPt], S.vT[:, :, base:base + TPt], ALU.mult)
        K.tt(POOL, B.t1[:, :, 0:TPt], B.t1[:, :, 0:TPt], B.t2[:, :, 0:TPt], ALU.add)
        K.tt(DVE, S.yout[:, :, base:base + TPt], B.t1[:, :, 0:TPt], S.g_sb[:, :, base:base + TPt], ALU.mult)
        yield

    def run_rr(gens):
        active = [g for g in gens if g is not None]
        while active:
            for g in list(active):
                try:
                    next(g)
                except StopIteration:
                    active.remove(g)

    p1 = [part1(tt, S.sets[tt % 2]) for tt in range(NTT)]
    p2 = [part2(tt, S.sets[tt % 2]) for tt in range(NTT)]
    run_rr([p1[0]])
    for tt in range(1, NTT):
        run_rr([p2[tt - 1], p1[tt]])
    run_rr([p2[NTT - 1]])
    if mine and STOP_AT > 7:
        K.store(ym_dram[:, 0:4, ym0:ym0 + nv], S.yout[:, :, 0:nv])


def seq_init_zero(K, seq):
    K.memset(POOL, seq.ST, 0.0)
    K.memset(POOL, seq.STb, 0.0)
    K.memset(POOL, seq.carry, 0.0)


def seq_init_state(K, G, S, seq, wkv_ap, shift_ap):
    K.load(S.wkvio, wkv_ap)
    K.load(seq.carry, shift_ap)
    bk = K.bank()
    for pr in range(2):
        K.mm(bk[:, pr * 128:pr * 128 + 128], S.wkvio[:, pr * 128:pr * 128 + 128], G.ident)
    K.cp(ACT, seq.ST, bk[:, 0:256])
    K.cp(DVE, seq.STb, seq.ST)


def seq_final(K, G, S, seq, wkv_out_ap, shift_out_ap):
    bk = K.bank()
    for pr in range(2):
        K.mm(bk[:, pr * 128:pr * 128 + 128], seq.ST[:, pr * 128:pr * 128 + 128], G.ident)
    K.cp(ACT, S.wkvio, bk[:, 0:256])
    K.store(wkv_out_ap, S.wkvio)
    K.store(shift_out_ap, seq.carry)


DEBUG = False
STOP_AT = 99


def build_program(TP, TM, PAST, stages=("R", "H", "F")):
    nc = bass.Bass("TRN2", target_bir_lowering=False)
    dr = {}

    def inp(name, shape, dt=F32):
        dr[name] = nc.dram_tensor(name, list(shape), dt, kind="ExternalInput").ap()

    def outp(name, shape, dt=F32):
        dr[name] = nc.dram_tensor(name, list(shape), dt, kind="ExternalOutput").ap()

    def scr(name, shape, dt=BF16):
        dr[name] = nc.dram_tensor(name, list(shape), dt, kind="ExternalOutput" if DEBUG else "Internal").ap()

    NTP, NTM = TP // 128, TM // 128
    inp("x_prev", [TP, D]); inp("x_mine", [TM, D]); inp("x_smp", [64, D])
    inp("cache_k", [PAST, 512]); inp("cache_v", [PAST, 512])
    inp("st_shift", [128, 14]); inp("st_wkv", [128, 256]); inp("st_conv", [128, 88])
    inp("w_in", [D, IN_DIM]); inp("w_out", [D, D]); inp("w_up", [D, 2 * DFF]); inp("w_down", [DFF, D])
    inp("w_w2", [64, RW]); inp("a_w2", [64, RW]); inp("g_w2", [128, RW])
    inp("cols", [128, NCOL]); inp("rows", [1, NROW])
    inp("c_ident", [128, 128]); inp("c_masks", [128, 5 * 512]); inp("c_bones", [128, 128]); inp("c_valid", [128, 64])
    inp("cs_prev", [128, NTP * 16]); inp("cs_mine", [128, NTM * 16]); inp("cs_smp", [128, 16])
    outp("y_mine", [TM, D]); outp("y_smp", [16, D])
    outp("nk_mine", [TM, 512]); outp("nv_mine", [TM, 512])
    outp("nshift", [128, 14]); outp("nwkv", [128, 256]); outp("nconv", [88, 128])
    outp("nk_smp", [16, 512]); outp("nv_smp", [16, 512])
    outp("nshift_s", [128, 14]); outp("nwkv_s", [128, 256]); outp("nconv_s", [88, 128])
    scr("hT_p", [128, 8, TP + TM]); scr("hT_s", [128, 8, 64])
    scr("ymT_p", [128, 8, TM]); scr("ymT_s", [128, 8, 64])
    scr("wup_bf", [8, 128, 8 * 768])

    with contextlib.ExitStack() as es:
        K = Ctx(nc, es)
        G = setup_common(K, dr)
        base_top = K.top
        if "R" in stages:
            S = sweepR_alloc(K, G, dr)
            pq = new_seq(K, "p")
            sq = new_seq(K, "s")
            seq_init_zero(K, pq)
            for m in range((TP + TM) // 512 if STOP_AT >= 1 else 0):
                t0 = m * 512
                if t0 < TP:
                    sweepR_mt(K, G, S, dr, pq, dr["x_prev"][t0:t0 + 512, :], 512, dr["hT_p"], t0, False)
                else:
                    sweepR_mt(K, G, S, dr, pq, dr["x_mine"][t0 - TP:t0 - TP + 512, :], 512, dr["hT_p"], t0, True,
                              ym_dram=dr["ymT_p"], ym0=t0 - TP)
            if STOP_AT >= 0.5:
                seq_final(K, G, S, pq, dr["nwkv"], dr["nshift"])
                seq_init_state(K, G, S, sq, dr["st_wkv"], dr["st_shift"])
            if STOP_AT >= 1:
                sweepR_mt(K, G, S, dr, sq, dr["x_smp"], 64, dr["hT_s"], 0, True, ym_dram=dr["ymT_s"], ym0=0, nvalid=16)
            if STOP_AT >= 0.5:
                seq_final(K, G, S, sq, dr["nwkv_s"], dr["nshift_s"])
            K.P.barrier()
            K.recycle()
            K.top = base_top
        wup_done = False
        if "H" in stages:
            for hp2 in range(2):
                import os
                HSTOP = int(os.environ.get("HSTOP", 9))
                S = sweepH_alloc(K, G, dr, hp2, TP, TM, PAST)
                conv = None
                if hp2 == 1 and "F" in stages:
                    cst = Ring(K, "cst", 2, [128, 768])
                    cob = Ring(K, "cob", 2, [128, 768], BF16)
                    conv = wup_convert(K, G, dr, cst, cob)
                    wup_done = True
                nmt = (TP + TM) // 512
                per = -(-64 // nmt)
                for m in range(nmt if HSTOP >= 2 else 0):
                    sweepH_mt(K, G, S, dr, hp2, m, TP, TM, m * 512 >= TP)
                    if conv is not None:
                        for _ in range(per):
                            next(conv, None)
                if conv is not None:
                    for _ in conv:
                        pass
                if HSTOP >= 4:
                    sweepH_sample(K, G, S, dr, hp2, PAST)
                K.P.barrier()
                K.recycle()
                K.top = base_top
        if "F" in stages:
            S = phaseF_alloc(K, G, dr, wup_done)
            ucp = K.sb("ucp", [128, 88])
            ucs = K.sb("ucs", [128, 88])
            K.memset(POOL, ucp, 0.0)
            K.load(ucs, dr["st_conv"])
            for m in range(TM // 512):
                phaseF_mt(K, G, S, dr, ucp, dr["ymT_p"][:, :, m * 512:(m + 1) * 512],
                          dr["x_mine"][m * 512:(m + 1) * 512, :], dr["y_mine"][m * 512:(m + 1) * 512, :], 512)
            conv_out(K, G, S, ucp, dr["nconv"])
            phaseF_mt(K, G, S, dr, ucs, dr["ymT_s"][:, :, 0:16], dr["x_smp"][0:16, :], dr["y_smp"], 16)
            conv_out(K, G, S, ucs, dr["nconv_s"])
        K.P.emit(nc, K.sems)
    return nc


def _consts():
    p = np.arange(128)[:, None]
    c = np.arange(512)[None, :]
    s, y = p % 64, c % 64
    masks = np.stack([(s < y), (s > y), (s <= y), (s == y), np.broadcast_to(y != 0, (128, 512))], axis=1)
    masks = masks.astype(np.float32).reshape(128, 5 * 512)
    q = np.arange(128)[None, :]
    bones = ((p // 64) == (q // 64)).astype(np.float32)
    valid = np.broadcast_to((np.arange(64)[None, :] < 16), (128, 64)).astype(np.float32)
    return masks, bones, valid


def _rope_table(pos):
    inv = (np.float32(ROPE_THETA) ** (-np.arange(0, 16, 2, dtype=np.float32) / np.float32(16))).astype(np.float32)
    ang = (pos.astype(np.float32)[:, None] * inv[None, :]).astype(np.float32)
    t = np.concatenate([np.cos(ang), np.sin(ang)], axis=1).astype(np.float32)
    n = pos.shape[0] // 128
    return np.ascontiguousarray(t.reshape(n, 128, 16).transpose(1, 0, 2).reshape(128, n * 16))


def _colpack(a, nchunk):
    return np.asarray(a, np.float32).reshape(nchunk, 128).T


def prepare_inputs(inp):
    xp = np.asarray(inp["x_prompt"], np.float32)
    xs = np.asarray(inp["x_sample"], np.float32)
    B, T, _ = xp.shape
    TH = T // 2
    TP, TM = TH - 512, TH + 512
    PAST = inp["cache_k"].shape[2]
    masks, bones, valid = _consts()
    ident = np.eye(128, dtype=np.float32)
    f = lambda k: np.asarray(inp[k], np.float32)[0]
    cw = f("conv_w").reshape(3, 44, 128).transpose(2, 0, 1).reshape(128, 132)
    shared_cols = [_colpack(f("norm1_g"), 8), _colpack(f("norm2_g"), 8), _colpack(f("rw_mu"), 14),
                   _colpack(f("rw_w0"), 4), _colpack(f("rw_a0"), 4), _colpack(f("rw_k_k"), 4),
                   _colpack(f("rw_k_a"), 4), _colpack(f("rw_r_k").reshape(-1), 4), _colpack(f("rw_gn_g"), 4),
                   _colpack(f("rw_gn_b"), 4), cw, _colpack(f("conv_b"), 44)]
    negk = np.where(np.arange(128) < 16, 0.0, NEG).astype(np.float32)[:, None]
    rows = np.concatenate([np.tile(f("df_q_g"), 4), np.tile(f("df_k_g"), 4), f("df_subln_g"), f("df_lq1"),
                           f("df_lq2"), f("df_lk1"), f("df_lk2")]).astype(np.float32)[None, :]
    assert rows.shape[1] == NROW
    cs_prev = _rope_table(np.arange(TP))
    cs_smp = _rope_table(np.concatenate([PAST + np.arange(16), np.zeros(112, np.int64)]))
    shared = {
        "w_in": f("w_in"), "w_out": f("w_out"), "w_up": f("w_up"), "w_down": f("w_down"),
        "w_w2": f("rw_w_w2"), "a_w2": f("rw_a_w2"), "g_w2": f("rw_g_w2"), "rows": rows,
        "c_ident": ident, "c_masks": masks, "c_bones": bones, "c_valid": valid,
        "cs_prev": cs_prev, "cs_smp": cs_smp,
    }
    maps = []
    for c in range(8):
        b, g = c // 2, c % 2
        flag = np.full((128, 1), 0.0 if g == 1 else NEG, np.float32)
        cols = np.concatenate(shared_cols + [flag, negk], axis=1).astype(np.float32)
        assert cols.shape[1] == NCOL
        xsm = np.zeros((64, D), np.float32)
        xsm[:16] = xs[c]
        W = np.asarray(inp["state_wkv"], np.float32)[0, c].reshape(2, 2, 2, 64, 64)
        st_wkv = W.transpose(1, 3, 0, 2, 4).reshape(128, 256)
        st_conv = np.asarray(inp["state_ffn_conv"], np.float32)[0, c].reshape(2, 44, 128).transpose(2, 1, 0).reshape(128, 88)
        m = dict(shared)
        m.update({
            "x_prev": np.ascontiguousarray(xp[b, 0:TP]) if g == 1 else np.zeros((TP, D), np.float32),
            "x_mine": np.ascontiguousarray(xp[b, g * TP:g * TP + TM]),
            "x_smp": xsm,
            "cache_k": np.ascontiguousarray(np.asarray(inp["cache_k"], np.float32)[0, c].reshape(PAST, 512)),
            "cache_v": np.ascontiguousarray(np.asarray(inp["cache_v"], np.float32)[0, c].reshape(PAST, 512)),
            "st_shift": np.ascontiguousarray(_colpack(np.asarray(inp["state_shift"], np.float32)[0, c, 0], 14)),
            "st_wkv": np.ascontiguousarray(st_wkv), "st_conv": np.ascontiguousarray(st_conv),
            "cols": np.ascontiguousarray(cols),
            "cs_mine": _rope_table(g * TP + np.arange(TM)),
        })
        maps.append({k: np.ascontiguousarray(v, dtype=np.float32) for k, v in m.items()})
    return maps, (B, T, TH, PAST, TP, TM)


def _unwkv(a):
    return a.reshape(2, 64, 2, 2, 64).transpose(2, 0, 3, 1, 4).reshape(8, 64, 64)


def _unconv(a):
    return a.reshape(44, 2, 128).transpose(1, 0, 2).reshape(2, 2 * DFF)


def assemble(res, dims):
    B, T, TH, PAST, TP, TM = dims
    y_p = np.zeros((B, T, D), np.float32)
    y_s = np.zeros((8, 16, D), np.float32)
    nk_p = np.zeros((1, B, T, 4, 128), np.float32)
    nv_p = np.zeros((1, B, T, 4, 128), np.float32)
    nsh_p = np.zeros((1, B, 1, RWKV_IN), np.float32)
    nwkv_p = np.zeros((1, B, 8, 64, 64), np.float32)
    ncv_p = np.zeros((1, B, 2, 2 * DFF), np.float32)
    nk_s = np.zeros((1, 8, 16, 4, 128), np.float32)
    nv_s = np.zeros((1, 8, 16, 4, 128), np.float32)
    nsh_s = np.zeros((1, 8, 1, RWKV_IN), np.float32)
    nwkv_s = np.zeros((1, 8, 8, 64, 64), np.float32)
    ncv_s = np.zeros((1, 8, 2, 2 * DFF), np.float32)
    for c in range(8):
        r = res[c]
        b, g = c // 2, c % 2
        sl = slice(g * TH, (g + 1) * TH)
        ms = slice(0, TH) if g == 0 else slice(TM - TH, TM)
        y_p[b, sl] = r["y_mine"][ms]
        nk_p[0, b, sl] = r["nk_mine"][ms].reshape(TH, 4, 128)
        nv_p[0, b, sl] = r["nv_mine"][ms].reshape(TH, 4, 128)
        if g == 1:
            nsh_p[0, b, 0] = r["nshift"].T.reshape(-1)
            nwkv_p[0, b] = _unwkv(r["nwkv"])
            ncv_p[0, b] = _unconv(r["nconv"])
        y_s[c] = r["y_smp"]
        nk_s[0, c] = r["nk_smp"].reshape(16, 4, 128)
        nv_s[0, c] = r["nv_smp"].reshape(16, 4, 128)
        nsh_s[0, c, 0] = r["nshift_s"].T.reshape(-1)
        nwkv_s[0, c] = _unwkv(r["nwkv_s"])
        ncv_s[0, c] = _unconv(r["nconv_s"])
    return (y_p, y_s, nk_p, nv_p, nsh_p, nwkv_p, ncv_p, nk_s, nv_s, nsh_s, nwkv_s, ncv_s)


_CACHE = {}


def run(inputs, stages=("R", "H", "F")):
    maps, dims = prepare_inputs(inputs)
    B, T, TH, PAST, TP, TM = dims
    key = (TP, TM, PAST, tuple(stages), DEBUG)
    if key not in _CACHE:
        _CACHE[key] = build_program(TP, TM, PAST, stages)
    nc = _CACHE[key]
    res = run_bass_kernel_spmd(nc, maps, core_ids=list(range(8)))
    return res.results, dims


def kernel(**inputs):
    res, dims = run(inputs)
    return assemble(res, dims)


def sweepH_alloc(K, G, dr, hp2, TP, TM, PAST):
    S = NS()
    NT = (TP + TM) // 128
    NPT = PAST // 128
    S.Wq = K.sb("Wq", [128, 8, 768], BF16)
    mark = K.top
    stage = Ring(K, "wstH", 2, [128, 1792])
    for part in range(3):
        col0 = RWKV_IN + part * 512 + hp2 * 256
        load_weight(K, S.Wq[:, :, part * 256:(part + 1) * 256], dr["w_in"], 1024, 256, stage,
                    scale_cols=G.cols[:, C_G1:C_G1 + 8], col0=col0)
    K.P.barrier()
    K.recycle()
    K.top = mark
    S.KT = K.sb("KT", [128, 2, TP + TM], BF16)
    S.Va = K.sb("Va", [128, NT, 2, 130], BF16)
    S.KTs = K.sb("KTs", [128, 2, PAST + 128], BF16)
    S.Vs = K.sb("Vs", [128, NPT + 1, 2, 130], BF16)
    K.memset(POOL, S.Va[:, :, :, 128:129], 1.0)
    K.memset(POOL, S.Vs[:, :, :, 128:129], 1.0)
    K.memset(POOL, S.KTs[:, :, PAST:PAST + 128], 0.0)
    K.memset(POOL, S.Vs[:, NPT, :, 0:128], 0.0)
    S.KTm = [S.KT[:, :, m * 512:(m + 1) * 512].alias(f"KTm{m}") for m in range((TP + TM) // 512)]
    S.Vam = [S.Va[:, m * 4:(m + 1) * 4].alias(f"Vam{m}") for m in range((TP + TM) // 512)]
    S.cs_prev = K.sb("cs_prev", [128, TP // 128, 16])
    S.cs_mine = K.sb("cs_mine", [128, TM // 128, 16])
    S.cs_smp = K.sb("cs_smp", [128, 1, 16])
    K.load(S.cs_prev, dr["cs_prev"])
    K.load(S.cs_mine, dr["cs_mine"])
    K.load(S.cs_smp, dr["cs_smp"])
    S.hT = Ring(K, "hTH", 2, [128, 8, 512], BF16)
    S.qk = Ring(K, "qk", 4, [128, 512])
    S.sq = Ring(K, "sqH", 4, [128, 512])
    S.st = Ring(K, "stH", 4, [128, 32])
    S.rt = Ring(K, "ropet", 4, [128, 4, 8, 8])
    S.qkb = Ring(K, "qkb", 4, [128, 512], BF16)
    S.vf = Ring(K, "vf", 4, [128, 256])
    S.QT = Ring(K, "QT", 2, [128, 2, 512], BF16)
    S.PT2 = Ring(K, "PT2", 3, [128, 2, 512], BF16)
    S.o = Ring(K, "oH", 2, [128, 128])
    S.est = Ring(K, "est", 2, [128, 8])
    S.ydf = Ring(K, "ydf", 2, [128, 128], BF16)
    S.ydfT = Ring(K, "ydfT", 2, [128, 2, 512], BF16)
    S.ck = Ring(K, "ck", 2, [128, 256])
    S.ckb = Ring(K, "ckb", 2, [128, 256], BF16)
    return S


def qkv_tile(K, G, S, hT, c0, TPt, cs, want_q, QT, qcol, KTdst, kcol, Vdst, nk_ap, nv_ap, nrows, blo=4):
    bA = K.bank(blo, 8)
    for kc in range(8):
        K.mm(bA[0:TPt, :], hT[:, kc, c0:c0 + TPt], S.Wq[:, kc, 0:512], start=(kc == 0), stop=(kc == 7))
    bB = K.bank(blo, 8)
    for kc in range(8):
        K.mm(bB[0:TPt, 0:256], hT[:, kc, c0:c0 + TPt], S.Wq[:, kc, 512:768], start=(kc == 0), stop=(kc == 7))
    qk = S.qk.next()[0:TPt]
    st = S.st.next()[0:TPt]
    K.cp(ACT, qk, bA[0:TPt, :])
    vf = S.vf.next()[0:TPt]
    K.cp(ACT, vf, bB[0:TPt, 0:256])
    yield
    sq = S.sq.next()[0:TPt]
    K.tt(POOL, sq, qk, qk, ALU.mult)
    K.reduce(st[:, 0:8], sq.rr("p (a b) -> p a b", b=64))
    K.act(st[:, 8:16], st[:, 0:8], AF.Sqrt, scale=1.0 / 64, bias=G.epsc[0:TPt, 0:1])
    K.recip(st[:, 16:24], st[:, 8:16])
    yield
    qk3 = qk.rr("p (a b) -> p a b", b=64)
    K.tt(DVE, qk3, qk3, st[:, 16:24].us(2).tb([TPt, 8, 64]), ALU.mult)
    K.tt(POOL, qk, qk, G.rows[0:TPt, R_GQK:R_GQK + 512], ALU.mult)
    yield
    x1, x2 = qk3[:, :, 0:8], qk3[:, :, 8:16]
    cosb = cs[0:TPt, 0:8].us(1).tb([TPt, 8, 8])
    sinb = cs[0:TPt, 8:16].us(1).tb([TPt, 8, 8])
    rt = S.rt.next()[0:TPt]
    K.tt(DVE, rt[:, 0], x1, cosb, ALU.mult)
    K.tt(POOL, rt[:, 1], x2, sinb, ALU.mult)
    K.tt(DVE, rt[:, 2], x2, cosb, ALU.mult)
    K.tt(POOL, rt[:, 3], x1, sinb, ALU.mult)
    K.tt(DVE, x1, rt[:, 0], rt[:, 1], ALU.subtract)
    K.tt(POOL, x2, rt[:, 2], rt[:, 3], ALU.add)
    yield
    if nk_ap is not None:
        K.store(nk_ap, qk[0:nrows, 256:512])
        K.store(nv_ap, vf[0:nrows])
    qkb = S.qkb.next()[0:TPt]
    K.cp(ACT, qkb, qk)
    K.cp(POOL, Vdst[0:TPt, :, 0:128], vf.rr("p (h d) -> p h d", d=128))
    yield
    bT = K.bank(blo, 8)
    bTb = bT.bc(BF16)
    blocks = range(4) if want_q else range(2, 4)
    for blk in blocks:
        K.tr(bTb[:, blk * 128:blk * 128 + TPt], qkb[:, blk * 128:(blk + 1) * 128], G.identb[0:TPt, 0:TPt])
    b3 = bTb[:, 0:512].rr("p (a t) -> p a t", t=128)
    if want_q:
        K.cp(DVE, QT[:, :, qcol:qcol + TPt], b3[:, 0:2, 0:TPt])
    K.cp(DVE, KTdst[:, :, kcol:kcol + TPt], b3[:, 2:4, 0:TPt])
    yield


def run_rr(gens):
    active = [g for g in gens if g is not None]
    while active:
        for g in list(active):
            try:
                next(g)
            except StopIteration:
                active.remove(g)


def attn_epilogue(K, G, S, acc, rows, ydfT_dst):
    est = S.est.next()[0:rows]
    o = S.o.next()[0:rows]
    K.recip(est[:, 0:1], acc[:, 128:129])
    K.recip(est[:, 1:2], acc[:, 384:385])
    K.tt(DVE, est[:, 1:2], est[:, 1:2], G.lam[0:rows, 3:4], ALU.mult)
    K.ts(DVE, o, acc[:, 0:128], est[:, 0:1], ALU.mult)
    K.stt(o, acc[:, 256:384], est[:, 1:2], o, ALU.mult, ALU.add)
    K.act(G.junk.alias("j")[0:rows, 0:128], o, AF.Square, accum=est[:, 2:3])
    K.act(est[:, 3:4], est[:, 2:3], AF.Sqrt, scale=1.0 / 128, bias=G.epsc[0:rows, 0:1])
    K.recip(est[:, 4:5], est[:, 3:4])
    ydf = S.ydf.next()[0:rows]
    K.stt(ydf, o, est[:, 4:5], G.gsub[0:rows], ALU.mult, ALU.mult)
    bT = K.bank(4, 8)
    bTb = bT.bc(BF16)
    K.tr(bTb[:, 0:rows], ydf, G.identb[0:rows, 0:rows])
    K.cp(ACT, ydfT_dst, bTb[:, 0:rows])


def sweepH_mt(K, G, S, dr, hp2, m, TP, TM, mine):
    NTP = TP // 128
    t0 = m * 512
    hT = S.hT.next()
    K.load(hT, dr["hT_p"][:, :, t0:t0 + 512])
    QT = S.QT.next()
    gens = []
    for i in range(4):
        tile = m * 4 + i
        if mine:
            cs = S.cs_mine[:, tile - NTP]
            r0 = t0 - TP + i * 128
            nk_ap = dr["nk_mine"][r0:r0 + 128, hp2 * 256:(hp2 + 1) * 256]
            nv_ap = dr["nv_mine"][r0:r0 + 128, hp2 * 256:(hp2 + 1) * 256]
        else:
            cs = S.cs_prev[:, tile]
            nk_ap = nv_ap = None
        gens.append(qkv_tile(K, G, S, hT, i * 128, 128, cs, mine, QT, i * 128, S.KTm[m], i * 128, S.Vam[m][:, i],
                             nk_ap, nv_ap, 128, blo=(4 if mine else 0)))
    if mine:
        run_rr(gens[0:2])
        run_rr(gens[2:4])
    else:
        run_rr(gens)
    import os
    if not mine or int(os.environ.get("HSTOP", 9)) < 3:
        return
    ml = m - TP // 512
    nk = NTP + (ml + 1) * 4
    ydfT = S.ydfT.next()
    for hl in range(2):
        acc = [K.banks[j] for j in range(4)]
        DEPTH = 1

        def stageA(kt):
            ktl = kt - NTP - ml * 4
            q0 = max(0, ktl) * 128
            kc0 = (kt % 4) * 128
            pb = 4 + 2 * (kt % 2)
            for comp in range(2):
                ph = slice(comp * 64, comp * 64 + 64)
                K.mm(K.banks[pb + comp][:, 0:512 - q0], S.KTm[kt // 4][ph, hl, kc0:kc0 + 128], QT[ph, hl, q0:512])

        pts = {}

        def stageB(kt):
            ktl = kt - NTP - ml * 4
            q0 = max(0, ktl) * 128
            Nq = 512 - q0
            pb = 4 + 2 * (kt % 2)
            PT = S.PT2.next()
            pts[kt] = PT
            K.act(PT[:, :, 0:Nq], K.bankpair(pb)[:, :, 0:Nq], AF.Exp,
                  bias=(G.cols[:, C_FLAG:C_FLAG + 1] if kt < NTP else None))
            if ktl >= 0:
                K.memset(POOL, PT[64:128, :, 0:64], 0.0)

        def stageC(kt):
            ktl = kt - NTP - ml * 4
            q0 = max(0, ktl) * 128
            Vs = S.Vam[kt // 4]
            PT = pts.pop(kt)
            for comp in range(2):
                for j in range(q0 // 128, 4):
                    lastkt = NTP + ml * 4 + j
                    K.mm(acc[j][:, comp * 256:comp * 256 + 129], PT[:, comp, j * 128 - q0:j * 128 - q0 + 128],
                         Vs[:, kt % 4, hl, 0:129], start=(kt == 0 and comp == 0),
                         stop=(kt == lastkt and comp == 1))

        for n in range(nk + 2):
            if n < nk:
                stageA(n)
            if 0 <= n - 1 < nk:
                stageB(n - 1)
            if n - 2 >= 0:
                stageC(n - 2)
        for j in range(4):
            attn_epilogue(K, G, S, acc[j], 128, ydfT[:, hl, j * 128:(j + 1) * 128])
    fo = 4 + hp2 * 2
    K.store(dr["ymT_p"][:, fo:fo + 2, t0 - TP:t0 - TP + 512], ydfT)


def sweepH_sample(K, G, S, dr, hp2, PAST):
    NPT = PAST // 128
    hT = S.hT.next()
    K.load(hT[:, :, 0:64], dr["hT_s"])
    QT = S.QT.next()
    run_rr([qkv_tile(K, G, S, hT, 0, 64, S.cs_smp[:, 0], True, QT, 0, S.KTs, PAST, S.Vs[:, NPT],
                     dr["nk_smp"][:, hp2 * 256:(hp2 + 1) * 256], dr["nv_smp"][:, hp2 * 256:(hp2 + 1) * 256], 16)])
    import os
    SSTOP = int(os.environ.get("SSTOP", 9))
    for ct in range(NPT if SSTOP >= 2 else 0):
        ck = S.ck.next()
        K.load(ck, dr["cache_k"][ct * 128:(ct + 1) * 128, hp2 * 256:(hp2 + 1) * 256])
        ckb = S.ckb.next()
        K.cp(POOL, ckb, ck)
        bT = K.bank(4, 8)
        bTb = bT.bc(BF16)
        for hl in range(2):
            K.tr(bTb[:, hl * 128:(hl + 1) * 128], ckb[:, hl * 128:(hl + 1) * 128], G.identb)
        K.cp(DVE, S.KTs[:, :, ct * 128:(ct + 1) * 128], bTb[:, 0:256].rr("p (a t) -> p a t", t=128))
        cv = S.ck.next()
        K.load(cv, dr["cache_v"][ct * 128:(ct + 1) * 128, hp2 * 256:(hp2 + 1) * 256])
        K.cp(POOL, S.Vs[:, ct, :, 0:128], cv.rr("p (h d) -> p h d", d=128))
    ydfT = S.ydfT.next()
    for hl in range(2):
        acc = K.banks[hl]
        nk = NPT + 1
        pts = {}

        def sA(kt):
            pb = 4 + 2 * (kt % 2)
            for comp in range(2):
                ph = slice(comp * 64, comp * 64 + 64)
                K.mm(K.banks[pb + comp][:, 0:128], S.KTs[ph, hl, kt * 128:kt * 128 + 128], QT[ph, hl, 0:128])

        def sB(kt):
            pb = 4 + 2 * (kt % 2)
            PT = S.PT2.next()
            pts[kt] = PT
            K.act(PT[:, :, 0:128], K.bankpair(pb)[:, :, 0:128], AF.Exp,
                  bias=(G.cols[:, C_NEGK:C_NEGK + 1] if kt == NPT else None))

        def sC(kt):
            PT = pts.pop(kt)
            for comp in range(2):
                K.mm(acc[:, comp * 256:comp * 256 + 129], PT[:, comp, 0:128], S.Vs[:, kt, hl, 0:129],
                     start=(kt == 0 and comp == 0), stop=(kt == NPT and comp == 1))

        for n in range(nk + 2):
            if n < nk:
                sA(n)
            if 0 <= n - 1 < nk:
                sB(n - 1)
            if n - 2 >= 0:
                sC(n - 2)
        if SSTOP >= 4:
            attn_epilogue(K, G, S, acc[0:16], 16, ydfT[:, hl, 0:16])
    fo = 4 + hp2 * 2
    if True:
        K.store(dr["ymT_s"][:, fo:fo + 2, 0:16], ydfT[:, :, 0:16])


FF_PIECES = [(0, 6), (6, 6), (12, 6), (18, 4)]


def wup_convert(K, G, dr, stage, ost):
    i = 0
    for pi, (g0, n) in enumerate(FF_PIECES):
        for isup in range(2):
            c0 = (isup * 22 + g0) * 128
            for kc in range(8):
                st = stage.next()
                K.load(st[:, 0:n * 128], dr["w_up"][kc * 128:(kc + 1) * 128, c0:c0 + n * 128])
                ob = ost.next()
                sc = G.cols[:, C_G2 + kc:C_G2 + kc + 1]
                if i % 2 == 0:
                    K.act(ob[:, 0:n * 128], st[:, 0:n * 128], AF.Copy, scale=sc)
                else:
                    K.ts(DVE, ob[:, 0:n * 128], st[:, 0:n * 128], sc, ALU.mult)
                i += 1
                K.store(dr["wup_bf"][pi * 2 + isup, :, kc * 768:kc * 768 + n * 128], ob[:, 0:n * 128])
                yield


def phaseF_alloc(K, G, dr, wup_done=False):
    S = NS()
    S.Wo = K.sb("Wo", [128, 8, D], BF16)
    S.Wd = K.sb("Wd", [128, 22, D], BF16)
    mark = K.top
    stage = Ring(K, "wstF", 2, [128, 1792])
    load_weight(K, S.Wo, dr["w_out"], 1024, D, stage)
    load_weight(K, S.Wd, dr["w_down"], DFF, D, stage)
    if not wup_done:
        ost = Ring(K, "wupo", 2, [128, 768], BF16)
        for _ in wup_convert(K, G, dr, stage, ost):
            pass
    K.P.barrier()
    K.recycle()
    K.top = mark
    S.wb = Ring(K, "wb", 3, [128, 8, 768], BF16)
    S.ymT = K.sb("ymTF", [128, 8, 512], BF16)
    S.x = Ring(K, "xF", 2, [128, D])
    S.xmid = K.sb("xmid", [128, 4, D])
    S.ss = Ring(K, "ssF", 4, [128, 4])
    S.h2b = K.sb("h2b", [128, D], BF16)
    S.h2T = K.sb("h2T", [128, 8, 512], BF16)
    S.uS = Ring(K, "uS", 3, [128, 520])
    S.ct = Ring(K, "ct", 3, [128, 512])
    S.sg = K.sb("sg", [128, 6, 512], BF16)
    S.mT = K.sb("mT", [128, 22, 512], BF16)
    S.cvo = K.sb("cvo", [128, 128])
    return S


def phaseF_mt(K, G, S, dr, ucarry, ym_ap, x_ap, y_ap, N):
    cols = G.cols
    TPt = min(N, 128)
    NTT = max(1, N // 128)
    K.load(S.ymT[:, :, 0:N], ym_ap)
    xm_tiles = []
    for i in range(NTT):
        xt = S.x.next()
        K.load(xt[0:TPt], x_ap[i * 128:i * 128 + TPt, :])
        xm = S.xmid[0:TPt, i].alias(f"xmid{i}")
        xm_tiles.append(xm)
        for half in range(2):
            hs = slice(half * 512, (half + 1) * 512)
            bk = K.bank()
            for kc in range(8):
                K.mm(bk[0:TPt, :], S.ymT[:, kc, i * 128:i * 128 + TPt], S.Wo[:, kc, hs], start=(kc == 0), stop=(kc == 7))
            K.tt(DVE, xm[:, hs], xt[0:TPt, hs], bk[0:TPt, :], ALU.add)
        ss = S.ss.next()
        K.act(G.junk.alias("j")[0:TPt], xm, AF.Square, accum=ss[0:TPt, 0:1])
        K.act(ss[0:TPt, 1:2], ss[0:TPt, 0:1], AF.Sqrt, scale=1.0 / D, bias=G.epsc[0:TPt, 0:1])
        K.recip(ss[0:TPt, 2:3], ss[0:TPt, 1:2])
        K.act(S.h2b[0:TPt], xm, AF.Copy, scale=ss[0:TPt, 2:3])
        bk = K.bank()
        bkb = bk.bc(BF16)
        for kc in range(8):
            K.tr(bkb[:, kc * 128:kc * 128 + TPt], S.h2b[0:TPt, kc * 128:(kc + 1) * 128], G.identb[0:TPt, 0:TPt])
        K.cp(DVE, S.h2T[:, :, i * 128:i * 128 + TPt], bkb.rr("p (k t) -> p k t", t=128)[:, :, 0:TPt])
    for pi, (g0, n) in enumerate(FF_PIECES):
        for isup in range(2):
            wb = S.wb.next()
            pending = None
            K.load(wb[:, :, 0:n * 128],
                   dr["wup_bf"][pi * 2 + isup].rearrange("p (k c) -> p k c", c=768)[:, :, 0:n * 128], eng=SP)
            for f in range(n):
                fc = isup * 22 + g0 + f
                bk = K.bank()
                for kc in range(8):
                    K.mm(bk[:, 0:N], wb[:, kc, f * 128:(f + 1) * 128], S.h2T[:, kc, 0:N], start=(kc == 0), stop=(kc == 7))
                uS = S.uS.next()
                K.cp(POOL, uS[:, 0:2], ucarry[:, 2 * fc:2 * fc + 2])
                K.cp(ACT, uS[:, 2:N + 2], bk[:, 0:N])
                K.cp(POOL, ucarry[:, 2 * fc:2 * fc + 2], uS[:, N:N + 2])
                ct = S.ct.next()[:, 0:N]
                K.act(ct, uS[:, 2:N + 2], AF.Identity, scale=cols[:, C_CW + 88 + fc:C_CW + 88 + fc + 1],
                      bias=cols[:, C_CB + fc:C_CB + fc + 1])
                K.stt(ct, uS[:, 1:N + 1], cols[:, C_CW + 44 + fc:C_CW + 44 + fc + 1], ct, ALU.mult, ALU.add)
                K.stt(ct, uS[:, 0:N], cols[:, C_CW + fc:C_CW + fc + 1], ct, ALU.mult, ALU.add)
                if pending is not None:
                    pending()

                def fin(ct=ct, f=f, isup=isup, g0=g0):
                    if not isup:
                        K.act(S.sg[:, f, 0:N], ct, AF.Silu)
                    else:
                        K.tt(POOL, S.mT[:, g0 + f, 0:N], ct, S.sg[:, f, 0:N], ALU.mult)
                pending = fin
            if pending is not None:
                pending()
                pending = None
    for i in range(NTT):
        xm = xm_tiles[i]
        for half in range(2):
            hs = slice(half * 512, (half + 1) * 512)
            bk = K.bank()
            for f in range(22):
                K.mm(bk[0:TPt, :], S.mT[:, f, i * 128:i * 128 + TPt], S.Wd[:, f, hs], start=(f == 0), stop=(f == 21))
            K.tt(DVE, xm[:, hs], xm[:, hs], bk[0:TPt, :], ALU.add)
        K.store(y_ap[i * 128:i * 128 + TPt, :], xm)


def conv_out(K, G, S, ucarry, out_ap):
    bk = K.bank()
    K.mm(bk[0:88, 0:128], ucarry, G.ident)
    K.cp(ACT, S.cvo[0:88], bk[0:88, 0:128])
    K.store(out_ap, S.cvo[0:88])
```

```python
import contextlib
import math
import numpy as np
import concourse.bass as bass
import concourse.mybir as mybir
from concourse.bass_utils import run_bass_kernel_spmd

F32 = mybir.dt.float32
BF16 = mybir.dt.bfloat16
AF = mybir.ActivationFunctionType
ALU = mybir.AluOpType
AX = mybir.AxisListType

PE, ACT, DVE, POOL, SP = "tensor", "scalar", "vector", "gpsimd", "sync"
ENGINES = [PE, ACT, DVE, POOL, SP]

D = 1024
RW = 512
RWKV_IN = 1792
IN_DIM = 3328
DFF = 2816
NORM_EPS = 1e-6
GN_EPS = 64e-5
ROPE_THETA = 500000.0
DEC_C = math.exp(-0.5)
LAM_INIT = 0.8 - 0.6 * math.exp(-0.3 * 0)
NEG = -30000.0


class Buf:
    __slots__ = ("name", "last_w", "readers", "psum")

    def __init__(self, name, psum=False):
        self.name = name
        self.last_w = None
        self.readers = []
        self.psum = psum


class V:
    __slots__ = ("ap", "buf")

    def __init__(self, ap, buf):
        self.ap = ap
        self.buf = buf

    def __getitem__(self, k):
        return V(self.ap[k], self.buf)

    def rr(self, s, **kw):
        return V(self.ap.rearrange(s, **kw), self.buf)

    def bc(self, dt):
        return V(self.ap.bitcast(dt), self.buf)

    def us(self, ax):
        return V(self.ap.unsqueeze(ax), self.buf)

    def tb(self, shape):
        return V(self.ap.to_broadcast(list(shape)), self.buf)

    def alias(self, name):
        return V(self.ap, Buf(name))


class Op:
    __slots__ = ("eng", "fn", "reads", "writes", "dma_sem", "deps", "sig", "idx", "acc", "dma_cnt", "bar")

    def __init__(self, eng, fn, reads, writes, dma_sem, acc):
        self.eng = eng
        self.fn = fn
        self.reads = reads
        self.writes = writes
        self.dma_sem = dma_sem
        self.deps = None
        self.sig = None
        self.acc = acc
        self.dma_cnt = None
        self.bar = None


class Prog:
    def __init__(self):
        self.ops = []
        self.group_sems = set()

    def op(self, eng, fn, reads=(), writes=(), dma_sem=None, acc=False):
        def bufs(vs):
            out = []
            for v in vs:
                if v is None:
                    continue
                if isinstance(v.buf, (list, tuple)):
                    out.extend(v.buf)
                else:
                    out.append(v.buf)
            return out
        o = Op(eng, fn, bufs(reads), bufs(writes), dma_sem, acc)
        o.idx = len(self.ops)
        self.ops.append(o)
        return o

    def barrier(self):
        if self.ops:
            self.ops[-1].bar = True

    def schedule(self):
        last_on = {e: None for e in ENGINES}
        last_dma = {}
        pend = {e: [] for e in ENGINES}
        for o in self.ops:
            deps = set(pend[o.eng])
            pend[o.eng] = []
            for b in o.reads:
                if b.last_w is not None:
                    deps.add(b.last_w)
            for b in o.writes:
                if b.last_w is not None:
                    deps.add(b.last_w)
                deps.update(b.readers)
            deps.discard(o)
            o.deps = deps
            for b in o.reads:
                if b.psum and b.readers and b.readers[0].eng != o.eng:
                    raise AssertionError(f"PSUM bank {b.name} read by two engines ({b.readers[0].eng}, {o.eng})")
                b.readers.append(o)
            for b in o.writes:
                b.last_w = o
                b.readers = []
            if o.dma_sem is not None:
                last_dma[id(o.dma_sem)] = o
            else:
                last_on[o.eng] = o
            if o.bar:
                allp = [x for x in last_on.values() if x is not None] + list(last_dma.values())
                for e in ENGINES:
                    pend[e] = list(allp)
        known = {e: {p: -1 for p in ENGINES} for e in ENGINES}
        dma_known = {e: {} for e in ENGINES}
        waits = []
        for o in self.ops:
            need = {}
            for d in o.deps:
                if d.dma_sem is not None:
                    key = ("dma", id(d.dma_sem))
                    if d.idx > dma_known[o.eng].get(key[1], -1):
                        cur = need.get(key)
                        if cur is None or d.idx > cur.idx:
                            need[key] = d
                else:
                    if d.eng == PE and o.eng == PE and d.acc and o.acc:
                        continue
                    if d.idx > known[o.eng][d.eng]:
                        key = ("eng", d.eng)
                        cur = need.get(key)
                        if cur is None or d.idx > cur.idx:
                            need[key] = d
            wl = []
            for (kind, key), d in need.items():
                wl.append(d)
                if kind == "dma":
                    dma_known[o.eng][key] = d.idx
                else:
                    known[o.eng][d.eng] = d.idx
                    d.sig = True
            waits.append(wl)
        self.waits = waits

    def emit(self, nc, sems):
        self.schedule()
        cnt = {e: 0 for e in ENGINES}
        dcnt = {}
        for o in self.ops:
            if o.dma_sem is not None:
                k = id(o.dma_sem)
                dcnt[k] = dcnt.get(k, 0) + 16
                o.dma_cnt = dcnt[k]
            elif o.sig:
                cnt[o.eng] += 1
                o.sig = cnt[o.eng]
        for o in self.ops:
            if o.dma_sem is not None and id(o.dma_sem) in self.group_sems:
                o.dma_cnt = dcnt[id(o.dma_sem)]
        print('SEMCOUNTS', cnt, 'max dma', max(dcnt.values()) if dcnt else 0, 'nops', len(self.ops), flush=True)
        for e in ENGINES:
            assert cnt[e] < 60000, (e, cnt[e])
        for v in dcnt.values():
            assert v < 60000, v
        per_eng = {e: [o for o in self.ops if o.eng == e] for e in ENGINES}
        last_dma = {}
        for o in self.ops:
            if o.dma_sem is not None:
                last_dma[id(o.dma_sem)] = o
        waits = self.waits

        with nc.Block() as block:
            def body(engname):
                def f(eng):
                    for o in per_eng[engname]:
                        for d in waits[o.idx]:
                            if d.dma_sem is not None:
                                eng.wait_ge(d.dma_sem, d.dma_cnt)
                            else:
                                eng.wait_ge(sems[d.eng], d.sig)
                        ins = o.fn(eng)
                        if o.dma_sem is not None:
                            ins.then_inc(o.dma_sem, 16)
                        elif o.sig:
                            ins.then_inc(sems[o.eng], 1)
                    if engname == SP:
                        for o in last_dma.values():
                            eng.wait_ge(o.dma_sem, o.dma_cnt)
                return f
            block.tensor(body(PE))
            block.scalar(body(ACT))
            block.vector(body(DVE))
            block.gpsimd(body(POOL))
            block.sync(body(SP))


class Ctx:
    def __init__(self, nc, es, arena_words=53000):
        self.nc, self.es = nc, es
        self.P = Prog()
        self.sems = {e: es.enter_context(nc.semaphore("s_" + e)) for e in ENGINES}
        self.arena = es.enter_context(nc.sbuf_tensor("arena", [128, arena_words], F32))
        self.words = arena_words
        self.top = 0
        self.banks = []
        self.ps = es.enter_context(nc.psum_tensor("psall", [128, 4096], F32))
        for i in range(8):
            self.banks.append(V(self.ps[:, i * 512:(i + 1) * 512], Buf(f"bank{i}", psum=True)))
        self.bi = 0
        self.nsem = 0
        self.lsem = {}
        self.ssem = {}
        self.gsem = self.newsem()
        self.P.group_sems.add(id(self.gsem))
        self.rot = {ACT: 0}

    def newsem(self):
        if getattr(self, "free_sems", None):
            return self.free_sems.pop()
        self.nsem += 1
        return self.es.enter_context(self.nc.semaphore(f"d{self.nsem}"))

    def recycle(self):
        if not hasattr(self, "free_sems"):
            self.free_sems = []
        self.free_sems.extend(self.lsem.values())
        self.free_sems.extend(self.ssem.values())
        self.lsem = {}
        self.ssem = {}

    def sb(self, name, shape, dt=F32, at=None):
        n = 1
        for s in shape[1:]:
            n *= s
        words = n if dt == F32 else (n + 1) // 2
        words = (words + 7) // 8 * 8
        if at is None:
            assert self.top + words <= self.words, (name, self.top, words)
            off = self.top
            self.top += words
        else:
            off = at
        self.last_alloc = (off, words)
        ap = self.arena[:, off:off + words]
        if dt == BF16:
            ap = ap.bitcast(BF16)
        ap = ap[:, 0:n]
        if len(shape) == 3:
            ap = ap.rearrange("p (a b) -> p a b", b=shape[2])
        elif len(shape) == 4:
            ap = ap.rearrange("p (a b c) -> p a b c", b=shape[2], c=shape[3])
        if shape[0] != 128:
            ap = ap[0:shape[0]]
        return V(ap, Buf(name))

    def bankpair(self, i):
        ap = self.ps[:, i * 512:(i + 2) * 512].rearrange("p (b n) -> p b n", b=2)
        return V(ap, [self.banks[i].buf, self.banks[i + 1].buf])

    def bank(self, lo=0, hi=8):
        n = hi - lo
        b = self.banks[lo + (self.bi % n)]
        self.bi += 1
        return b

    def mm(self, out, lhsT, rhs, start=True, stop=True):
        self.P.op(PE, lambda e: e.matmul(out.ap, lhsT=lhsT.ap, rhs=rhs.ap, start=start, stop=stop),
                  [lhsT, rhs], [out], acc=True)

    def tr(self, out, in_, ident):
        self.P.op(PE, lambda e: e.transpose(out.ap, in_.ap, ident.ap), [in_, ident], [out], acc=True)

    def act(self, out, in_, func, bias=None, scale=None, accum=None):
        kw = {}
        reads = [in_]
        if bias is not None:
            kw["bias"] = bias.ap if isinstance(bias, V) else float(bias)
            if isinstance(bias, V):
                reads.append(bias)
        if scale is not None:
            kw["scale"] = scale.ap if isinstance(scale, V) else float(scale)
            if isinstance(scale, V):
                reads.append(scale)
        writes = [out]
        if accum is not None:
            kw["accum_out"] = accum.ap
            writes.append(accum)
        self.P.op(ACT, lambda e: e.activation(out=out.ap, in_=in_.ap, func=func, **kw), reads, writes)

    def tt(self, eng, out, a, b, op):
        self.P.op(eng, lambda e: e.tensor_tensor(out=out.ap, in0=a.ap, in1=b.ap, op=op), [a, b], [out])

    def ts(self, eng, out, a, s1, op0, s2=None, op1=None):
        reads = [a] + [s for s in (s1, s2) if isinstance(s, V)]
        v1 = s1.ap if isinstance(s1, V) else float(s1)
        v2 = None if s2 is None else (s2.ap if isinstance(s2, V) else float(s2))
        if op1 is None:
            self.P.op(eng, lambda e: e.tensor_scalar(out=out.ap, in0=a.ap, scalar1=v1, scalar2=None, op0=op0),
                      reads, [out])
        else:
            self.P.op(eng, lambda e: e.tensor_scalar(out=out.ap, in0=a.ap, scalar1=v1, scalar2=v2, op0=op0,
                                                      op1=op1), reads, [out])

    def stt(self, out, a, s, b, op0, op1):
        reads = [a, b] + ([s] if isinstance(s, V) else [])
        sv = s.ap if isinstance(s, V) else float(s)
        self.P.op(DVE, lambda e: e.scalar_tensor_tensor(out=out.ap, in0=a.ap, scalar=sv, in1=b.ap, op0=op0,
                                                         op1=op1), reads, [out])

    def cp(self, eng, out, in_):
        if eng == ACT:
            self.act(out, in_, AF.Copy)
        else:
            self.P.op(eng, lambda e: e.tensor_copy(out=out.ap, in_=in_.ap), [in_], [out])

    def memset(self, eng, out, val):
        self.P.op(eng, lambda e: e.memset(out.ap, float(val)), [], [out])

    def fence(self, dummy, vs):
        self.P.op(POOL, lambda e: e.memset(dummy.ap, 0.0), [], [dummy] + list(vs))

    def recip(self, out, in_):
        self.P.op(DVE, lambda e: e.reciprocal(out=out.ap, in_=in_.ap), [in_], [out])

    def reduce(self, out, in_, op=ALU.add):
        self.P.op(DVE, lambda e: e.tensor_reduce(out=out.ap, in_=in_.ap, axis=AX.X, op=op), [in_], [out])

    def scan(self, out, d0, d1):
        self.P.op(DVE, lambda e: e.tensor_tensor_scan(out=out.ap, data0=d0.ap, data1=d1.ap, initial=0.0,
                                                       op0=ALU.mult, op1=ALU.add), [d0, d1], [out])

    def load(self, dst, src_ap, eng=SP, group=False):
        if group:
            sem = self.gsem
        else:
            k = id(dst.buf)
            if k not in self.lsem:
                self.lsem[k] = self.newsem()
            sem = self.lsem[k]
        self.P.op(eng, lambda e: e.dma_start(out=dst.ap, in_=src_ap), [], [dst], dma_sem=sem)

    def store(self, dst_ap, src, eng=POOL):
        k = id(src.buf)
        if k not in self.ssem:
            self.ssem[k] = self.newsem()
        sem = self.ssem[k]
        self.P.op(eng, lambda e: e.dma_start(out=dst_ap, in_=src.ap), [src], [], dma_sem=sem)


class Ring:
    def __init__(self, K, name, n, shape, dt=F32):
        self.slots = [K.sb(f"{name}{i}", shape, dt) for i in range(n)]
        self.i = 0

    def next(self):
        s = self.slots[self.i % len(self.slots)]
        self.i += 1
        return s


C_G1, C_G2, C_MU, C_W0, C_A0, C_KK, C_KA, C_RK, C_GNG, C_GNB, C_CW, C_CB, C_FLAG, C_NEGK = (
    0, 8, 16, 30, 34, 38, 42, 46, 50, 54, 58, 190, 234, 235)
NCOL = 236
R_GQK, R_GSUB, R_LQ, R_LK = 0, 512, 640, 768
NROW = 896


class NS:
    pass


def blkview(x, c):
    return x.rr("p (f c t) -> p f c t", f=4, c=2, t=64)[:, :, c]


def setup_common(K, dr):
    G = NS()
    G.cols = K.sb("cols", [128, NCOL])
    G.rows = K.sb("rows", [128, NROW])
    G.ident = K.sb("ident", [128, 128])
    G.masks = K.sb("masks", [128, 5, 512])
    G.identb = K.sb("identb", [128, 128], BF16)
    G.bonesb = K.sb("bonesb", [128, 128], BF16)
    G.valid = K.sb("valid", [128, 64])
    G.oka = K.sb("oka", [128, 4])
    G.lam = K.sb("lam", [128, 4])
    G.gsub = K.sb("gsub", [128, 128])
    tmpb = K.sb("tmpb", [128, 128])
    K.load(G.cols, dr["cols"], group=True)
    K.load(G.rows, dr["rows"][0:1, :].partition_broadcast(128), group=True)
    K.load(G.ident, dr["c_ident"], group=True)
    K.load(G.masks, dr["c_masks"], group=True)
    K.load(tmpb, dr["c_bones"], group=True)
    K.load(G.valid, dr["c_valid"], group=True)
    K.cp(DVE, G.identb, G.ident)
    K.cp(DVE, G.bonesb, tmpb)
    K.ts(POOL, G.oka, G.cols[:, C_KA:C_KA + 4], -1.0, ALU.mult, 1.0, ALU.add)
    prod = K.sb("lamprod", [128, 128])
    K.tt(DVE, prod, G.rows[:, R_LQ:R_LQ + 128], G.rows[:, R_LK:R_LK + 128], ALU.mult)
    K.reduce(G.lam[:, 0:2], prod.rr("p (a b) -> p a b", b=64))
    K.act(G.lam[:, 0:2], G.lam[:, 0:2], AF.Exp)
    K.tt(DVE, G.lam[:, 2:3], G.lam[:, 0:1], G.lam[:, 1:2], ALU.subtract)
    K.ts(DVE, G.lam[:, 3:4], G.lam[:, 2:3], LAM_INIT, ALU.add, -1.0, ALU.mult)
    K.ts(POOL, G.gsub, G.rows[:, R_GSUB:R_GSUB + 128], 1.0 - LAM_INIT, ALU.mult)
    K.ts(POOL, G.rows[:, R_GQK:R_GQK + 256], G.rows[:, R_GQK:R_GQK + 256], 0.125, ALU.mult)
    G.junk = K.sb("junk", [128, 1024], BF16)
    G.epsc = K.sb("epsc", [128, 4])
    K.memset(POOL, G.epsc[:, 0:1], NORM_EPS)
    K.memset(POOL, G.epsc[:, 1:2], GN_EPS)
    K.memset(POOL, G.epsc[:, 2:3], 1e-30)
    K.memset(POOL, G.epsc[:, 3:4], 0.0)
    return G


def load_weight(K, dst, src, rows, ncols, stage, scale_cols=None, col0=0):
    engs = [ACT, DVE, POOL] if scale_cols is None else [ACT, DVE]
    i = 0
    for kc in range(rows // 128):
        for c0 in range(0, ncols, 1792):
            cw = min(1792, ncols - c0)
            st = stage.next()
            K.load(st[:, 0:cw], src[kc * 128:(kc + 1) * 128, col0 + c0:col0 + c0 + cw])
            eng = engs[i % len(engs)]
            i += 1
            d = dst[:, kc, c0:c0 + cw]
            if scale_cols is None:
                K.cp(eng, d, st[:, 0:cw])
            elif eng == ACT:
                K.act(d, st[:, 0:cw], AF.Copy, scale=scale_cols[:, kc:kc + 1])
            else:
                K.ts(eng, d, st[:, 0:cw], scale_cols[:, kc:kc + 1], ALU.mult)


def rmsnorm_to_hT(K, G, S, x_rows, TPt, hT, col0):
    xt = S.x.next()
    K.load(xt[0:TPt], x_rows)
    ss = S.ss.next()
    K.act(G.junk.alias("j")[0:TPt], xt[0:TPt], AF.Square, accum=ss[0:TPt, 0:1])
    K.act(ss[0:TPt, 1:2], ss[0:TPt, 0:1], AF.Sqrt, scale=1.0 / D, bias=G.epsc[0:TPt, 0:1])
    K.recip(ss[0:TPt, 2:3], ss[0:TPt, 1:2])
    hb = S.hb.next()
    K.act(hb[0:TPt], xt[0:TPt], AF.Copy, scale=ss[0:TPt, 2:3])
    bk = K.bank()
    bkb = bk.bc(BF16)
    for kc in range(8):
        K.tr(bkb[:, kc * 128:kc * 128 + TPt], hb[0:TPt, kc * 128:(kc + 1) * 128], G.identb[0:TPt, 0:TPt])
    K.cp(DVE, hT[:, :, col0:col0 + TPt], bkb.rr("p (k t) -> p k t", t=128)[:, :, 0:TPt])
    return xt


def blockmm(K, lhs, rhs, fcs_cols, NC, lhs3=None, rhs3=None):
    bk = K.bank()
    for fc in range(4):
        for c in range(NC):
            blk = fc * 2 + c
            for hp in range(2):
                ph = slice(hp * 64, hp * 64 + 64)
                bc = slice(blk * 64, blk * 64 + 64)
                if lhs3 is not None:
                    cs = slice(lhs3[1] + c * 64, lhs3[1] + c * 64 + 64)
                    a = lhs3[0][ph, fc, cs]
                else:
                    a = lhs[ph, bc]
                if rhs3 is not None:
                    cs = slice(rhs3[1] + c * 64, rhs3[1] + c * 64 + 64)
                    b = rhs3[0][ph, fc, cs]
                else:
                    b = rhs[ph, bc]
                K.mm(bk[ph, bc], a, b)
    return bk


def sweepR_alloc(K, G, dr):
    S = NS()
    S.Wr = K.sb("Wr", [128, 8, RWKV_IN], BF16)
    S.Wl = K.sb("Wl", [128, 2, RW], BF16)
    S.Wg = K.sb("Wg", [128, RW], BF16)
    mark = K.top
    stage = Ring(K, "wst", 2, [128, 1792])
    load_weight(K, S.Wr, dr["w_in"], 1024, RWKV_IN, stage, scale_cols=G.cols[:, C_G1:C_G1 + 8])
    st = stage.next()
    K.load(st[0:64, 0:RW], dr["w_w2"])
    K.load(st[64:128, 0:RW], dr["a_w2"])
    K.cp(DVE, S.Wl[0:64, 0], st[0:64, 0:RW])
    K.cp(DVE, S.Wl[64:128, 1], st[64:128, 0:RW])
    st = stage.next()
    K.load(st[:, 0:RW], dr["g_w2"])
    K.cp(DVE, S.Wg, st[:, 0:RW])
    K.P.barrier()
    K.recycle()
    K.top = mark
    S.x = Ring(K, "x", 2, [128, D])
    S.ss = Ring(K, "ss", 4, [128, 4])
    S.hb = Ring(K, "hb", 2, [128, D], BF16)
    S.hT = Ring(K, "hT", 2, [128, 8, 512], BF16)
    S.pS = Ring(K, "pS", 2, [128, 520])
    S.zT = K.sb("zT", [128, 14, 512])
    S.lora = K.sb("lora", [128, 512], BF16)
    S.sgl = K.sb("sgl", [128, 512], BF16)
    for n in ("aT", "rT", "bT", "kT", "vT", "rk", "g_sb", "yout"):
        setattr(S, n, K.sb(n, [128, 4, 512], BF16))
    for n in ("lwr", "alr", "cum", "E", "Einv", "cump", "kk", "nrm", "tfac", "bvec"):
        setattr(S, n, K.sb(n, [128, 512]))
    S.Eprev, S.rn, S.kkn, S.kmod = S.cump, S.nrm, S.kk, S.tfac
    S.kk2 = K.sb("kk2", [128, 512], BF16)
    S.gC = K.sb("gC", [128, 4, 8])
    TILE_BF = ("Vtok", "Ktok", "Btok", "M", "Nm", "AKT", "RBT", "RKT", "Pm", "Qm", "M2a", "N2a")
    S.sets = []
    for si in range(2):
        B = NS()
        for n in TILE_BF:
            setattr(B, n, K.sb(f"{n}_{si}", [128, 512], BF16))
        S.sets.append(B)
    shared = {}
    for n in ("RHSb", "Ub", "ynb"):
        shared[n] = K.sb(n, [128, 512], BF16)
    for n, shp in (("Ysb", [128, 512]), ("Ysq", [128, 512]), ("yst", [128, 8, 8]), ("t1", [128, 4, 128]),
                   ("t2", [128, 4, 128])):
        shared[n] = K.sb(n, shp)
    for B in S.sets:
        for n, v in shared.items():
            setattr(B, n, v)
    S.wkvio = K.sb("wkvio", [128, 256])
    return S


def new_seq(K, name):
    q = NS()
    q.ST = K.sb(name + "ST", [128, 256])
    q.STb = K.sb(name + "STb", [128, 256], BF16)
    q.carry = K.sb(name + "carry", [128, 14])
    return q


def sweepR_mt(K, G, S, dr, seq, x_ap, N, hT_dram, tok0, mine, ym_dram=None, ym0=0, nvalid=None):
    cols = G.cols
    TPt = min(N, 128)
    NTT = max(1, N // 128)
    NC = 2 if N >= 128 else 1
    NCH = N // 64
    valid = G.valid if nvalid is not None else None
    nv = N if nvalid is None else nvalid
    hT = S.hT.next()
    for i in range(NTT):
        rmsnorm_to_hT(K, G, S, x_ap[i * 128:i * 128 + TPt, :], TPt, hT, i * 128)
    K.store(hT_dram[:, :, tok0:tok0 + N], hT[:, :, 0:N])
    if STOP_AT <= 1:
        return
    for fc in range(14):
        bk = K.bank()
        for kc in range(8):
            K.mm(bk[:, 0:N], S.Wr[:, kc, fc * 128:(fc + 1) * 128], hT[:, kc, 0:N], start=(kc == 0), stop=(kc == 7))
        pS = S.pS.next()
        K.cp(POOL, pS[:, 0:1], seq.carry[:, fc:fc + 1])
        K.cp(ACT, pS[:, 1:N + 1], bk[:, 0:N])
        K.cp(POOL, seq.carry[:, fc:fc + 1], pS[:, nv:nv + 1])
        z = S.zT[:, fc, 0:N]
        K.tt(POOL, z, pS[:, 0:N], pS[:, 1:N + 1], ALU.subtract)
        K.stt(z, z, cols[:, C_MU + fc:C_MU + fc + 1], pS[:, 1:N + 1], ALU.mult, ALU.add)
        if valid is not None:
            K.tt(POOL, z, z, valid[:, 0:N], ALU.mult)
    if STOP_AT <= 2:
        return
    K.act(S.lora[0:64, 0:N], S.zT[0:64, 12, 0:N], AF.Tanh)
    K.cp(POOL, S.lora[64:128, 0:N], S.zT[64:128, 12, 0:N])
    K.act(S.sgl[:, 0:N], S.zT[:, 13, 0:N], AF.Sigmoid)
    for fc in range(4):
        fs = slice(fc * 128, (fc + 1) * 128)
        zr, zk, zv = S.zT[:, fc, 0:N], S.zT[:, 4 + fc, 0:N], S.zT[:, 8 + fc, 0:N]
        lwr, alr, cum, E, Einv, cump, Eprev = (x[:, 0:N] for x in (S.lwr, S.alr, S.cum, S.E, S.Einv, S.cump, S.Eprev))
        kk, nrm, rn, kkn, tfac, kmod, bvec = (x[:, 0:N] for x in (S.kk, S.nrm, S.rn, S.kkn, S.tfac, S.kmod, S.bvec))
        K.act(kk, zk, AF.Copy, scale=cols[:, C_KK + fc:C_KK + fc + 1])
        K.tt(POOL, S.kk2[:, 0:N], kk, kk, ALU.mult)
        b4 = K.bank()
        K.mm(b4[:, 0:N], G.bonesb, S.kk2[:, 0:N])
        b1 = K.bank()
        K.mm(b1[:, 0:N], S.Wl[0:64, 0, fs], S.lora[0:64, 0:N])
        b2 = K.bank()
        K.mm(b2[:, 0:N], S.Wl[64:128, 1, fs], S.lora[64:128, 0:N])
        b3 = K.bank()
        K.mm(b3[:, 0:N], S.Wg[:, fs], S.sgl[:, 0:N])
        K.act(lwr, b1[:, 0:N], AF.Sigmoid, bias=cols[:, C_W0 + fc:C_W0 + fc + 1])
        if valid is not None:
            K.tt(POOL, lwr, lwr, valid[:, 0:N], ALU.mult)
        K.scan(cum, G.masks[:, 4, 0:N], lwr)
        K.act(nrm, b4[:, 0:N], AF.Sqrt, bias=G.epsc[:, 2:3])
        K.recip(rn, nrm)
        K.act(alr, b2[:, 0:N], AF.Sigmoid, bias=cols[:, C_A0 + fc:C_A0 + fc + 1])
        K.tt(DVE, kkn, kk, rn, ALU.mult)
        K.tt(POOL, cump, cum, lwr, ALU.subtract)
        K.act(E, cum, AF.Exp, scale=-DEC_C)
        K.act(Eprev, cump, AF.Exp, scale=-DEC_C)
        K.act(Einv, cum, AF.Exp, scale=DEC_C)
        K.tt(POOL, bvec, kkn, alr, ALU.mult)
        K.ts(DVE, tfac, alr, cols[:, C_KA + fc:C_KA + fc + 1], ALU.mult, G.oka[:, fc:fc + 1], ALU.add)
        K.tt(DVE, kmod, zk, tfac, ALU.mult)
        K.stt(S.aT[:, fc, 0:N], kkn, -1.0, Eprev, ALU.mult, ALU.mult)
        K.tt(POOL, S.bT[:, fc, 0:N], bvec, Einv, ALU.mult)
        K.tt(DVE, S.rT[:, fc, 0:N], zr, E, ALU.mult)
        K.tt(DVE, S.kT[:, fc, 0:N], kmod, Einv, ALU.mult)
        K.stt(S.rk[:, fc, 0:N], zr, cols[:, C_RK + fc:C_RK + fc + 1], kmod, ALU.mult, ALU.mult)
        K.cp(POOL, S.gC[:, fc, 0:NCH], E.rr("p (c t) -> p c t", t=64)[:, :, 63])
        K.cp(POOL, S.vT[:, fc, 0:N], zv)
        K.cp(ACT, S.g_sb[:, fc, 0:N], b3[:, 0:N])
    mk = G.masks
    YB = K.banks[7]

    def bankR():
        return K.bank(0, 7)

    def bmm(lhs, rhs, lhs3=None, rhs3=None):
        bk = bankR()
        for fc in range(4):
            for c in range(NC):
                blk = fc * 2 + c
                for hp in range(2):
                    ph = slice(hp * 64, hp * 64 + 64)
                    bc = slice(blk * 64, blk * 64 + 64)
                    if lhs3 is not None:
                        a_ = lhs3[0][ph, fc, lhs3[1] + c * 64:lhs3[1] + c * 64 + 64]
                    else:
                        a_ = lhs[ph, bc]
                    if rhs3 is not None:
                        b_ = rhs3[0][ph, fc, rhs3[1] + c * 64:rhs3[1] + c * 64 + 64]
                    else:
                        b_ = rhs[ph, bc]
                    K.mm(bk[ph, bc], a_, b_)
        return bk

    def part1(tt, B):
        base = tt * 128
        for dst, src in ((B.Vtok, S.vT), (B.Ktok, S.kT), (B.Btok, S.bT)):
            bk = bankR()
            bkb = bk.bc(BF16)
            for fc in range(4):
                for c in range(NC):
                    blk = fc * 2 + c
                    for hp in range(2):
                        ph = slice(hp * 64, hp * 64 + 64)
                        K.tr(bkb[ph, blk * 64:blk * 64 + 64], src[ph, fc, base + c * 64:base + c * 64 + 64],
                             G.identb[ph, ph])
            K.cp(ACT, dst, bkb[:, 0:512])
            yield
        for dst, l3, r3, mi in ((B.M, S.bT, S.aT, 0), (B.Nm, S.aT, S.bT, 1), (B.AKT, S.kT, S.aT, 0),
                                (B.RBT, S.bT, S.rT, 2), (B.RKT, S.kT, S.rT, 2)):
            bk = bmm(None, None, lhs3=(l3, base), rhs3=(r3, base))
            K.tt(DVE, dst, bk, mk[:, mi], ALU.mult)
            yield
        K.tt(DVE, B.Pm, B.M, mk[:, 3], ALU.add)
        K.tt(DVE, B.Qm, B.Nm, mk[:, 3], ALU.add)
        Mc, Nc = B.M, B.Nm
        for lvl in range(5):
            last = lvl == 4
            M2 = B.M2a if lvl % 2 == 0 else B.M
            N2 = B.N2a if lvl % 2 == 0 else B.Nm
            bM2 = bmm(Nc, Mc)
            K.cp(ACT, M2, bM2)
            yield
            if not last:
                bN2 = bmm(Mc, Nc)
                K.cp(ACT, N2, bN2)
                yield
            bPM = bmm(B.Qm, M2)
            K.tt(DVE, B.Pm, B.Pm, bPM, ALU.add)
            yield
            if not last:
                bNQ = bmm(M2, B.Qm)
                K.tt(DVE, B.Qm, B.Qm, bNQ, ALU.add)
                yield
            Mc, Nc = M2, N2

    def part2(tt, B):
        base = tt * 128
        for c in range(NC):
            cg = tt * 2 + c
            cs = slice(base + c * 64, base + c * 64 + 64)
            bR = bankR()
            for fc in range(4):
                bc = slice((fc * 2 + c) * 64, (fc * 2 + c) * 64 + 64)
                for hp in range(2):
                    ph = slice(hp * 64, hp * 64 + 64)
                    K.mm(bR[ph, bc], B.AKT[ph, bc], B.Vtok[ph, bc], start=True, stop=False)
                    K.mm(bR[ph, bc], S.aT[ph, fc, cs], seq.STb[ph, fc * 64:fc * 64 + 64], start=False, stop=True)
            K.cp(DVE, blkview(B.RHSb, c), blkview(bR, c))
            yield
            bU = bankR()
            for fc in range(4):
                bc = slice((fc * 2 + c) * 64, (fc * 2 + c) * 64 + 64)
                for hp in range(2):
                    ph = slice(hp * 64, hp * 64 + 64)
                    K.mm(bU[ph, bc], B.Pm[ph, bc], B.RHSb[ph, bc])
            K.cp(ACT, blkview(B.Ub, c), blkview(bU, c))
            yield
            bS = bankR()
            for fc in range(4):
                bc = slice((fc * 2 + c) * 64, (fc * 2 + c) * 64 + 64)
                for hp in range(2):
                    ph = slice(hp * 64, hp * 64 + 64)
                    if mine:
                        K.mm(YB[ph, bc], S.rT[ph, fc, cs], seq.STb[ph, fc * 64:fc * 64 + 64], start=True, stop=False)
                        K.mm(YB[ph, bc], B.RBT[ph, bc], B.Ub[ph, bc], start=False, stop=False)
                        K.mm(YB[ph, bc], B.RKT[ph, bc], B.Vtok[ph, bc], start=False, stop=True)
                    K.mm(bS[ph, fc * 64:fc * 64 + 64], B.Btok[ph, bc], B.Ub[ph, bc], start=True, stop=False)
                    K.mm(bS[ph, fc * 64:fc * 64 + 64], B.Ktok[ph, bc], B.Vtok[ph, bc], start=False, stop=True)
            K.tt(DVE, seq.ST, seq.ST, bS[:, 0:256], ALU.add)
            ST3 = seq.ST.rr("p (f i) -> p f i", i=64)
            K.tt(DVE, ST3, ST3, S.gC[:, :, cg].us(2).tb([128, 4, 64]), ALU.mult)
            K.cp(ACT, seq.STb, seq.ST)
            yield
        if not mine:
            return
        K.cp(ACT, B.Ysb, YB)
        Y3 = B.Ysb.rr("p (b i) -> p b i", i=64)
        st = B.yst
        K.reduce(st[:, 0], Y3)
        K.tt(POOL, B.Ysq, B.Ysb, B.Ysb, ALU.mult)
        K.reduce(st[:, 1], B.Ysq.rr("p (b i) -> p b i", i=64))
        yield
        K.ts(DVE, st[:, 2], st[:, 0], 1.0 / 64, ALU.mult)
        K.tt(DVE, st[:, 3], st[:, 2], st[:, 2], ALU.mult)
        K.stt(st[:, 4], st[:, 1], 1.0 / 64, st[:, 3], ALU.mult, ALU.subtract)
        K.act(st[:, 5], st[:, 4], AF.Sqrt, bias=G.epsc[:, 1:2])
        K.recip(st[:, 6], st[:, 5])
        K.tt(DVE, Y3, Y3, st[:, 2].us(2).tb([128, 8, 64]), ALU.subtract)
        K.tt(DVE, B.ynb.rr("p (b i) -> p b i", i=64), Y3, st[:, 6].us(2).tb([128, 8, 64]), ALU.mult)
        yield
        bk = bankR()
        bkb = bk.bc(BF16)
        for fc in range(4):
            for c in range(NC):
                bc = slice((fc * 2 + c) * 64, (fc * 2 + c) * 64 + 64)
                for hp in range(2):
                    ph = slice(hp * 64, hp * 64 + 64)
                    K.tr(bkb[ph, bc], B.ynb[ph, bc], G.identb[ph, ph])
        yT = bkb[:, 0:512].rr("p (f t) -> p f t", f=4)
        for fc in range(4):
            K.act(B.t1[:, fc, 0:TPt], yT[:, fc, 0:TPt], AF.Identity, scale=cols[:, C_GNG + fc:C_GNG + fc + 1],
                  bias=cols[:, C_GNB + fc:C_GNB + fc + 1])
        yield
        bB = bankR()
        for fc in range(4):
            K.mm(bB[:, fc * 128:fc * 128 + TPt], G.bonesb, S.rk[:, fc, base:base + TPt])
        K.tt(DVE, B.t2[:, :, 0:TPt], bB.rr("p (f t) -> p f t", f=4)[:, :, 0:TPt], S.vT[:, :, base:base + TPt], ALU.mult)
        K.tt(POOL, B.t1[:, :, 0:TPt], B.t1[:, :, 0:TPt], B.t2[:, :, 0:TPt], ALU.add)
        K.tt(DVE, S.yout[:, :, base:base + TPt], B.t1[:, :, 0:TPt], S.g_sb[:, :, base:base + TPt], ALU.mult)
        yield

    def run_rr(gens):
        active = [g for g in gens if g is not None]
        while active:
            for g in list(active):
                try:
                    next(g)
                except StopIteration:
                    active.remove(g)

    p1 = [part1(tt, S.sets[tt % 2]) for tt in range(NTT)]
    p2 = [part2(tt, S.sets[tt % 2]) for tt in range(NTT)]
    run_rr([p1[0]])
    for tt in range(1, NTT):
        run_rr([p2[tt - 1], p1[tt]])
    run_rr([p2[NTT - 1]])
    if mine and STOP_AT > 7:
        K.store(ym_dram[:, 0:4, ym0:ym0 + nv], S.yout[:, :, 0:nv])


def seq_init_zero(K, seq):
    K.memset(POOL, seq.ST, 0.0)
    K.memset(POOL, seq.STb, 0.0)
    K.memset(POOL, seq.carry, 0.0)


def seq_init_state(K, G, S, seq, wkv_ap, shift_ap):
    K.load(S.wkvio, wkv_ap)
    K.load(seq.carry, shift_ap)
    bk = K.bank()
    for pr in range(2):
        K.mm(bk[:, pr * 128:pr * 128 + 128], S.wkvio[:, pr * 128:pr * 128 + 128], G.ident)
    K.cp(ACT, seq.ST, bk[:, 0:256])
    K.cp(DVE, seq.STb, seq.ST)


def seq_final(K, G, S, seq, wkv_out_ap, shift_out_ap):
    bk = K.bank()
    for pr in range(2):
        K.mm(bk[:, pr * 128:pr * 128 + 128], seq.ST[:, pr * 128:pr * 128 + 128], G.ident)
    K.cp(ACT, S.wkvio, bk[:, 0:256])
    K.store(wkv_out_ap, S.wkvio)
    K.store(shift_out_ap, seq.carry)


DEBUG = False
STOP_AT = 99


def build_program(TP, TM, PAST, stages=("R", "H", "F")):
    nc = bass.Bass("TRN2", target_bir_lowering=False)
    dr = {}

    def inp(name, shape, dt=F32):
        dr[name] = nc.dram_tensor(name, list(shape), dt, kind="ExternalInput").ap()

    def outp(name, shape, dt=F32):
        dr[name] = nc.dram_tensor(name, list(shape), dt, kind="ExternalOutput").ap()

    def scr(name, shape, dt=BF16):
        dr[name] = nc.dram_tensor(name, list(shape), dt, kind="ExternalOutput" if DEBUG else "Internal").ap()

    NTP, NTM = TP // 128, TM // 128
    inp("x_prev", [TP, D]); inp("x_mine", [TM, D]); inp("x_smp", [64, D])
    inp("cache_k", [PAST, 512]); inp("cache_v", [PAST, 512])
    inp("st_shift", [128, 14]); inp("st_wkv", [128, 256]); inp("st_conv", [128, 88])
    inp("w_in", [D, IN_DIM]); inp("w_out", [D, D]); inp("w_up", [D, 2 * DFF]); inp("w_down", [DFF, D])
    inp("w_w2", [64, RW]); inp("a_w2", [64, RW]); inp("g_w2", [128, RW])
    inp("cols", [128, NCOL]); inp("rows", [1, NROW])
    inp("c_ident", [128, 128]); inp("c_masks", [128, 5 * 512]); inp("c_bones", [128, 128]); inp("c_valid", [128, 64])
    inp("cs_prev", [128, NTP * 16]); inp("cs_mine", [128, NTM * 16]); inp("cs_smp", [128, 16])
    outp("y_mine", [TM, D]); outp("y_smp", [16, D])
    outp("nk_mine", [TM, 512]); outp("nv_mine", [TM, 512])
    outp("nshift", [128, 14]); outp("nwkv", [128, 256]); outp("nconv", [88, 128])
    outp("nk_smp", [16, 512]); outp("nv_smp", [16, 512])
    outp("nshift_s", [128, 14]); outp("nwkv_s", [128, 256]); outp("nconv_s", [88, 128])
    scr("hT_p", [128, 8, TP + TM]); scr("hT_s", [128, 8, 64])
    scr("ymT_p", [128, 8, TM]); scr("ymT_s", [128, 8, 64])
    scr("wup_bf", [8, 128, 8 * 768])

    with contextlib.ExitStack() as es:
        K = Ctx(nc, es)
        G = setup_common(K, dr)
        base_top = K.top
        if "R" in stages:
            S = sweepR_alloc(K, G, dr)
            pq = new_seq(K, "p")
            sq = new_seq(K, "s")
            seq_init_zero(K, pq)
            for m in range((TP + TM) // 512 if STOP_AT >= 1 else 0):
                t0 = m * 512
                if t0 < TP:
                    sweepR_mt(K, G, S, dr, pq, dr["x_prev"][t0:t0 + 512, :], 512, dr["hT_p"], t0, False)
                else:
                    sweepR_mt(K, G, S, dr, pq, dr["x_mine"][t0 - TP:t0 - TP + 512, :], 512, dr["hT_p"], t0, True,
                              ym_dram=dr["ymT_p"], ym0=t0 - TP)
            if STOP_AT >= 0.5:
                seq_final(K, G, S, pq, dr["nwkv"], dr["nshift"])
                seq_init_state(K, G, S, sq, dr["st_wkv"], dr["st_shift"])
            if STOP_AT >= 1:
                sweepR_mt(K, G, S, dr, sq, dr["x_smp"], 64, dr["hT_s"], 0, True, ym_dram=dr["ymT_s"], ym0=0, nvalid=16)
            if STOP_AT >= 0.5:
                seq_final(K, G, S, sq, dr["nwkv_s"], dr["nshift_s"])
            K.P.barrier()
            K.recycle()
            K.top = base_top
        wup_done = False
        if "H" in stages:
            for hp2 in range(2):
                import os
                HSTOP = int(os.environ.get("HSTOP", 9))
                S = sweepH_alloc(K, G, dr, hp2, TP, TM, PAST)
                conv = None
                if hp2 == 1 and "F" in stages:
                    cst = Ring(K, "cst", 2, [128, 768])
                    cob = Ring(K, "cob", 2, [128, 768], BF16)
                    conv = wup_convert(K, G, dr, cst, cob)
                    wup_done = True
                nmt = (TP + TM) // 512
                per = -(-64 // nmt)
                for m in range(nmt if HSTOP >= 2 else 0):
                    sweepH_mt(K, G, S, dr, hp2, m, TP, TM, m * 512 >= TP)
                    if conv is not None:
                        for _ in range(per):
                            next(conv, None)
                if conv is not None:
                    for _ in conv:
                        pass
                if HSTOP >= 4:
                    sweepH_sample(K, G, S, dr, hp2, PAST)
                K.P.barrier()
                K.recycle()
                K.top = base_top
        if "F" in stages:
            S = phaseF_alloc(K, G, dr, wup_done)
            ucp = K.sb("ucp", [128, 88])
            ucs = K.sb("ucs", [128, 88])
            K.memset(POOL, ucp, 0.0)
            K.load(ucs, dr["st_conv"])
            for m in range(TM // 512):
                phaseF_mt(K, G, S, dr, ucp, dr["ymT_p"][:, :, m * 512:(m + 1) * 512],
                          dr["x_mine"][m * 512:(m + 1) * 512, :], dr["y_mine"][m * 512:(m + 1) * 512, :], 512)
            conv_out(K, G, S, ucp, dr["nconv"])
            phaseF_mt(K, G, S, dr, ucs, dr["ymT_s"][:, :, 0:16], dr["x_smp"][0:16, :], dr["y_smp"], 16)
            conv_out(K, G, S, ucs, dr["nconv_s"])
        K.P.emit(nc, K.sems)
    return nc


def _consts():
    p = np.arange(128)[:, None]
    c = np.arange(512)[None, :]
    s, y = p % 64, c % 64
    masks = np.stack([(s < y), (s > y), (s <= y), (s == y), np.broadcast_to(y != 0, (128, 512))], axis=1)
    masks = masks.astype(np.float32).reshape(128, 5 * 512)
    q = np.arange(128)[None, :]
    bones = ((p // 64) == (q // 64)).astype(np.float32)
    valid = np.broadcast_to((np.arange(64)[None, :] < 16), (128, 64)).astype(np.float32)
    return masks, bones, valid


def _rope_table(pos):
    inv = (np.float32(ROPE_THETA) ** (-np.arange(0, 16, 2, dtype=np.float32) / np.float32(16))).astype(np.float32)
    ang = (pos.astype(np.float32)[:, None] * inv[None, :]).astype(np.float32)
    t = np.concatenate([np.cos(ang), np.sin(ang)], axis=1).astype(np.float32)
    n = pos.shape[0] // 128
    return np.ascontiguousarray(t.reshape(n, 128, 16).transpose(1, 0, 2).reshape(128, n * 16))


def _colpack(a, nchunk):
    return np.asarray(a, np.float32).reshape(nchunk, 128).T


def prepare_inputs(inp):
    xp = np.asarray(inp["x_prompt"], np.float32)
    xs = np.asarray(inp["x_sample"], np.float32)
    B, T, _ = xp.shape
    TH = T // 2
    TP, TM = TH - 512, TH + 512
    PAST = inp["cache_k"].shape[2]
    masks, bones, valid = _consts()
    ident = np.eye(128, dtype=np.float32)
    f = lambda k: np.asarray(inp[k], np.float32)[0]
    cw = f("conv_w").reshape(3, 44, 128).transpose(2, 0, 1).reshape(128, 132)
    shared_cols = [_colpack(f("norm1_g"), 8), _colpack(f("norm2_g"), 8), _colpack(f("rw_mu"), 14),
                   _colpack(f("rw_w0"), 4), _colpack(f("rw_a0"), 4), _colpack(f("rw_k_k"), 4),
                   _colpack(f("rw_k_a"), 4), _colpack(f("rw_r_k").reshape(-1), 4), _colpack(f("rw_gn_g"), 4),
                   _colpack(f("rw_gn_b"), 4), cw, _colpack(f("conv_b"), 44)]
    negk = np.where(np.arange(128) < 16, 0.0, NEG).astype(np.float32)[:, None]
    rows = np.concatenate([np.tile(f("df_q_g"), 4), np.tile(f("df_k_g"), 4), f("df_subln_g"), f("df_lq1"),
                           f("df_lq2"), f("df_lk1"), f("df_lk2")]).astype(np.float32)[None, :]
    assert rows.shape[1] == NROW
    cs_prev = _rope_table(np.arange(TP))
    cs_smp = _rope_table(np.concatenate([PAST + np.arange(16), np.zeros(112, np.int64)]))
    shared = {
        "w_in": f("w_in"), "w_out": f("w_out"), "w_up": f("w_up"), "w_down": f("w_down"),
        "w_w2": f("rw_w_w2"), "a_w2": f("rw_a_w2"), "g_w2": f("rw_g_w2"), "rows": rows,
        "c_ident": ident, "c_masks": masks, "c_bones": bones, "c_valid": valid,
        "cs_prev": cs_prev, "cs_smp": cs_smp,
    }
    maps = []
    for c in range(8):
        b, g = c // 2, c % 2
        flag = np.full((128, 1), 0.0 if g == 1 else NEG, np.float32)
        cols = np.concatenate(shared_cols + [flag, negk], axis=1).astype(np.float32)
        assert cols.shape[1] == NCOL
        xsm = np.zeros((64, D), np.float32)
        xsm[:16] = xs[c]
        W = np.asarray(inp["state_wkv"], np.float32)[0, c].reshape(2, 2, 2, 64, 64)
        st_wkv = W.transpose(1, 3, 0, 2, 4).reshape(128, 256)
        st_conv = np.asarray(inp["state_ffn_conv"], np.float32)[0, c].reshape(2, 44, 128).transpose(2, 1, 0).reshape(128, 88)
        m = dict(shared)
        m.update({
            "x_prev": np.ascontiguousarray(xp[b, 0:TP]) if g == 1 else np.zeros((TP, D), np.float32),
            "x_mine": np.ascontiguousarray(xp[b, g * TP:g * TP + TM]),
            "x_smp": xsm,
            "cache_k": np.ascontiguousarray(np.asarray(inp["cache_k"], np.float32)[0, c].reshape(PAST, 512)),
            "cache_v": np.ascontiguousarray(np.asarray(inp["cache_v"], np.float32)[0, c].reshape(PAST, 512)),
            "st_shift": np.ascontiguousarray(_colpack(np.asarray(inp["state_shift"], np.float32)[0, c, 0], 14)),
            "st_wkv": np.ascontiguousarray(st_wkv), "st_conv": np.ascontiguousarray(st_conv),
            "cols": np.ascontiguousarray(cols),
            "cs_mine": _rope_table(g * TP + np.arange(TM)),
        })
        maps.append({k: np.ascontiguousarray(v, dtype=np.float32) for k, v in m.items()})
    return maps, (B, T, TH, PAST, TP, TM)


def _unwkv(a):
    return a.reshape(2, 64, 2, 2, 64).transpose(2, 0, 3, 1, 4).reshape(8, 64, 64)


def _unconv(a):
    return a.reshape(44, 2, 128).transpose(1, 0, 2).reshape(2, 2 * DFF)


def assemble(res, dims):
    B, T, TH, PAST, TP, TM = dims
    y_p = np.zeros((B, T, D), np.float32)
    y_s = np.zeros((8, 16, D), np.float32)
    nk_p = np.zeros((1, B, T, 4, 128), np.float32)
    nv_p = np.zeros((1, B, T, 4, 128), np.float32)
    nsh_p = np.zeros((1, B, 1, RWKV_IN), np.float32)
    nwkv_p = np.zeros((1, B, 8, 64, 64), np.float32)
    ncv_p = np.zeros((1, B, 2, 2 * DFF), np.float32)
    nk_s = np.zeros((1, 8, 16, 4, 128), np.float32)
    nv_s = np.zeros((1, 8, 16, 4, 128), np.float32)
    nsh_s = np.zeros((1, 8, 1, RWKV_IN), np.float32)
    nwkv_s = np.zeros((1, 8, 8, 64, 64), np.float32)
    ncv_s = np.zeros((1, 8, 2, 2 * DFF), np.float32)
    for c in range(8):
        r = res[c]
        b, g = c // 2, c % 2
        sl = slice(g * TH, (g + 1) * TH)
        ms = slice(0, TH) if g == 0 else slice(TM - TH, TM)
        y_p[b, sl] = r["y_mine"][ms]
        nk_p[0, b, sl] = r["nk_mine"][ms].reshape(TH, 4, 128)
        nv_p[0, b, sl] = r["nv_mine"][ms].reshape(TH, 4, 128)
        if g == 1:
            nsh_p[0, b, 0] = r["nshift"].T.reshape(-1)
            nwkv_p[0, b] = _unwkv(r["nwkv"])
            ncv_p[0, b] = _unconv(r["nconv"])
        y_s[c] = r["y_smp"]
        nk_s[0, c] = r["nk_smp"].reshape(16, 4, 128)
        nv_s[0, c] = r["nv_smp"].reshape(16, 4, 128)
        nsh_s[0, c, 0] = r["nshift_s"].T.reshape(-1)
        nwkv_s[0, c] = _unwkv(r["nwkv_s"])
        ncv_s[0, c] = _unconv(r["nconv_s"])
    return (y_p, y_s, nk_p, nv_p, nsh_p, nwkv_p, ncv_p, nk_s, nv_s, nsh_s, nwkv_s, ncv_s)


_CACHE = {}


def run(inputs, stages=("R", "H", "F")):
    maps, dims = prepare_inputs(inputs)
    B, T, TH, PAST, TP, TM = dims
    key = (TP, TM, PAST, tuple(stages), DEBUG)
    if key not in _CACHE:
        _CACHE[key] = build_program(TP, TM, PAST, stages)
    nc = _CACHE[key]
    res = run_bass_kernel_spmd(nc, maps, core_ids=list(range(8)))
    return res.results, dims


def kernel(**inputs):
    res, dims = run(inputs)
    return assemble(res, dims)


def sweepH_alloc(K, G, dr, hp2, TP, TM, PAST):
    S = NS()
    NT = (TP + TM) // 128
    NPT = PAST // 128
    S.Wq = K.sb("Wq", [128, 8, 768], BF16)
    mark = K.top
    stage = Ring(K, "wstH", 2, [128, 1792])
    for part in range(3):
        col0 = RWKV_IN + part * 512 + hp2 * 256
        load_weight(K, S.Wq[:, :, part * 256:(part + 1) * 256], dr["w_in"], 1024, 256, stage,
                    scale_cols=G.cols[:, C_G1:C_G1 + 8], col0=col0)
    K.P.barrier()
    K.recycle()
    K.top = mark
    S.KT = K.sb("KT", [128, 2, TP + TM], BF16)
    S.Va = K.sb("Va", [128, NT, 2, 130], BF16)
    S.KTs = K.sb("KTs", [128, 2, PAST + 128], BF16)
    S.Vs = K.sb("Vs", [128, NPT + 1, 2, 130], BF16)
    K.memset(POOL, S.Va[:, :, :, 128:129], 1.0)
    K.memset(POOL, S.Vs[:, :, :, 128:129], 1.0)
    K.memset(POOL, S.KTs[:, :, PAST:PAST + 128], 0.0)
    K.memset(POOL, S.Vs[:, NPT, :, 0:128], 0.0)
    S.KTm = [S.KT[:, :, m * 512:(m + 1) * 512].alias(f"KTm{m}") for m in range((TP + TM) // 512)]
    S.Vam = [S.Va[:, m * 4:(m + 1) * 4].alias(f"Vam{m}") for m in range((TP + TM) // 512)]
    S.cs_prev = K.sb("cs_prev", [128, TP // 128, 16])
    S.cs_mine = K.sb("cs_mine", [128, TM // 128, 16])
    S.cs_smp = K.sb("cs_smp", [128, 1, 16])
    K.load(S.cs_prev, dr["cs_prev"])
    K.load(S.cs_mine, dr["cs_mine"])
    K.load(S.cs_smp, dr["cs_smp"])
    S.hT = Ring(K, "hTH", 2, [128, 8, 512], BF16)
    S.qk = Ring(K, "qk", 4, [128, 512])
    S.sq = Ring(K, "sqH", 4, [128, 512])
    S.st = Ring(K, "stH", 4, [128, 32])
    S.rt = Ring(K, "ropet", 4, [128, 4, 8, 8])
    S.qkb = Ring(K, "qkb", 4, [128, 512], BF16)
    S.vf = Ring(K, "vf", 4, [128, 256])
    S.QT = Ring(K, "QT", 2, [128, 2, 512], BF16)
    S.PT2 = Ring(K, "PT2", 3, [128, 2, 512], BF16)
    S.o = Ring(K, "oH", 2, [128, 128])
    S.est = Ring(K, "est", 2, [128, 8])
    S.ydf = Ring(K, "ydf", 2, [128, 128], BF16)
    S.ydfT = Ring(K, "ydfT", 2, [128, 2, 512], BF16)
    S.ck = Ring(K, "ck", 2, [128, 256])
    S.ckb = Ring(K, "ckb", 2, [128, 256], BF16)
    return S


def qkv_tile(K, G, S, hT, c0, TPt, cs, want_q, QT, qcol, KTdst, kcol, Vdst, nk_ap, nv_ap, nrows, blo=4):
    bA = K.bank(blo, 8)
    for kc in range(8):
        K.mm(bA[0:TPt, :], hT[:, kc, c0:c0 + TPt], S.Wq[:, kc, 0:512], start=(kc == 0), stop=(kc == 7))
    bB = K.bank(blo, 8)
    for kc in range(8):
        K.mm(bB[0:TPt, 0:256], hT[:, kc, c0:c0 + TPt], S.Wq[:, kc, 512:768], start=(kc == 0), stop=(kc == 7))
    qk = S.qk.next()[0:TPt]
    st = S.st.next()[0:TPt]
    K.cp(ACT, qk, bA[0:TPt, :])
    vf = S.vf.next()[0:TPt]
    K.cp(ACT, vf, bB[0:TPt, 0:256])
    yield
    sq = S.sq.next()[0:TPt]
    K.tt(POOL, sq, qk, qk, ALU.mult)
    K.reduce(st[:, 0:8], sq.rr("p (a b) -> p a b", b=64))
    K.act(st[:, 8:16], st[:, 0:8], AF.Sqrt, scale=1.0 / 64, bias=G.epsc[0:TPt, 0:1])
    K.recip(st[:, 16:24], st[:, 8:16])
    yield
    qk3 = qk.rr("p (a b) -> p a b", b=64)
    K.tt(DVE, qk3, qk3, st[:, 16:24].us(2).tb([TPt, 8, 64]), ALU.mult)
    K.tt(POOL, qk, qk, G.rows[0:TPt, R_GQK:R_GQK + 512], ALU.mult)
    yield
    x1, x2 = qk3[:, :, 0:8], qk3[:, :, 8:16]
    cosb = cs[0:TPt, 0:8].us(1).tb([TPt, 8, 8])
    sinb = cs[0:TPt, 8:16].us(1).tb([TPt, 8, 8])
    rt = S.rt.next()[0:TPt]
    K.tt(DVE, rt[:, 0], x1, cosb, ALU.mult)
    K.tt(POOL, rt[:, 1], x2, sinb, ALU.mult)
    K.tt(DVE, rt[:, 2], x2, cosb, ALU.mult)
    K.tt(POOL, rt[:, 3], x1, sinb, ALU.mult)
    K.tt(DVE, x1, rt[:, 0], rt[:, 1], ALU.subtract)
    K.tt(POOL, x2, rt[:, 2], rt[:, 3], ALU.add)
    yield
    if nk_ap is not None:
        K.store(nk_ap, qk[0:nrows, 256:512])
        K.store(nv_ap, vf[0:nrows])
    qkb = S.qkb.next()[0:TPt]
    K.cp(ACT, qkb, qk)
    K.cp(POOL, Vdst[0:TPt, :, 0:128], vf.rr("p (h d) -> p h d", d=128))
    yield
    bT = K.bank(blo, 8)
    bTb = bT.bc(BF16)
    blocks = range(4) if want_q else range(2, 4)
    for blk in blocks:
        K.tr(bTb[:, blk * 128:blk * 128 + TPt], qkb[:, blk * 128:(blk + 1) * 128], G.identb[0:TPt, 0:TPt])
    b3 = bTb[:, 0:512].rr("p (a t) -> p a t", t=128)
    if want_q:
        K.cp(DVE, QT[:, :, qcol:qcol + TPt], b3[:, 0:2, 0:TPt])
    K.cp(DVE, KTdst[:, :, kcol:kcol + TPt], b3[:, 2:4, 0:TPt])
    yield


def run_rr(gens):
    active = [g for g in gens if g is not None]
    while active:
        for g in list(active):
            try:
                next(g)
            except StopIteration:
                active.remove(g)


def attn_epilogue(K, G, S, acc, rows, ydfT_dst):
    est = S.est.next()[0:rows]
    o = S.o.next()[0:rows]
    K.recip(est[:, 0:1], acc[:, 128:129])
    K.recip(est[:, 1:2], acc[:, 384:385])
    K.tt(DVE, est[:, 1:2], est[:, 1:2], G.lam[0:rows, 3:4], ALU.mult)
    K.ts(DVE, o, acc[:, 0:128], est[:, 0:1], ALU.mult)
    K.stt(o, acc[:, 256:384], est[:, 1:2], o, ALU.mult, ALU.add)
    K.act(G.junk.alias("j")[0:rows, 0:128], o, AF.Square, accum=est[:, 2:3])
    K.act(est[:, 3:4], est[:, 2:3], AF.Sqrt, scale=1.0 / 128, bias=G.epsc[0:rows, 0:1])
    K.recip(est[:, 4:5], est[:, 3:4])
    ydf = S.ydf.next()[0:rows]
    K.stt(ydf, o, est[:, 4:5], G.gsub[0:rows], ALU.mult, ALU.mult)
    bT = K.bank(4, 8)
    bTb = bT.bc(BF16)
    K.tr(bTb[:, 0:rows], ydf, G.identb[0:rows, 0:rows])
    K.cp(ACT, ydfT_dst, bTb[:, 0:rows])


def sweepH_mt(K, G, S, dr, hp2, m, TP, TM, mine):
    NTP = TP // 128
    t0 = m * 512
    hT = S.hT.next()
    K.load(hT, dr["hT_p"][:, :, t0:t0 + 512])
    QT = S.QT.next()
    gens = []
    for i in range(4):
        tile = m * 4 + i
        if mine:
            cs = S.cs_mine[:, tile - NTP]
            r0 = t0 - TP + i * 128
            nk_ap = dr["nk_mine"][r0:r0 + 128, hp2 * 256:(hp2 + 1) * 256]
            nv_ap = dr["nv_mine"][r0:r0 + 128, hp2 * 256:(hp2 + 1) * 256]
        else:
            cs = S.cs_prev[:, tile]
            nk_ap = nv_ap = None
        gens.append(qkv_tile(K, G, S, hT, i * 128, 128, cs, mine, QT, i * 128, S.KTm[m], i * 128, S.Vam[m][:, i],
                             nk_ap, nv_ap, 128, blo=(4 if mine else 0)))
    if mine:
        run_rr(gens[0:2])
        run_rr(gens[2:4])
    else:
        run_rr(gens)
    import os
    if not mine or int(os.environ.get("HSTOP", 9)) < 3:
        return
    ml = m - TP // 512
    nk = NTP + (ml + 1) * 4
    ydfT = S.ydfT.next()
    for hl in range(2):
        acc = [K.banks[j] for j in range(4)]
        DEPTH = 1

        def stageA(kt):
            ktl = kt - NTP - ml * 4
            q0 = max(0, ktl) * 128
            kc0 = (kt % 4) * 128
            pb = 4 + 2 * (kt % 2)
            for comp in range(2):
                ph = slice(comp * 64, comp * 64 + 64)
                K.mm(K.banks[pb + comp][:, 0:512 - q0], S.KTm[kt // 4][ph, hl, kc0:kc0 + 128], QT[ph, hl, q0:512])

        pts = {}

        def stageB(kt):
            ktl = kt - NTP - ml * 4
            q0 = max(0, ktl) * 128
            Nq = 512 - q0
            pb = 4 + 2 * (kt % 2)
            PT = S.PT2.next()
            pts[kt] = PT
            K.act(PT[:, :, 0:Nq], K.bankpair(pb)[:, :, 0:Nq], AF.Exp,
                  bias=(G.cols[:, C_FLAG:C_FLAG + 1] if kt < NTP else None))
            if ktl >= 0:
                K.memset(POOL, PT[64:128, :, 0:64], 0.0)

        def stageC(kt):
            ktl = kt - NTP - ml * 4
            q0 = max(0, ktl) * 128
            Vs = S.Vam[kt // 4]
            PT = pts.pop(kt)
            for comp in range(2):
                for j in range(q0 // 128, 4):
                    lastkt = NTP + ml * 4 + j
                    K.mm(acc[j][:, comp * 256:comp * 256 + 129], PT[:, comp, j * 128 - q0:j * 128 - q0 + 128],
                         Vs[:, kt % 4, hl, 0:129], start=(kt == 0 and comp == 0),
                         stop=(kt == lastkt and comp == 1))

        for n in range(nk + 2):
            if n < nk:
                stageA(n)
            if 0 <= n - 1 < nk:
                stageB(n - 1)
            if n - 2 >= 0:
                stageC(n - 2)
        for j in range(4):
            attn_epilogue(K, G, S, acc[j], 128, ydfT[:, hl, j * 128:(j + 1) * 128])
    fo = 4 + hp2 * 2
    K.store(dr["ymT_p"][:, fo:fo + 2, t0 - TP:t0 - TP + 512], ydfT)


def sweepH_sample(K, G, S, dr, hp2, PAST):
    NPT = PAST // 128
    hT = S.hT.next()
    K.load(hT[:, :, 0:64], dr["hT_s"])
    QT = S.QT.next()
    run_rr([qkv_tile(K, G, S, hT, 0, 64, S.cs_smp[:, 0], True, QT, 0, S.KTs, PAST, S.Vs[:, NPT],
                     dr["nk_smp"][:, hp2 * 256:(hp2 + 1) * 256], dr["nv_smp"][:, hp2 * 256:(hp2 + 1) * 256], 16)])
    import os
    SSTOP = int(os.environ.get("SSTOP", 9))
    for ct in range(NPT if SSTOP >= 2 else 0):
        ck = S.ck.next()
        K.load(ck, dr["cache_k"][ct * 128:(ct + 1) * 128, hp2 * 256:(hp2 + 1) * 256])
        ckb = S.ckb.next()
        K.cp(POOL, ckb, ck)
        bT = K.bank(4, 8)
        bTb = bT.bc(BF16)
        for hl in range(2):
            K.tr(bTb[:, hl * 128:(hl + 1) * 128], ckb[:, hl * 128:(hl + 1) * 128], G.identb)
        K.cp(DVE, S.KTs[:, :, ct * 128:(ct + 1) * 128], bTb[:, 0:256].rr("p (a t) -> p a t", t=128))
        cv = S.ck.next()
        K.load(cv, dr["cache_v"][ct * 128:(ct + 1) * 128, hp2 * 256:(hp2 + 1) * 256])
        K.cp(POOL, S.Vs[:, ct, :, 0:128], cv.rr("p (h d) -> p h d", d=128))
    ydfT = S.ydfT.next()
    for hl in range(2):
        acc = K.banks[hl]
        nk = NPT + 1
        pts = {}

        def sA(kt):
            pb = 4 + 2 * (kt % 2)
            for comp in range(2):
                ph = slice(comp * 64, comp * 64 + 64)
                K.mm(K.banks[pb + comp][:, 0:128], S.KTs[ph, hl, kt * 128:kt * 128 + 128], QT[ph, hl, 0:128])

        def sB(kt):
            pb = 4 + 2 * (kt % 2)
            PT = S.PT2.next()
            pts[kt] = PT
            K.act(PT[:, :, 0:128], K.bankpair(pb)[:, :, 0:128], AF.Exp,
                  bias=(G.cols[:, C_NEGK:C_NEGK + 1] if kt == NPT else None))

        def sC(kt):
            PT = pts.pop(kt)
            for comp in range(2):
                K.mm(acc[:, comp * 256:comp * 256 + 129], PT[:, comp, 0:128], S.Vs[:, kt, hl, 0:129],
                     start=(kt == 0 and comp == 0), stop=(kt == NPT and comp == 1))

        for n in range(nk + 2):
            if n < nk:
                sA(n)
            if 0 <= n - 1 < nk:
                sB(n - 1)
            if n - 2 >= 0:
                sC(n - 2)
        if SSTOP >= 4:
            attn_epilogue(K, G, S, acc[0:16], 16, ydfT[:, hl, 0:16])
    fo = 4 + hp2 * 2
    if True:
        K.store(dr["ymT_s"][:, fo:fo + 2, 0:16], ydfT[:, :, 0:16])


FF_PIECES = [(0, 6), (6, 6), (12, 6), (18, 4)]


def wup_convert(K, G, dr, stage, ost):
    i = 0
    for pi, (g0, n) in enumerate(FF_PIECES):
        for isup in range(2):
            c0 = (isup * 22 + g0) * 128
            for kc in range(8):
                st = stage.next()
                K.load(st[:, 0:n * 128], dr["w_up"][kc * 128:(kc + 1) * 128, c0:c0 + n * 128])
                ob = ost.next()
                sc = G.cols[:, C_G2 + kc:C_G2 + kc + 1]
                if i % 2 == 0:
                    K.act(ob[:, 0:n * 128], st[:, 0:n * 128], AF.Copy, scale=sc)
                else:
                    K.ts(DVE, ob[:, 0:n * 128], st[:, 0:n * 128], sc, ALU.mult)
                i += 1
                K.store(dr["wup_bf"][pi * 2 + isup, :, kc * 768:kc * 768 + n * 128], ob[:, 0:n * 128])
                yield


def phaseF_alloc(K, G, dr, wup_done=False):
    S = NS()
    S.Wo = K.sb("Wo", [128, 8, D], BF16)
    S.Wd = K.sb("Wd", [128, 22, D], BF16)
    mark = K.top
    stage = Ring(K, "wstF", 2, [128, 1792])
    load_weight(K, S.Wo, dr["w_out"], 1024, D, stage)
    load_weight(K, S.Wd, dr["w_down"], DFF, D, stage)
    if not wup_done:
        ost = Ring(K, "wupo", 2, [128, 768], BF16)
        for _ in wup_convert(K, G, dr, stage, ost):
            pass
    K.P.barrier()
    K.recycle()
    K.top = mark
    S.wb = Ring(K, "wb", 3, [128, 8, 768], BF16)
    S.ymT = K.sb("ymTF", [128, 8, 512], BF16)
    S.x = Ring(K, "xF", 2, [128, D])
    S.xmid = K.sb("xmid", [128, 4, D])
    S.ss = Ring(K, "ssF", 4, [128, 4])
    S.h2b = K.sb("h2b", [128, D], BF16)
    S.h2T = K.sb("h2T", [128, 8, 512], BF16)
    S.uS = Ring(K, "uS", 3, [128, 520])
    S.ct = Ring(K, "ct", 3, [128, 512])
    S.sg = K.sb("sg", [128, 6, 512], BF16)
    S.mT = K.sb("mT", [128, 22, 512], BF16)
    S.cvo = K.sb("cvo", [128, 128])
    return S


def phaseF_mt(K, G, S, dr, ucarry, ym_ap, x_ap, y_ap, N):
    cols = G.cols
    TPt = min(N, 128)
    NTT = max(1, N // 128)
    K.load(S.ymT[:, :, 0:N], ym_ap)
    xm_tiles = []
    for i in range(NTT):
        xt = S.x.next()
        K.load(xt[0:TPt], x_ap[i * 128:i * 128 + TPt, :])
        xm = S.xmid[0:TPt, i].alias(f"xmid{i}")
        xm_tiles.append(xm)
        for half in range(2):
            hs = slice(half * 512, (half + 1) * 512)
            bk = K.bank()
            for kc in range(8):
                K.mm(bk[0:TPt, :], S.ymT[:, kc, i * 128:i * 128 + TPt], S.Wo[:, kc, hs], start=(kc == 0), stop=(kc == 7))
            K.tt(DVE, xm[:, hs], xt[0:TPt, hs], bk[0:TPt, :], ALU.add)
        ss = S.ss.next()
        K.act(G.junk.alias("j")[0:TPt], xm, AF.Square, accum=ss[0:TPt, 0:1])
        K.act(ss[0:TPt, 1:2], ss[0:TPt, 0:1], AF.Sqrt, scale=1.0 / D, bias=G.epsc[0:TPt, 0:1])
        K.recip(ss[0:TPt, 2:3], ss[0:TPt, 1:2])
        K.act(S.h2b[0:TPt], xm, AF.Copy, scale=ss[0:TPt, 2:3])
        bk = K.bank()
        bkb = bk.bc(BF16)
        for kc in range(8):
            K.tr(bkb[:, kc * 128:kc * 128 + TPt], S.h2b[0:TPt, kc * 128:(kc + 1) * 128], G.identb[0:TPt, 0:TPt])
        K.cp(DVE, S.h2T[:, :, i * 128:i * 128 + TPt], bkb.rr("p (k t) -> p k t", t=128)[:, :, 0:TPt])
    for pi, (g0, n) in enumerate(FF_PIECES):
        for isup in range(2):
            wb = S.wb.next()
            pending = None
            K.load(wb[:, :, 0:n * 128],
                   dr["wup_bf"][pi * 2 + isup].rearrange("p (k c) -> p k c", c=768)[:, :, 0:n * 128], eng=SP)
            for f in range(n):
                fc = isup * 22 + g0 + f
                bk = K.bank()
                for kc in range(8):
                    K.mm(bk[:, 0:N], wb[:, kc, f * 128:(f + 1) * 128], S.h2T[:, kc, 0:N], start=(kc == 0), stop=(kc == 7))
                uS = S.uS.next()
                K.cp(POOL, uS[:, 0:2], ucarry[:, 2 * fc:2 * fc + 2])
                K.cp(ACT, uS[:, 2:N + 2], bk[:, 0:N])
                K.cp(POOL, ucarry[:, 2 * fc:2 * fc + 2], uS[:, N:N + 2])
                ct = S.ct.next()[:, 0:N]
                if (not isup) and f % 2 == 1:
                    K.ts(DVE, ct, uS[:, 2:N + 2], cols[:, C_CW + 88 + fc:C_CW + 88 + fc + 1], ALU.mult,
                         cols[:, C_CB + fc:C_CB + fc + 1], ALU.add)
                else:
                    K.act(ct, uS[:, 2:N + 2], AF.Identity, scale=cols[:, C_CW + 88 + fc:C_CW + 88 + fc + 1],
                          bias=cols[:, C_CB + fc:C_CB + fc + 1])
                K.stt(ct, uS[:, 1:N + 1], cols[:, C_CW + 44 + fc:C_CW + 44 + fc + 1], ct, ALU.mult, ALU.add)
                K.stt(ct, uS[:, 0:N], cols[:, C_CW + fc:C_CW + fc + 1], ct, ALU.mult, ALU.add)
                if pending is not None:
                    pending()

                def fin(ct=ct, f=f, isup=isup, g0=g0):
                    if not isup:
                        K.act(S.sg[:, f, 0:N], ct, AF.Silu)
                    else:
                        K.tt(POOL, S.mT[:, g0 + f, 0:N], ct, S.sg[:, f, 0:N], ALU.mult)
                pending = fin
            if pending is not None:
                pending()
                pending = None
    for i in range(NTT):
        xm = xm_tiles[i]
        for half in range(2):
            hs = slice(half * 512, (half + 1) * 512)
            bk = K.bank()
            for f in range(22):
                K.mm(bk[0:TPt, :], S.mT[:, f, i * 128:i * 128 + TPt], S.Wd[:, f, hs], start=(f == 0), stop=(f == 21))
            K.tt(DVE, xm[:, hs], xm[:, hs], bk[0:TPt, :], ALU.add)
        K.store(y_ap[i * 128:i * 128 + TPt, :], xm)


def conv_out(K, G, S, ucarry, out_ap):
    bk = K.bank()
    K.mm(bk[0:88, 0:128], ucarry, G.ident)
    K.cp(ACT, S.cvo[0:88], bk[0:88, 0:128])
    K.store(out_ap, S.cvo[0:88])
```

```python
import contextlib
import math
import numpy as np
import concourse.bass as bass
import concourse.mybir as mybir
from concourse.bass_utils import run_bass_kernel_spmd

F32 = mybir.dt.float32
BF16 = mybir.dt.bfloat16
AF = mybir.ActivationFunctionType
ALU = mybir.AluOpType
AX = mybir.AxisListType

PE, ACT, DVE, POOL, SP = "tensor", "scalar", "vector", "gpsimd", "sync"
ENGINES = [PE, ACT, DVE, POOL, SP]

D = 1024
RW = 512
RWKV_IN = 1792
IN_DIM = 3328
DFF = 2816
NORM_EPS = 1e-6
GN_EPS = 64e-5
ROPE_THETA = 500000.0
DEC_C = math.exp(-0.5)
LAM_INIT = 0.8 - 0.6 * math.exp(-0.3 * 0)
NEG = -30000.0


class Buf:
    __slots__ = ("name", "last_w", "readers", "psum")

    def __init__(self, name, psum=False):
        self.name = name
        self.last_w = None
        self.readers = []
        self.psum = psum


class V:
    __slots__ = ("ap", "buf")

    def __init__(self, ap, buf):
        self.ap = ap
        self.buf = buf

    def __getitem__(self, k):
        return V(self.ap[k], self.buf)

    def rr(self, s, **kw):
        return V(self.ap.rearrange(s, **kw), self.buf)

    def bc(self, dt):
        return V(self.ap.bitcast(dt), self.buf)

    def us(self, ax):
        return V(self.ap.unsqueeze(ax), self.buf)

    def tb(self, shape):
        return V(self.ap.to_broadcast(list(shape)), self.buf)

    def alias(self, name):
        return V(self.ap, Buf(name))


class Op:
    __slots__ = ("eng", "fn", "reads", "writes", "dma_sem", "deps", "sig", "idx", "acc", "dma_cnt", "bar")

    def __init__(self, eng, fn, reads, writes, dma_sem, acc):
        self.eng = eng
        self.fn = fn
        self.reads = reads
        self.writes = writes
        self.dma_sem = dma_sem
        self.deps = None
        self.sig = None
        self.acc = acc
        self.dma_cnt = None
        self.bar = None


class Prog:
    def __init__(self):
        self.ops = []
        self.group_sems = set()

    def op(self, eng, fn, reads=(), writes=(), dma_sem=None, acc=False):
        def bufs(vs):
            out = []
            for v in vs:
                if v is None:
                    continue
                if isinstance(v.buf, (list, tuple)):
                    out.extend(v.buf)
                else:
                    out.append(v.buf)
            return out
        o = Op(eng, fn, bufs(reads), bufs(writes), dma_sem, acc)
        o.idx = len(self.ops)
        self.ops.append(o)
        return o

    def barrier(self):
        if self.ops:
            self.ops[-1].bar = True

    def schedule(self):
        last_on = {e: None for e in ENGINES}
        last_dma = {}
        pend = {e: [] for e in ENGINES}
        for o in self.ops:
            deps = set(pend[o.eng])
            pend[o.eng] = []
            for b in o.reads:
                if b.last_w is not None:
                    deps.add(b.last_w)
            for b in o.writes:
                if b.last_w is not None:
                    deps.add(b.last_w)
                deps.update(b.readers)
            deps.discard(o)
            o.deps = deps
            for b in o.reads:
                if b.psum and b.readers and b.readers[0].eng != o.eng:
                    raise AssertionError(f"PSUM bank {b.name} read by two engines ({b.readers[0].eng}, {o.eng})")
                b.readers.append(o)
            for b in o.writes:
                b.last_w = o
                b.readers = []
            if o.dma_sem is not None:
                last_dma[id(o.dma_sem)] = o
            else:
                last_on[o.eng] = o
            if o.bar:
                allp = [x for x in last_on.values() if x is not None] + list(last_dma.values())
                for e in ENGINES:
                    pend[e] = list(allp)
        known = {e: {p: -1 for p in ENGINES} for e in ENGINES}
        dma_known = {e: {} for e in ENGINES}
        waits = []
        for o in self.ops:
            need = {}
            for d in o.deps:
                if d.dma_sem is not None:
                    key = ("dma", id(d.dma_sem))
                    if d.idx > dma_known[o.eng].get(key[1], -1):
                        cur = need.get(key)
                        if cur is None or d.idx > cur.idx:
                            need[key] = d
                else:
                    if d.eng == PE and o.eng == PE and d.acc and o.acc:
                        continue
                    if d.idx > known[o.eng][d.eng]:
                        key = ("eng", d.eng)
                        cur = need.get(key)
                        if cur is None or d.idx > cur.idx:
                            need[key] = d
            wl = []
            for (kind, key), d in need.items():
                wl.append(d)
                if kind == "dma":
                    dma_known[o.eng][key] = d.idx
                else:
                    known[o.eng][d.eng] = d.idx
                    d.sig = True
            waits.append(wl)
        self.waits = waits

    def emit(self, nc, sems):
        self.schedule()
        cnt = {e: 0 for e in ENGINES}
        dcnt = {}
        for o in self.ops:
            if o.dma_sem is not None:
                k = id(o.dma_sem)
                dcnt[k] = dcnt.get(k, 0) + 16
                o.dma_cnt = dcnt[k]
            elif o.sig:
                cnt[o.eng] += 1
                o.sig = cnt[o.eng]
        for o in self.ops:
            if o.dma_sem is not None and id(o.dma_sem) in self.group_sems:
                o.dma_cnt = dcnt[id(o.dma_sem)]
        print('SEMCOUNTS', cnt, 'max dma', max(dcnt.values()) if dcnt else 0, 'nops', len(self.ops), flush=True)
        for e in ENGINES:
            assert cnt[e] < 60000, (e, cnt[e])
        for v in dcnt.values():
            assert v < 60000, v
        per_eng = {e: [o for o in self.ops if o.eng == e] for e in ENGINES}
        last_dma = {}
        for o in self.ops:
            if o.dma_sem is not None:
                last_dma[id(o.dma_sem)] = o
        waits = self.waits

        with nc.Block() as block:
            def body(engname):
                def f(eng):
                    for o in per_eng[engname]:
                        for d in waits[o.idx]:
                            if d.dma_sem is not None:
                                eng.wait_ge(d.dma_sem, d.dma_cnt)
                            else:
                                eng.wait_ge(sems[d.eng], d.sig)
                        ins = o.fn(eng)
                        if o.dma_sem is not None:
                            ins.then_inc(o.dma_sem, 16)
                        elif o.sig:
                            ins.then_inc(sems[o.eng], 1)
                    if engname == SP:
                        for o in last_dma.values():
                            eng.wait_ge(o.dma_sem, o.dma_cnt)
                return f
            block.tensor(body(PE))
            block.scalar(body(ACT))
            block.vector(body(DVE))
            block.gpsimd(body(POOL))
            block.sync(body(SP))


class Ctx:
    def __init__(self, nc, es, arena_words=53000):
        self.nc, self.es = nc, es
        self.P = Prog()
        self.sems = {e: es.enter_context(nc.semaphore("s_" + e)) for e in ENGINES}
        self.arena = es.enter_context(nc.sbuf_tensor("arena", [128, arena_words], F32))
        self.words = arena_words
        self.top = 0
        self.banks = []
        self.ps = es.enter_context(nc.psum_tensor("psall", [128, 4096], F32))
        for i in range(8):
            self.banks.append(V(self.ps[:, i * 512:(i + 1) * 512], Buf(f"bank{i}", psum=True)))
        self.bi = 0
        self.nsem = 0
        self.lsem = {}
        self.ssem = {}
        self.gsem = self.newsem()
        self.P.group_sems.add(id(self.gsem))
        self.rot = {ACT: 0}

    def newsem(self):
        if getattr(self, "free_sems", None):
            return self.free_sems.pop()
        self.nsem += 1
        return self.es.enter_context(self.nc.semaphore(f"d{self.nsem}"))

    def recycle(self):
        if not hasattr(self, "free_sems"):
            self.free_sems = []
        self.free_sems.extend(self.lsem.values())
        self.free_sems.extend(self.ssem.values())
        self.lsem = {}
        self.ssem = {}

    def sb(self, name, shape, dt=F32, at=None):
        n = 1
        for s in shape[1:]:
            n *= s
        words = n if dt == F32 else (n + 1) // 2
        words = (words + 7) // 8 * 8
        if at is None:
            assert self.top + words <= self.words, (name, self.top, words)
            off = self.top
            self.top += words
        else:
            off = at
        self.last_alloc = (off, words)
        ap = self.arena[:, off:off + words]
        if dt == BF16:
            ap = ap.bitcast(BF16)
        ap = ap[:, 0:n]
        if len(shape) == 3:
            ap = ap.rearrange("p (a b) -> p a b", b=shape[2])
        elif len(shape) == 4:
            ap = ap.rearrange("p (a b c) -> p a b c", b=shape[2], c=shape[3])
        if shape[0] != 128:
            ap = ap[0:shape[0]]
        return V(ap, Buf(name))

    def bankpair(self, i):
        ap = self.ps[:, i * 512:(i + 2) * 512].rearrange("p (b n) -> p b n", b=2)
        return V(ap, [self.banks[i].buf, self.banks[i + 1].buf])

    def bank(self, lo=0, hi=8):
        n = hi - lo
        b = self.banks[lo + (self.bi % n)]
        self.bi += 1
        return b

    def mm(self, out, lhsT, rhs, start=True, stop=True):
        self.P.op(PE, lambda e: e.matmul(out.ap, lhsT=lhsT.ap, rhs=rhs.ap, start=start, stop=stop),
                  [lhsT, rhs], [out], acc=True)

    def tr(self, out, in_, ident):
        self.P.op(PE, lambda e: e.transpose(out.ap, in_.ap, ident.ap), [in_, ident], [out], acc=True)

    def act(self, out, in_, func, bias=None, scale=None, accum=None):
        kw = {}
        reads = [in_]
        if bias is not None:
            kw["bias"] = bias.ap if isinstance(bias, V) else float(bias)
            if isinstance(bias, V):
                reads.append(bias)
        if scale is not None:
            kw["scale"] = scale.ap if isinstance(scale, V) else float(scale)
            if isinstance(scale, V):
                reads.append(scale)
        writes = [out]
        if accum is not None:
            kw["accum_out"] = accum.ap
            writes.append(accum)
        self.P.op(ACT, lambda e: e.activation(out=out.ap, in_=in_.ap, func=func, **kw), reads, writes)

    def tt(self, eng, out, a, b, op):
        self.P.op(eng, lambda e: e.tensor_tensor(out=out.ap, in0=a.ap, in1=b.ap, op=op), [a, b], [out])

    def ts(self, eng, out, a, s1, op0, s2=None, op1=None):
        reads = [a] + [s for s in (s1, s2) if isinstance(s, V)]
        v1 = s1.ap if isinstance(s1, V) else float(s1)
        v2 = None if s2 is None else (s2.ap if isinstance(s2, V) else float(s2))
        if op1 is None:
            self.P.op(eng, lambda e: e.tensor_scalar(out=out.ap, in0=a.ap, scalar1=v1, scalar2=None, op0=op0),
                      reads, [out])
        else:
            self.P.op(eng, lambda e: e.tensor_scalar(out=out.ap, in0=a.ap, scalar1=v1, scalar2=v2, op0=op0,
                                                      op1=op1), reads, [out])

    def stt(self, out, a, s, b, op0, op1):
        reads = [a, b] + ([s] if isinstance(s, V) else [])
        sv = s.ap if isinstance(s, V) else float(s)
        self.P.op(DVE, lambda e: e.scalar_tensor_tensor(out=out.ap, in0=a.ap, scalar=sv, in1=b.ap, op0=op0,
                                                         op1=op1), reads, [out])

    def cp(self, eng, out, in_):
        if eng == ACT:
            self.act(out, in_, AF.Copy)
        else:
            self.P.op(eng, lambda e: e.tensor_copy(out=out.ap, in_=in_.ap), [in_], [out])

    def memset(self, eng, out, val):
        self.P.op(eng, lambda e: e.memset(out.ap, float(val)), [], [out])

    def fence(self, dummy, vs):
        self.P.op(POOL, lambda e: e.memset(dummy.ap, 0.0), [], [dummy] + list(vs))

    def recip(self, out, in_):
        self.P.op(DVE, lambda e: e.reciprocal(out=out.ap, in_=in_.ap), [in_], [out])

    def reduce(self, out, in_, op=ALU.add):
        self.P.op(DVE, lambda e: e.tensor_reduce(out=out.ap, in_=in_.ap, axis=AX.X, op=op), [in_], [out])

    def scan(self, out, d0, d1):
        self.P.op(DVE, lambda e: e.tensor_tensor_scan(out=out.ap, data0=d0.ap, data1=d1.ap, initial=0.0,
                                                       op0=ALU.mult, op1=ALU.add), [d0, d1], [out])

    def load(self, dst, src_ap, eng=SP, group=False):
        if group:
            sem = self.gsem
        else:
            k = id(dst.buf)
            if k not in self.lsem:
                self.lsem[k] = self.newsem()
            sem = self.lsem[k]
        self.P.op(eng, lambda e: e.dma_start(out=dst.ap, in_=src_ap), [], [dst], dma_sem=sem)

    def store(self, dst_ap, src, eng=POOL):
        k = id(src.buf)
        if k not in self.ssem:
            self.ssem[k] = self.newsem()
        sem = self.ssem[k]
        self.P.op(eng, lambda e: e.dma_start(out=dst_ap, in_=src.ap), [src], [], dma_sem=sem)


class Ring:
    def __init__(self, K, name, n, shape, dt=F32):
        self.slots = [K.sb(f"{name}{i}", shape, dt) for i in range(n)]
        self.i = 0

    def next(self):
        s = self.slots[self.i % len(self.slots)]
        self.i += 1
        return s


C_G1, C_G2, C_MU, C_W0, C_A0, C_KK, C_KA, C_RK, C_GNG, C_GNB, C_CW, C_CB, C_FLAG, C_NEGK = (
    0, 8, 16, 30, 34, 38, 42, 46, 50, 54, 58, 190, 234, 235)
NCOL = 236
R_GQK, R_GSUB, R_LQ, R_LK = 0, 512, 640, 768
NROW = 896


class NS:
    pass


def blkview(x, c):
    return x.rr("p (f c t) -> p f c t", f=4, c=2, t=64)[:, :, c]


def setup_common(K, dr):
    G = NS()
    G.cols = K.sb("cols", [128, NCOL])
    G.rows = K.sb("rows", [128, NROW])
    G.ident = K.sb("ident", [128, 128])
    G.masks = K.sb("masks", [128, 5, 512])
    G.identb = K.sb("identb", [128, 128], BF16)
    G.bonesb = K.sb("bonesb", [128, 128], BF16)
    G.valid = K.sb("valid", [128, 64])
    G.oka = K.sb("oka", [128, 4])
    G.lam = K.sb("lam", [128, 4])
    G.gsub = K.sb("gsub", [128, 128])
    tmpb = K.sb("tmpb", [128, 128])
    K.load(G.cols, dr["cols"], group=True)
    K.load(G.rows, dr["rows"][0:1, :].partition_broadcast(128), group=True)
    K.load(G.ident, dr["c_ident"], group=True)
    K.load(G.masks, dr["c_masks"], group=True)
    K.load(tmpb, dr["c_bones"], group=True)
    K.load(G.valid, dr["c_valid"], group=True)
    K.cp(DVE, G.identb, G.ident)
    K.cp(DVE, G.bonesb, tmpb)
    K.ts(POOL, G.oka, G.cols[:, C_KA:C_KA + 4], -1.0, ALU.mult, 1.0, ALU.add)
    prod = K.sb("lamprod", [128, 128])
    K.tt(DVE, prod, G.rows[:, R_LQ:R_LQ + 128], G.rows[:, R_LK:R_LK + 128], ALU.mult)
    K.reduce(G.lam[:, 0:2], prod.rr("p (a b) -> p a b", b=64))
    K.act(G.lam[:, 0:2], G.lam[:, 0:2], AF.Exp)
    K.tt(DVE, G.lam[:, 2:3], G.lam[:, 0:1], G.lam[:, 1:2], ALU.subtract)
    K.ts(DVE, G.lam[:, 3:4], G.lam[:, 2:3], LAM_INIT, ALU.add, -1.0, ALU.mult)
    K.ts(POOL, G.gsub, G.rows[:, R_GSUB:R_GSUB + 128], 1.0 - LAM_INIT, ALU.mult)
    K.ts(POOL, G.rows[:, R_GQK:R_GQK + 256], G.rows[:, R_GQK:R_GQK + 256], 0.125, ALU.mult)
    G.flag01 = K.sb("flag01", [128, 1])
    K.ts(DVE, G.flag01, G.cols[:, C_FLAG:C_FLAG + 1], 1.0 / 30000.0, ALU.mult, 1.0, ALU.add)
    G.junk = K.sb("junk", [128, 1024], BF16)
    G.epsc = K.sb("epsc", [128, 4])
    K.memset(POOL, G.epsc[:, 0:1], NORM_EPS)
    K.memset(POOL, G.epsc[:, 1:2], GN_EPS)
    K.memset(POOL, G.epsc[:, 2:3], 1e-30)
    K.memset(POOL, G.epsc[:, 3:4], 0.0)
    return G


def load_weight(K, dst, src, rows, ncols, stage, scale_cols=None, col0=0):
    engs = [ACT, DVE, POOL] if scale_cols is None else [ACT, DVE]
    i = 0
    for kc in range(rows // 128):
        for c0 in range(0, ncols, 1792):
            cw = min(1792, ncols - c0)
            st = stage.next()
            K.load(st[:, 0:cw], src[kc * 128:(kc + 1) * 128, col0 + c0:col0 + c0 + cw])
            eng = engs[i % len(engs)]
            i += 1
            d = dst[:, kc, c0:c0 + cw]
            if scale_cols is None:
                K.cp(eng, d, st[:, 0:cw])
            elif eng == ACT:
                K.act(d, st[:, 0:cw], AF.Copy, scale=scale_cols[:, kc:kc + 1])
            else:
                K.ts(eng, d, st[:, 0:cw], scale_cols[:, kc:kc + 1], ALU.mult)


def rmsnorm_to_hT(K, G, S, x_rows, TPt, hT, col0):
    xt = S.x.next()
    K.load(xt[0:TPt], x_rows)
    ss = S.ss.next()
    K.act(G.junk.alias("j")[0:TPt], xt[0:TPt], AF.Square, accum=ss[0:TPt, 0:1])
    K.act(ss[0:TPt, 1:2], ss[0:TPt, 0:1], AF.Sqrt, scale=1.0 / D, bias=G.epsc[0:TPt, 0:1])
    K.recip(ss[0:TPt, 2:3], ss[0:TPt, 1:2])
    hb = S.hb.next()
    K.act(hb[0:TPt], xt[0:TPt], AF.Copy, scale=ss[0:TPt, 2:3])
    bk = K.bank()
    bkb = bk.bc(BF16)
    for kc in range(8):
        K.tr(bkb[:, kc * 128:kc * 128 + TPt], hb[0:TPt, kc * 128:(kc + 1) * 128], G.identb[0:TPt, 0:TPt])
    K.cp(DVE, hT[:, :, col0:col0 + TPt], bkb.rr("p (k t) -> p k t", t=128)[:, :, 0:TPt])
    return xt


def blockmm(K, lhs, rhs, fcs_cols, NC, lhs3=None, rhs3=None):
    bk = K.bank()
    for fc in range(4):
        for c in range(NC):
            blk = fc * 2 + c
            for hp in range(2):
                ph = slice(hp * 64, hp * 64 + 64)
                bc = slice(blk * 64, blk * 64 + 64)
                if lhs3 is not None:
                    cs = slice(lhs3[1] + c * 64, lhs3[1] + c * 64 + 64)
                    a = lhs3[0][ph, fc, cs]
                else:
                    a = lhs[ph, bc]
                if rhs3 is not None:
                    cs = slice(rhs3[1] + c * 64, rhs3[1] + c * 64 + 64)
                    b = rhs3[0][ph, fc, cs]
                else:
                    b = rhs[ph, bc]
                K.mm(bk[ph, bc], a, b)
    return bk


def sweepR_alloc(K, G, dr):
    S = NS()
    S.Wr = K.sb("Wr", [128, 8, RWKV_IN], BF16)
    S.Wl = K.sb("Wl", [128, 2, RW], BF16)
    S.Wg = K.sb("Wg", [128, RW], BF16)
    mark = K.top
    stage = Ring(K, "wst", 2, [128, 1792])
    load_weight(K, S.Wr, dr["w_in"], 1024, RWKV_IN, stage, scale_cols=G.cols[:, C_G1:C_G1 + 8])
    st = stage.next()
    K.load(st[0:64, 0:RW], dr["w_w2"])
    K.load(st[64:128, 0:RW], dr["a_w2"])
    K.cp(DVE, S.Wl[0:64, 0], st[0:64, 0:RW])
    K.cp(DVE, S.Wl[64:128, 1], st[64:128, 0:RW])
    st = stage.next()
    K.load(st[:, 0:RW], dr["g_w2"])
    K.cp(DVE, S.Wg, st[:, 0:RW])
    K.P.barrier()
    K.recycle()
    K.top = mark
    S.x = Ring(K, "x", 2, [128, D])
    S.ss = Ring(K, "ss", 4, [128, 4])
    S.hb = Ring(K, "hb", 2, [128, D], BF16)
    S.hT = Ring(K, "hT", 2, [128, 8, 512], BF16)
    S.pS = Ring(K, "pS", 2, [128, 520])
    S.zT = K.sb("zT", [128, 14, 512])
    S.lora = K.sb("lora", [128, 512], BF16)
    S.sgl = K.sb("sgl", [128, 512], BF16)
    for n in ("aT", "rT", "bT", "kT", "vT", "rk", "g_sb", "yout"):
        setattr(S, n, K.sb(n, [128, 4, 512], BF16))
    for n in ("lwr", "alr", "cum", "E", "Einv", "cump", "kk", "nrm", "tfac", "bvec"):
        setattr(S, n, K.sb(n, [128, 512]))
    S.Eprev, S.rn, S.kkn, S.kmod = S.cump, S.nrm, S.kk, S.tfac
    S.kk2 = K.sb("kk2", [128, 512], BF16)
    S.gC = K.sb("gC", [128, 4, 8])
    TILE_BF = ("Vtok", "Ktok", "Btok", "M", "Nm", "AKT", "RBT", "RKT", "Pm", "Qm", "M2a", "N2a")
    S.sets = []
    for si in range(2):
        B = NS()
        for n in TILE_BF:
            setattr(B, n, K.sb(f"{n}_{si}", [128, 512], BF16))
        S.sets.append(B)
    shared = {}
    for n in ("RHSb", "Ub", "ynb"):
        shared[n] = K.sb(n, [128, 512], BF16)
    for n, shp in (("Ysb", [128, 512]), ("Ysq", [128, 512]), ("yst", [128, 8, 8]), ("t1", [128, 4, 128]),
                   ("t2", [128, 4, 128])):
        shared[n] = K.sb(n, shp)
    for B in S.sets:
        for n, v in shared.items():
            setattr(B, n, v)
    S.wkvio = K.sb("wkvio", [128, 256])
    return S


def new_seq(K, name):
    q = NS()
    q.ST = K.sb(name + "ST", [128, 256])
    q.STb = K.sb(name + "STb", [128, 256], BF16)
    q.carry = K.sb(name + "carry", [128, 14])
    return q


def sweepR_mt(K, G, S, dr, seq, x_ap, N, hT_dram, tok0, mine, ym_dram=None, ym0=0, nvalid=None):
    cols = G.cols
    TPt = min(N, 128)
    NTT = max(1, N // 128)
    NC = 2 if N >= 128 else 1
    NCH = N // 64
    valid = G.valid if nvalid is not None else None
    nv = N if nvalid is None else nvalid
    hT = S.hT.next()
    for i in range(NTT):
        rmsnorm_to_hT(K, G, S, x_ap[i * 128:i * 128 + TPt, :], TPt, hT, i * 128)
    K.store(hT_dram[:, :, tok0:tok0 + N], hT[:, :, 0:N])
    if STOP_AT <= 1:
        return
    for fc in range(14):
        bk = K.bank()
        for kc in range(8):
            K.mm(bk[:, 0:N], S.Wr[:, kc, fc * 128:(fc + 1) * 128], hT[:, kc, 0:N], start=(kc == 0), stop=(kc == 7))
        pS = S.pS.next()
        K.cp(POOL, pS[:, 0:1], seq.carry[:, fc:fc + 1])
        K.cp(ACT, pS[:, 1:N + 1], bk[:, 0:N])
        K.cp(POOL, seq.carry[:, fc:fc + 1], pS[:, nv:nv + 1])
        z = S.zT[:, fc, 0:N]
        K.tt(POOL, z, pS[:, 0:N], pS[:, 1:N + 1], ALU.subtract)
        K.stt(z, z, cols[:, C_MU + fc:C_MU + fc + 1], pS[:, 1:N + 1], ALU.mult, ALU.add)
        if valid is not None:
            K.tt(POOL, z, z, valid[:, 0:N], ALU.mult)
    if STOP_AT <= 2:
        return
    K.act(S.lora[0:64, 0:N], S.zT[0:64, 12, 0:N], AF.Tanh)
    K.cp(POOL, S.lora[64:128, 0:N], S.zT[64:128, 12, 0:N])
    K.act(S.sgl[:, 0:N], S.zT[:, 13, 0:N], AF.Sigmoid)
    for fc in range(4):
        fs = slice(fc * 128, (fc + 1) * 128)
        zr, zk, zv = S.zT[:, fc, 0:N], S.zT[:, 4 + fc, 0:N], S.zT[:, 8 + fc, 0:N]
        lwr, alr, cum, E, Einv, cump, Eprev = (x[:, 0:N] for x in (S.lwr, S.alr, S.cum, S.E, S.Einv, S.cump, S.Eprev))
        kk, nrm, rn, kkn, tfac, kmod, bvec = (x[:, 0:N] for x in (S.kk, S.nrm, S.rn, S.kkn, S.tfac, S.kmod, S.bvec))
        K.act(kk, zk, AF.Copy, scale=cols[:, C_KK + fc:C_KK + fc + 1])
        K.tt(POOL, S.kk2[:, 0:N], kk, kk, ALU.mult)
        b4 = K.bank()
        K.mm(b4[:, 0:N], G.bonesb, S.kk2[:, 0:N])
        b1 = K.bank()
        K.mm(b1[:, 0:N], S.Wl[0:64, 0, fs], S.lora[0:64, 0:N])
        b2 = K.bank()
        K.mm(b2[:, 0:N], S.Wl[64:128, 1, fs], S.lora[64:128, 0:N])
        b3 = K.bank()
        K.mm(b3[:, 0:N], S.Wg[:, fs], S.sgl[:, 0:N])
        K.act(lwr, b1[:, 0:N], AF.Sigmoid, bias=cols[:, C_W0 + fc:C_W0 + fc + 1])
        if valid is not None:
            K.tt(POOL, lwr, lwr, valid[:, 0:N], ALU.mult)
        K.scan(cum, G.masks[:, 4, 0:N], lwr)
        K.act(nrm, b4[:, 0:N], AF.Sqrt, bias=G.epsc[:, 2:3])
        K.recip(rn, nrm)
        K.act(alr, b2[:, 0:N], AF.Sigmoid, bias=cols[:, C_A0 + fc:C_A0 + fc + 1])
        K.tt(DVE, kkn, kk, rn, ALU.mult)
        K.tt(POOL, cump, cum, lwr, ALU.subtract)
        K.act(E, cum, AF.Exp, scale=-DEC_C)
        K.act(Eprev, cump, AF.Exp, scale=-DEC_C)
        K.act(Einv, cum, AF.Exp, scale=DEC_C)
        K.tt(POOL, bvec, kkn, alr, ALU.mult)
        K.ts(DVE, tfac, alr, cols[:, C_KA + fc:C_KA + fc + 1], ALU.mult, G.oka[:, fc:fc + 1], ALU.add)
        K.tt(DVE, kmod, zk, tfac, ALU.mult)
        K.stt(S.aT[:, fc, 0:N], kkn, -1.0, Eprev, ALU.mult, ALU.mult)
        K.tt(POOL, S.bT[:, fc, 0:N], bvec, Einv, ALU.mult)
        K.tt(DVE, S.rT[:, fc, 0:N], zr, E, ALU.mult)
        K.tt(DVE, S.kT[:, fc, 0:N], kmod, Einv, ALU.mult)
        K.stt(S.rk[:, fc, 0:N], zr, cols[:, C_RK + fc:C_RK + fc + 1], kmod, ALU.mult, ALU.mult)
        K.cp(POOL, S.gC[:, fc, 0:NCH], E.rr("p (c t) -> p c t", t=64)[:, :, 63])
        K.cp(POOL, S.vT[:, fc, 0:N], zv)
        K.cp(ACT, S.g_sb[:, fc, 0:N], b3[:, 0:N])
    mk = G.masks
    YB = K.banks[7]

    def bankR():
        return K.bank(0, 7)

    def bmm(lhs, rhs, lhs3=None, rhs3=None):
        bk = bankR()
        for fc in range(4):
            for c in range(NC):
                blk = fc * 2 + c
                for hp in range(2):
                    ph = slice(hp * 64, hp * 64 + 64)
                    bc = slice(blk * 64, blk * 64 + 64)
                    if lhs3 is not None:
                        a_ = lhs3[0][ph, fc, lhs3[1] + c * 64:lhs3[1] + c * 64 + 64]
                    else:
                        a_ = lhs[ph, bc]
                    if rhs3 is not None:
                        b_ = rhs3[0][ph, fc, rhs3[1] + c * 64:rhs3[1] + c * 64 + 64]
                    else:
                        b_ = rhs[ph, bc]
                    K.mm(bk[ph, bc], a_, b_)
        return bk

    def part1(tt, B):
        base = tt * 128
        for dst, src in ((B.Vtok, S.vT), (B.Ktok, S.kT), (B.Btok, S.bT)):
            bk = bankR()
            bkb = bk.bc(BF16)
            for fc in range(4):
                for c in range(NC):
                    blk = fc * 2 + c
                    for hp in range(2):
                        ph = slice(hp * 64, hp * 64 + 64)
                        K.tr(bkb[ph, blk * 64:blk * 64 + 64], src[ph, fc, base + c * 64:base + c * 64 + 64],
                             G.identb[ph, ph])
            K.cp(ACT, dst, bkb[:, 0:512])
            yield
        for dst, l3, r3, mi in ((B.M, S.bT, S.aT, 0), (B.Nm, S.aT, S.bT, 1), (B.AKT, S.kT, S.aT, 0),
                                (B.RBT, S.bT, S.rT, 2), (B.RKT, S.kT, S.rT, 2)):
            bk = bmm(None, None, lhs3=(l3, base), rhs3=(r3, base))
            K.tt(DVE, dst, bk, mk[:, mi], ALU.mult)
            yield
        K.tt(DVE, B.Pm, B.M, mk[:, 3], ALU.add)
        K.tt(DVE, B.Qm, B.Nm, mk[:, 3], ALU.add)
        Mc, Nc = B.M, B.Nm
        for lvl in range(5):
            last = lvl == 4
            M2 = B.M2a if lvl % 2 == 0 else B.M
            N2 = B.N2a if lvl % 2 == 0 else B.Nm
            bM2 = bmm(Nc, Mc)
            K.cp(ACT, M2, bM2)
            yield
            if not last:
                bN2 = bmm(Mc, Nc)
                K.cp(ACT, N2, bN2)
                yield
            bPM = bmm(B.Qm, M2)
            K.tt(DVE, B.Pm, B.Pm, bPM, ALU.add)
            yield
            if not last:
                bNQ = bmm(M2, B.Qm)
                K.tt(DVE, B.Qm, B.Qm, bNQ, ALU.add)
                yield
            Mc, Nc = M2, N2

    def part2(tt, B):
        base = tt * 128
        for c in range(NC):
            cg = tt * 2 + c
            cs = slice(base + c * 64, base + c * 64 + 64)
            bR = bankR()
            for fc in range(4):
                bc = slice((fc * 2 + c) * 64, (fc * 2 + c) * 64 + 64)
                for hp in range(2):
                    ph = slice(hp * 64, hp * 64 + 64)
                    K.mm(bR[ph, bc], B.AKT[ph, bc], B.Vtok[ph, bc], start=True, stop=False)
                    K.mm(bR[ph, bc], S.aT[ph, fc, cs], seq.STb[ph, fc * 64:fc * 64 + 64], start=False, stop=True)
            K.cp(DVE, blkview(B.RHSb, c), blkview(bR, c))
            yield
            bU = bankR()
            for fc in range(4):
                bc = slice((fc * 2 + c) * 64, (fc * 2 + c) * 64 + 64)
                for hp in range(2):
                    ph = slice(hp * 64, hp * 64 + 64)
                    K.mm(bU[ph, bc], B.Pm[ph, bc], B.RHSb[ph, bc])
            K.cp(ACT, blkview(B.Ub, c), blkview(bU, c))
            yield
            bS = bankR()
            for fc in range(4):
                bc = slice((fc * 2 + c) * 64, (fc * 2 + c) * 64 + 64)
                for hp in range(2):
                    ph = slice(hp * 64, hp * 64 + 64)
                    if mine:
                        K.mm(YB[ph, bc], S.rT[ph, fc, cs], seq.STb[ph, fc * 64:fc * 64 + 64], start=True, stop=False)
                        K.mm(YB[ph, bc], B.RBT[ph, bc], B.Ub[ph, bc], start=False, stop=False)
                        K.mm(YB[ph, bc], B.RKT[ph, bc], B.Vtok[ph, bc], start=False, stop=True)
                    K.mm(bS[ph, fc * 64:fc * 64 + 64], B.Btok[ph, bc], B.Ub[ph, bc], start=True, stop=False)
                    K.mm(bS[ph, fc * 64:fc * 64 + 64], B.Ktok[ph, bc], B.Vtok[ph, bc], start=False, stop=True)
            K.tt(DVE, seq.ST, seq.ST, bS[:, 0:256], ALU.add)
            ST3 = seq.ST.rr("p (f i) -> p f i", i=64)
            K.tt(DVE, ST3, ST3, S.gC[:, :, cg].us(2).tb([128, 4, 64]), ALU.mult)
            K.cp(ACT, seq.STb, seq.ST)
            yield
        if not mine:
            return
        K.cp(ACT, B.Ysb, YB)
        Y3 = B.Ysb.rr("p (b i) -> p b i", i=64)
        st = B.yst
        K.reduce(st[:, 0], Y3)
        K.tt(POOL, B.Ysq, B.Ysb, B.Ysb, ALU.mult)
        K.reduce(st[:, 1], B.Ysq.rr("p (b i) -> p b i", i=64))
        yield
        K.ts(DVE, st[:, 2], st[:, 0], 1.0 / 64, ALU.mult)
        K.tt(DVE, st[:, 3], st[:, 2], st[:, 2], ALU.mult)
        K.stt(st[:, 4], st[:, 1], 1.0 / 64, st[:, 3], ALU.mult, ALU.subtract)
        K.act(st[:, 5], st[:, 4], AF.Sqrt, bias=G.epsc[:, 1:2])
        K.recip(st[:, 6], st[:, 5])
        K.tt(DVE, Y3, Y3, st[:, 2].us(2).tb([128, 8, 64]), ALU.subtract)
        K.tt(DVE, B.ynb.rr("p (b i) -> p b i", i=64), Y3, st[:, 6].us(2).tb([128, 8, 64]), ALU.mult)
        yield
        bk = bankR()
        bkb = bk.bc(BF16)
        for fc in range(4):
            for c in range(NC):
                bc = slice((fc * 2 + c) * 64, (fc * 2 + c) * 64 + 64)
                for hp in range(2):
                    ph = slice(hp * 64, hp * 64 + 64)
                    K.tr(bkb[ph, bc], B.ynb[ph, bc], G.identb[ph, ph])
        yT = bkb[:, 0:512].rr("p (f t) -> p f t", f=4)
        for fc in range(4):
            K.act(B.t1[:, fc, 0:TPt], yT[:, fc, 0:TPt], AF.Identity, scale=cols[:, C_GNG + fc:C_GNG + fc + 1],
                  bias=cols[:, C_GNB + fc:C_GNB + fc + 1])
        yield
        bB = bankR()
        for fc in range(4):
            K.mm(bB[:, fc * 128:fc * 128 + TPt], G.bonesb, S.rk[:, fc, base:base + TPt])
        K.tt(DVE, B.t2[:, :, 0:TPt], bB.rr("p (f t) -> p f t", f=4)[:, :, 0:TPt], S.vT[:, :, base:base + TPt], ALU.mult)
        K.tt(POOL, B.t1[:, :, 0:TPt], B.t1[:, :, 0:TPt], B.t2[:, :, 0:TPt], ALU.add)
        K.tt(DVE, S.yout[:, :, base:base + TPt], B.t1[:, :, 0:TPt], S.g_sb[:, :, base:base + TPt], ALU.mult)
        yield

    def run_rr(gens):
        active = [g for g in gens if g is not None]
        while active:
            for g in list(active):
                try:
                    next(g)
                except StopIteration:
                    active.remove(g)

    p1 = [part1(tt, S.sets[tt % 2]) for tt in range(NTT)]
    p2 = [part2(tt, S.sets[tt % 2]) for tt in range(NTT)]
    run_rr([p1[0]])
    for tt in range(1, NTT):
        run_rr([p2[tt - 1], p1[tt]])
    run_rr([p2[NTT - 1]])
    if mine and STOP_AT > 7:
        K.store(ym_dram[:, 0:4, ym0:ym0 + nv], S.yout[:, :, 0:nv])


def seq_init_zero(K, seq):
    K.memset(POOL, seq.ST, 0.0)
    K.memset(POOL, seq.STb, 0.0)
    K.memset(POOL, seq.carry, 0.0)


def seq_init_state(K, G, S, seq, wkv_ap, shift_ap):
    K.load(S.wkvio, wkv_ap)
    K.load(seq.carry, shift_ap)
    bk = K.bank()
    for pr in range(2):
        K.mm(bk[:, pr * 128:pr * 128 + 128], S.wkvio[:, pr * 128:pr * 128 + 128], G.ident)
    K.cp(ACT, seq.ST, bk[:, 0:256])
    K.cp(DVE, seq.STb, seq.ST)


def seq_final(K, G, S, seq, wkv_out_ap, shift_out_ap):
    bk = K.bank()
    for pr in range(2):
        K.mm(bk[:, pr * 128:pr * 128 + 128], seq.ST[:, pr * 128:pr * 128 + 128], G.ident)
    K.cp(ACT, S.wkvio, bk[:, 0:256])
    K.store(wkv_out_ap, S.wkvio)
    K.store(shift_out_ap, seq.carry)


DEBUG = False
STOP_AT = 99


def build_program(TP, TM, PAST, stages=("R", "H", "F")):
    nc = bass.Bass("TRN2", target_bir_lowering=False)
    dr = {}

    def inp(name, shape, dt=F32):
        dr[name] = nc.dram_tensor(name, list(shape), dt, kind="ExternalInput").ap()

    def outp(name, shape, dt=F32):
        dr[name] = nc.dram_tensor(name, list(shape), dt, kind="ExternalOutput").ap()

    def scr(name, shape, dt=BF16):
        dr[name] = nc.dram_tensor(name, list(shape), dt, kind="ExternalOutput" if DEBUG else "Internal").ap()

    NTP, NTM = TP // 128, TM // 128
    inp("x_prev", [TP, D]); inp("x_mine", [TM, D]); inp("x_smp", [64, D])
    inp("cache_k", [PAST, 512]); inp("cache_v", [PAST, 512])
    inp("st_shift", [128, 14]); inp("st_wkv", [128, 256]); inp("st_conv", [128, 88])
    inp("w_in", [D, IN_DIM]); inp("w_out", [D, D]); inp("w_up", [D, 2 * DFF]); inp("w_down", [DFF, D])
    inp("w_w2", [64, RW]); inp("a_w2", [64, RW]); inp("g_w2", [128, RW])
    inp("cols", [128, NCOL]); inp("rows", [1, NROW])
    inp("c_ident", [128, 128]); inp("c_masks", [128, 5 * 512]); inp("c_bones", [128, 128]); inp("c_valid", [128, 64])
    inp("cs_prev", [128, NTP * 16]); inp("cs_mine", [128, NTM * 16]); inp("cs_smp", [128, 16])
    outp("y_mine", [TM, D]); outp("y_smp", [16, D])
    outp("nk_mine", [TM, 512]); outp("nv_mine", [TM, 512])
    outp("nshift", [128, 14]); outp("nwkv", [128, 256]); outp("nconv", [88, 128])
    outp("nk_smp", [16, 512]); outp("nv_smp", [16, 512])
    outp("nshift_s", [128, 14]); outp("nwkv_s", [128, 256]); outp("nconv_s", [88, 128])
    scr("hT_p", [128, 8, TP + TM]); scr("hT_s", [128, 8, 64])
    scr("ymT_p", [128, 8, TM]); scr("ymT_s", [128, 8, 64])
    scr("wup_bf", [8, 128, 8 * 768])

    with contextlib.ExitStack() as es:
        K = Ctx(nc, es)
        G = setup_common(K, dr)
        base_top = K.top
        if "R" in stages:
            S = sweepR_alloc(K, G, dr)
            pq = new_seq(K, "p")
            sq = new_seq(K, "s")
            seq_init_zero(K, pq)
            for m in range((TP + TM) // 512 if STOP_AT >= 1 else 0):
                t0 = m * 512
                if t0 < TP:
                    sweepR_mt(K, G, S, dr, pq, dr["x_prev"][t0:t0 + 512, :], 512, dr["hT_p"], t0, False)
                else:
                    sweepR_mt(K, G, S, dr, pq, dr["x_mine"][t0 - TP:t0 - TP + 512, :], 512, dr["hT_p"], t0, True,
                              ym_dram=dr["ymT_p"], ym0=t0 - TP)
            if STOP_AT >= 0.5:
                seq_final(K, G, S, pq, dr["nwkv"], dr["nshift"])
                seq_init_state(K, G, S, sq, dr["st_wkv"], dr["st_shift"])
            if STOP_AT >= 1:
                sweepR_mt(K, G, S, dr, sq, dr["x_smp"], 64, dr["hT_s"], 0, True, ym_dram=dr["ymT_s"], ym0=0, nvalid=16)
            if STOP_AT >= 0.5:
                seq_final(K, G, S, sq, dr["nwkv_s"], dr["nshift_s"])
            K.P.barrier()
            K.recycle()
            K.top = base_top
        wup_done = False
        if "H" in stages:
            for hp2 in range(2):
                import os
                HSTOP = int(os.environ.get("HSTOP", 9))
                S = sweepH_alloc(K, G, dr, hp2, TP, TM, PAST)
                conv = None
                if hp2 == 1 and "F" in stages:
                    cst = Ring(K, "cst", 2, [128, 768])
                    cob = Ring(K, "cob", 2, [128, 768], BF16)
                    conv = wup_convert(K, G, dr, cst, cob)
                    wup_done = True
                nmt = (TP + TM) // 512
                per = -(-64 // nmt)
                for m in range(nmt if HSTOP >= 2 else 0):
                    sweepH_mt(K, G, S, dr, hp2, m, TP, TM, m * 512 >= TP)
                    if conv is not None:
                        for _ in range(per):
                            next(conv, None)
                if conv is not None:
                    for _ in conv:
                        pass
                if HSTOP >= 4:
                    sweepH_sample(K, G, S, dr, hp2, PAST)
                K.P.barrier()
                K.recycle()
                K.top = base_top
        if "F" in stages:
            S = phaseF_alloc(K, G, dr, wup_done)
            ucp = K.sb("ucp", [128, 88])
            ucs = K.sb("ucs", [128, 88])
            K.memset(POOL, ucp, 0.0)
            K.load(ucs, dr["st_conv"])
            for m in range(TM // 512):
                phaseF_mt(K, G, S, dr, ucp, dr["ymT_p"][:, :, m * 512:(m + 1) * 512],
                          dr["x_mine"][m * 512:(m + 1) * 512, :], dr["y_mine"][m * 512:(m + 1) * 512, :], 512)
            conv_out(K, G, S, ucp, dr["nconv"])
            phaseF_mt(K, G, S, dr, ucs, dr["ymT_s"][:, :, 0:16], dr["x_smp"][0:16, :], dr["y_smp"], 16)
            conv_out(K, G, S, ucs, dr["nconv_s"])
        K.P.emit(nc, K.sems)
    return nc


def _consts():
    p = np.arange(128)[:, None]
    c = np.arange(512)[None, :]
    s, y = p % 64, c % 64
    masks = np.stack([(s < y), (s > y), (s <= y), (s == y), np.broadcast_to(y != 0, (128, 512))], axis=1)
    masks = masks.astype(np.float32).reshape(128, 5 * 512)
    q = np.arange(128)[None, :]
    bones = ((p // 64) == (q // 64)).astype(np.float32)
    valid = np.broadcast_to((np.arange(64)[None, :] < 16), (128, 64)).astype(np.float32)
    return masks, bones, valid


def _rope_table(pos):
    inv = (np.float32(ROPE_THETA) ** (-np.arange(0, 16, 2, dtype=np.float32) / np.float32(16))).astype(np.float32)
    ang = (pos.astype(np.float32)[:, None] * inv[None, :]).astype(np.float32)
    t = np.concatenate([np.cos(ang), np.sin(ang)], axis=1).astype(np.float32)
    n = pos.shape[0] // 128
    return np.ascontiguousarray(t.reshape(n, 128, 16).transpose(1, 0, 2).reshape(128, n * 16))


def _colpack(a, nchunk):
    return np.asarray(a, np.float32).reshape(nchunk, 128).T


def prepare_inputs(inp):
    xp = np.asarray(inp["x_prompt"], np.float32)
    xs = np.asarray(inp["x_sample"], np.float32)
    B, T, _ = xp.shape
    TH = T // 2
    TP, TM = TH - 512, TH + 512
    PAST = inp["cache_k"].shape[2]
    masks, bones, valid = _consts()
    ident = np.eye(128, dtype=np.float32)
    f = lambda k: np.asarray(inp[k], np.float32)[0]
    cw = f("conv_w").reshape(3, 44, 128).transpose(2, 0, 1).reshape(128, 132)
    shared_cols = [_colpack(f("norm1_g"), 8), _colpack(f("norm2_g"), 8), _colpack(f("rw_mu"), 14),
                   _colpack(f("rw_w0"), 4), _colpack(f("rw_a0"), 4), _colpack(f("rw_k_k"), 4),
                   _colpack(f("rw_k_a"), 4), _colpack(f("rw_r_k").reshape(-1), 4), _colpack(f("rw_gn_g"), 4),
                   _colpack(f("rw_gn_b"), 4), cw, _colpack(f("conv_b"), 44)]
    negk = np.where(np.arange(128) < 16, 0.0, NEG).astype(np.float32)[:, None]
    rows = np.concatenate([np.tile(f("df_q_g"), 4), np.tile(f("df_k_g"), 4), f("df_subln_g"), f("df_lq1"),
                           f("df_lq2"), f("df_lk1"), f("df_lk2")]).astype(np.float32)[None, :]
    assert rows.shape[1] == NROW
    cs_prev = _rope_table(np.arange(TP))
    cs_smp = _rope_table(np.concatenate([PAST + np.arange(16), np.zeros(112, np.int64)]))
    shared = {
        "w_in": f("w_in"), "w_out": f("w_out"), "w_up": f("w_up"), "w_down": f("w_down"),
        "w_w2": f("rw_w_w2"), "a_w2": f("rw_a_w2"), "g_w2": f("rw_g_w2"), "rows": rows,
        "c_ident": ident, "c_masks": masks, "c_bones": bones, "c_valid": valid,
        "cs_prev": cs_prev, "cs_smp": cs_smp,
    }
    maps = []
    for c in range(8):
        b, g = c // 2, c % 2
        flag = np.full((128, 1), 0.0 if g == 1 else NEG, np.float32)
        cols = np.concatenate(shared_cols + [flag, negk], axis=1).astype(np.float32)
        assert cols.shape[1] == NCOL
        xsm = np.zeros((64, D), np.float32)
        xsm[:16] = xs[c]
        W = np.asarray(inp["state_wkv"], np.float32)[0, c].reshape(2, 2, 2, 64, 64)
        st_wkv = W.transpose(1, 3, 0, 2, 4).reshape(128, 256)
        st_conv = np.asarray(inp["state_ffn_conv"], np.float32)[0, c].reshape(2, 44, 128).transpose(2, 1, 0).reshape(128, 88)
        m = dict(shared)
        m.update({
            "x_prev": np.ascontiguousarray(xp[b, 0:TP]) if g == 1 else np.zeros((TP, D), np.float32),
            "x_mine": np.ascontiguousarray(xp[b, g * TP:g * TP + TM]),
            "x_smp": xsm,
            "cache_k": np.ascontiguousarray(np.asarray(inp["cache_k"], np.float32)[0, c].reshape(PAST, 512)),
            "cache_v": np.ascontiguousarray(np.asarray(inp["cache_v"], np.float32)[0, c].reshape(PAST, 512)),
            "st_shift": np.ascontiguousarray(_colpack(np.asarray(inp["state_shift"], np.float32)[0, c, 0], 14)),
            "st_wkv": np.ascontiguousarray(st_wkv), "st_conv": np.ascontiguousarray(st_conv),
            "cols": np.ascontiguousarray(cols),
            "cs_mine": _rope_table(g * TP + np.arange(TM)),
        })
        maps.append({k: np.ascontiguousarray(v, dtype=np.float32) for k, v in m.items()})
    return maps, (B, T, TH, PAST, TP, TM)


def _unwkv(a):
    return a.reshape(2, 64, 2, 2, 64).transpose(2, 0, 3, 1, 4).reshape(8, 64, 64)


def _unconv(a):
    return a.reshape(44, 2, 128).transpose(1, 0, 2).reshape(2, 2 * DFF)


def assemble(res, dims):
    B, T, TH, PAST, TP, TM = dims
    y_p = np.zeros((B, T, D), np.float32)
    y_s = np.zeros((8, 16, D), np.float32)
    nk_p = np.zeros((1, B, T, 4, 128), np.float32)
    nv_p = np.zeros((1, B, T, 4, 128), np.float32)
    nsh_p = np.zeros((1, B, 1, RWKV_IN), np.float32)
    nwkv_p = np.zeros((1, B, 8, 64, 64), np.float32)
    ncv_p = np.zeros((1, B, 2, 2 * DFF), np.float32)
    nk_s = np.zeros((1, 8, 16, 4, 128), np.float32)
    nv_s = np.zeros((1, 8, 16, 4, 128), np.float32)
    nsh_s = np.zeros((1, 8, 1, RWKV_IN), np.float32)
    nwkv_s = np.zeros((1, 8, 8, 64, 64), np.float32)
    ncv_s = np.zeros((1, 8, 2, 2 * DFF), np.float32)
    for c in range(8):
        r = res[c]
        b, g = c // 2, c % 2
        sl = slice(g * TH, (g + 1) * TH)
        ms = slice(0, TH) if g == 0 else slice(TM - TH, TM)
        y_p[b, sl] = r["y_mine"][ms]
        nk_p[0, b, sl] = r["nk_mine"][ms].reshape(TH, 4, 128)
        nv_p[0, b, sl] = r["nv_mine"][ms].reshape(TH, 4, 128)
        if g == 1:
            nsh_p[0, b, 0] = r["nshift"].T.reshape(-1)
            nwkv_p[0, b] = _unwkv(r["nwkv"])
            ncv_p[0, b] = _unconv(r["nconv"])
        y_s[c] = r["y_smp"]
        nk_s[0, c] = r["nk_smp"].reshape(16, 4, 128)
        nv_s[0, c] = r["nv_smp"].reshape(16, 4, 128)
        nsh_s[0, c, 0] = r["nshift_s"].T.reshape(-1)
        nwkv_s[0, c] = _unwkv(r["nwkv_s"])
        ncv_s[0, c] = _unconv(r["nconv_s"])
    return (y_p, y_s, nk_p, nv_p, nsh_p, nwkv_p, ncv_p, nk_s, nv_s, nsh_s, nwkv_s, ncv_s)


_CACHE = {}


def run(inputs, stages=("R", "H", "F")):
    maps, dims = prepare_inputs(inputs)
    B, T, TH, PAST, TP, TM = dims
    key = (TP, TM, PAST, tuple(stages), DEBUG)
    if key not in _CACHE:
        _CACHE[key] = build_program(TP, TM, PAST, stages)
    nc = _CACHE[key]
    res = run_bass_kernel_spmd(nc, maps, core_ids=list(range(8)))
    return res.results, dims


def kernel(**inputs):
    res, dims = run(inputs)
    return assemble(res, dims)


def sweepH_alloc(K, G, dr, hp2, TP, TM, PAST):
    S = NS()
    NT = (TP + TM) // 128
    NPT = PAST // 128
    S.Wq = K.sb("Wq", [128, 8, 768], BF16)
    mark = K.top
    stage = Ring(K, "wstH", 2, [128, 1792])
    for part in range(3):
        col0 = RWKV_IN + part * 512 + hp2 * 256
        load_weight(K, S.Wq[:, :, part * 256:(part + 1) * 256], dr["w_in"], 1024, 256, stage,
                    scale_cols=G.cols[:, C_G1:C_G1 + 8], col0=col0)
    K.P.barrier()
    K.recycle()
    K.top = mark
    S.KT = K.sb("KT", [128, 2, TP + TM], BF16)
    S.Va = K.sb("Va", [128, NT, 2, 130], BF16)
    S.KTs = K.sb("KTs", [128, 2, PAST + 128], BF16)
    S.Vs = K.sb("Vs", [128, NPT + 1, 2, 130], BF16)
    K.memset(POOL, S.Va[:, :, :, 128:129], 1.0)
    K.memset(POOL, S.Vs[:, :, :, 128:129], 1.0)
    if TP > 0:
        onesp = S.Va[:, 0:TP // 128].rr("p t h c -> p (t h) c")[:, :, 128:129]
        K.ts(DVE, onesp, onesp, G.flag01, ALU.mult)
    K.memset(POOL, S.KTs[:, :, PAST:PAST + 128], 0.0)
    K.memset(POOL, S.Vs[:, NPT, :, 0:128], 0.0)
    S.KTm = [S.KT[:, :, m * 512:(m + 1) * 512].alias(f"KTm{m}") for m in range((TP + TM) // 512)]
    S.Vam = [S.Va[:, m * 4:(m + 1) * 4].alias(f"Vam{m}") for m in range((TP + TM) // 512)]
    S.cs_prev = K.sb("cs_prev", [128, TP // 128, 16])
    S.cs_mine = K.sb("cs_mine", [128, TM // 128, 16])
    S.cs_smp = K.sb("cs_smp", [128, 1, 16])
    K.load(S.cs_prev, dr["cs_prev"])
    K.load(S.cs_mine, dr["cs_mine"])
    K.load(S.cs_smp, dr["cs_smp"])
    S.hT = Ring(K, "hTH", 2, [128, 8, 512], BF16)
    S.qk = Ring(K, "qk", 4, [128, 512])
    S.sq = Ring(K, "sqH", 4, [128, 512])
    S.st = Ring(K, "stH", 4, [128, 32])
    S.rt = Ring(K, "ropet", 4, [128, 4, 8, 8])
    S.qkb = Ring(K, "qkb", 4, [128, 512], BF16)
    S.vf = Ring(K, "vf", 4, [128, 256])
    S.QT = Ring(K, "QT", 2, [128, 2, 512], BF16)
    S.PT2 = Ring(K, "PT2", 3, [128, 2, 512], BF16)
    S.o = Ring(K, "oH", 2, [128, 128])
    S.est = Ring(K, "est", 2, [128, 8])
    S.ydf = Ring(K, "ydf", 2, [128, 128], BF16)
    S.ydfT = Ring(K, "ydfT", 2, [128, 2, 512], BF16)
    S.ck = Ring(K, "ck", 2, [128, 256])
    S.ckb = Ring(K, "ckb", 2, [128, 256], BF16)
    return S


def qkv_tile(K, G, S, hT, c0, TPt, cs, want_q, QT, qcol, KTdst, kcol, Vdst, nk_ap, nv_ap, nrows, blo=4, vscale=None):
    bA = K.bank(blo, 8)
    for kc in range(8):
        K.mm(bA[0:TPt, :], hT[:, kc, c0:c0 + TPt], S.Wq[:, kc, 0:512], start=(kc == 0), stop=(kc == 7))
    bB = K.bank(blo, 8)
    for kc in range(8):
        K.mm(bB[0:TPt, 0:256], hT[:, kc, c0:c0 + TPt], S.Wq[:, kc, 512:768], start=(kc == 0), stop=(kc == 7))
    qk = S.qk.next()[0:TPt]
    st = S.st.next()[0:TPt]
    K.cp(ACT, qk, bA[0:TPt, :])
    vf = S.vf.next()[0:TPt]
    K.cp(ACT, vf, bB[0:TPt, 0:256])
    yield
    sq = S.sq.next()[0:TPt]
    K.tt(POOL, sq, qk, qk, ALU.mult)
    K.reduce(st[:, 0:8], sq.rr("p (a b) -> p a b", b=64))
    K.act(st[:, 8:16], st[:, 0:8], AF.Sqrt, scale=1.0 / 64, bias=G.epsc[0:TPt, 0:1])
    K.recip(st[:, 16:24], st[:, 8:16])
    yield
    qk3 = qk.rr("p (a b) -> p a b", b=64)
    K.tt(DVE, qk3, qk3, st[:, 16:24].us(2).tb([TPt, 8, 64]), ALU.mult)
    K.tt(POOL, qk, qk, G.rows[0:TPt, R_GQK:R_GQK + 512], ALU.mult)
    yield
    x1, x2 = qk3[:, :, 0:8], qk3[:, :, 8:16]
    cosb = cs[0:TPt, 0:8].us(1).tb([TPt, 8, 8])
    sinb = cs[0:TPt, 8:16].us(1).tb([TPt, 8, 8])
    rt = S.rt.next()[0:TPt]
    K.tt(DVE, rt[:, 0], x1, cosb, ALU.mult)
    K.tt(POOL, rt[:, 1], x2, sinb, ALU.mult)
    K.tt(DVE, rt[:, 2], x2, cosb, ALU.mult)
    K.tt(POOL, rt[:, 3], x1, sinb, ALU.mult)
    K.tt(DVE, x1, rt[:, 0], rt[:, 1], ALU.subtract)
    K.tt(POOL, x2, rt[:, 2], rt[:, 3], ALU.add)
    yield
    if nk_ap is not None:
        K.store(nk_ap, qk[0:nrows, 256:512])
        K.store(nv_ap, vf[0:nrows])
    qkb = S.qkb.next()[0:TPt]
    K.cp(ACT, qkb, qk)
    if vscale is None:
        K.cp(POOL, Vdst[0:TPt, :, 0:128], vf.rr("p (h d) -> p h d", d=128))
    else:
        K.ts(DVE, Vdst[0:TPt, :, 0:128], vf.rr("p (h d) -> p h d", d=128), vscale[0:TPt], ALU.mult)
    yield
    bT = K.bank(blo, 8)
    bTb = bT.bc(BF16)
    blocks = range(4) if want_q else range(2, 4)
    for blk in blocks:
        K.tr(bTb[:, blk * 128:blk * 128 + TPt], qkb[:, blk * 128:(blk + 1) * 128], G.identb[0:TPt, 0:TPt])
    b3 = bTb[:, 0:512].rr("p (a t) -> p a t", t=128)
    if want_q:
        K.cp(DVE, QT[:, :, qcol:qcol + TPt], b3[:, 0:2, 0:TPt])
    K.cp(DVE, KTdst[:, :, kcol:kcol + TPt], b3[:, 2:4, 0:TPt])
    yield


def run_rr(gens):
    active = [g for g in gens if g is not None]
    while active:
        for g in list(active):
            try:
                next(g)
            except StopIteration:
                active.remove(g)


def attn_epilogue(K, G, S, acc, rows, ydfT_dst):
    est = S.est.next()[0:rows]
    o = S.o.next()[0:rows]
    K.recip(est[:, 0:1], acc[:, 128:129])
    K.recip(est[:, 1:2], acc[:, 384:385])
    K.tt(DVE, est[:, 1:2], est[:, 1:2], G.lam[0:rows, 3:4], ALU.mult)
    K.ts(DVE, o, acc[:, 0:128], est[:, 0:1], ALU.mult)
    K.stt(o, acc[:, 256:384], est[:, 1:2], o, ALU.mult, ALU.add)
    K.act(G.junk.alias("j")[0:rows, 0:128], o, AF.Square, accum=est[:, 2:3])
    K.act(est[:, 3:4], est[:, 2:3], AF.Sqrt, scale=1.0 / 128, bias=G.epsc[0:rows, 0:1])
    K.recip(est[:, 4:5], est[:, 3:4])
    ydf = S.ydf.next()[0:rows]
    K.stt(ydf, o, est[:, 4:5], G.gsub[0:rows], ALU.mult, ALU.mult)
    bT = K.bank(4, 8)
    bTb = bT.bc(BF16)
    K.tr(bTb[:, 0:rows], ydf, G.identb[0:rows, 0:rows])
    K.cp(ACT, ydfT_dst, bTb[:, 0:rows])


def sweepH_mt(K, G, S, dr, hp2, m, TP, TM, mine):
    NTP = TP // 128
    t0 = m * 512
    hT = S.hT.next()
    K.load(hT, dr["hT_p"][:, :, t0:t0 + 512])
    QT = S.QT.next()
    gens = []
    for i in range(4):
        tile = m * 4 + i
        if mine:
            cs = S.cs_mine[:, tile - NTP]
            r0 = t0 - TP + i * 128
            nk_ap = dr["nk_mine"][r0:r0 + 128, hp2 * 256:(hp2 + 1) * 256]
            nv_ap = dr["nv_mine"][r0:r0 + 128, hp2 * 256:(hp2 + 1) * 256]
        else:
            cs = S.cs_prev[:, tile]
            nk_ap = nv_ap = None
        gens.append(qkv_tile(K, G, S, hT, i * 128, 128, cs, mine, QT, i * 128, S.KTm[m], i * 128, S.Vam[m][:, i],
                             nk_ap, nv_ap, 128, blo=(4 if mine else 0), vscale=(None if mine else G.flag01)))
    if mine:
        run_rr(gens[0:2])
        run_rr(gens[2:4])
    else:
        run_rr(gens)
    import os
    if not mine or int(os.environ.get("HSTOP", 9)) < 3:
        return
    ml = m - TP // 512
    nk = NTP + (ml + 1) * 4
    ydfT = S.ydfT.next()
    for hl in range(2):
        acc = [K.banks[j] for j in range(4)]
        DEPTH = 1

        def stageA(kt):
            ktl = kt - NTP - ml * 4
            q0 = max(0, ktl) * 128
            kc0 = (kt % 4) * 128
            pb = 4 + 2 * (kt % 2)
            for comp in range(2):
                ph = slice(comp * 64, comp * 64 + 64)
                K.mm(K.banks[pb + comp][:, 0:512 - q0], S.KTm[kt // 4][ph, hl, kc0:kc0 + 128], QT[ph, hl, q0:512])

        pts = {}

        def stageB(kt):
            ktl = kt - NTP - ml * 4
            q0 = max(0, ktl) * 128
            Nq = 512 - q0
            pb = 4 + 2 * (kt % 2)
            PT = S.PT2.next()
            pts[kt] = PT
            K.act(PT[:, :, 0:Nq], K.bankpair(pb)[:, :, 0:Nq], AF.Exp,
                  bias=None)
            if ktl >= 0:
                K.memset(POOL, PT[64:128, :, 0:64], 0.0)

        def stageC(kt):
            ktl = kt - NTP - ml * 4
            q0 = max(0, ktl) * 128
            Vs = S.Vam[kt // 4]
            PT = pts.pop(kt)
            for comp in range(2):
                for j in range(q0 // 128, 4):
                    lastkt = NTP + ml * 4 + j
                    K.mm(acc[j][:, comp * 256:comp * 256 + 129], PT[:, comp, j * 128 - q0:j * 128 - q0 + 128],
                         Vs[:, kt % 4, hl, 0:129], start=(kt == 0 and comp == 0),
                         stop=(kt == lastkt and comp == 1))

        for n in range(nk + 2):
            if n < nk:
                stageA(n)
            if 0 <= n - 1 < nk:
                stageB(n - 1)
            if n - 2 >= 0:
                stageC(n - 2)
        for j in range(4):
            attn_epilogue(K, G, S, acc[j], 128, ydfT[:, hl, j * 128:(j + 1) * 128])
    fo = 4 + hp2 * 2
    K.store(dr["ymT_p"][:, fo:fo + 2, t0 - TP:t0 - TP + 512], ydfT)


def sweepH_sample(K, G, S, dr, hp2, PAST):
    NPT = PAST // 128
    hT = S.hT.next()
    K.load(hT[:, :, 0:64], dr["hT_s"])
    QT = S.QT.next()
    run_rr([qkv_tile(K, G, S, hT, 0, 64, S.cs_smp[:, 0], True, QT, 0, S.KTs, PAST, S.Vs[:, NPT],
                     dr["nk_smp"][:, hp2 * 256:(hp2 + 1) * 256], dr["nv_smp"][:, hp2 * 256:(hp2 + 1) * 256], 16)])
    import os
    SSTOP = int(os.environ.get("SSTOP", 9))
    for ct in range(NPT if SSTOP >= 2 else 0):
        ck = S.ck.next()
        K.load(ck, dr["cache_k"][ct * 128:(ct + 1) * 128, hp2 * 256:(hp2 + 1) * 256])
        ckb = S.ckb.next()
        K.cp(POOL, ckb, ck)
        bT = K.bank(4, 8)
        bTb = bT.bc(BF16)
        for hl in range(2):
            K.tr(bTb[:, hl * 128:(hl + 1) * 128], ckb[:, hl * 128:(hl + 1) * 128], G.identb)
        K.cp(DVE, S.KTs[:, :, ct * 128:(ct + 1) * 128], bTb[:, 0:256].rr("p (a t) -> p a t", t=128))
        cv = S.ck.next()
        K.load(cv, dr["cache_v"][ct * 128:(ct + 1) * 128, hp2 * 256:(hp2 + 1) * 256])
        K.cp(POOL, S.Vs[:, ct, :, 0:128], cv.rr("p (h d) -> p h d", d=128))
    ydfT = S.ydfT.next()
    for hl in range(2):
        acc = K.banks[hl]
        nk = NPT + 1
        pts = {}

        def sA(kt):
            pb = 4 + 2 * (kt % 2)
            for comp in range(2):
                ph = slice(comp * 64, comp * 64 + 64)
                K.mm(K.banks[pb + comp][:, 0:128], S.KTs[ph, hl, kt * 128:kt * 128 + 128], QT[ph, hl, 0:128])

        def sB(kt):
            pb = 4 + 2 * (kt % 2)
            PT = S.PT2.next()
            pts[kt] = PT
            K.act(PT[:, :, 0:128], K.bankpair(pb)[:, :, 0:128], AF.Exp,
                  bias=(G.cols[:, C_NEGK:C_NEGK + 1] if kt == NPT else None))

        def sC(kt):
            PT = pts.pop(kt)
            for comp in range(2):
                K.mm(acc[:, comp * 256:comp * 256 + 129], PT[:, comp, 0:128], S.Vs[:, kt, hl, 0:129],
                     start=(kt == 0 and comp == 0), stop=(kt == NPT and comp == 1))

        for n in range(nk + 2):
            if n < nk:
                sA(n)
            if 0 <= n - 1 < nk:
                sB(n - 1)
            if n - 2 >= 0:
                sC(n - 2)
        if SSTOP >= 4:
            attn_epilogue(K, G, S, acc[0:16], 16, ydfT[:, hl, 0:16])
    fo = 4 + hp2 * 2
    if True:
        K.store(dr["ymT_s"][:, fo:fo + 2, 0:16], ydfT[:, :, 0:16])


FF_PIECES = [(0, 6), (6, 6), (12, 6), (18, 4)]


def wup_convert(K, G, dr, stage, ost):
    i = 0
    for pi, (g0, n) in enumerate(FF_PIECES):
        for isup in range(2):
            c0 = (isup * 22 + g0) * 128
            for kc in range(8):
                st = stage.next()
                K.load(st[:, 0:n * 128], dr["w_up"][kc * 128:(kc + 1) * 128, c0:c0 + n * 128])
                ob = ost.next()
                sc = G.cols[:, C_G2 + kc:C_G2 + kc + 1]
                if i % 2 == 0:
                    K.act(ob[:, 0:n * 128], st[:, 0:n * 128], AF.Copy, scale=sc)
                else:
                    K.ts(DVE, ob[:, 0:n * 128], st[:, 0:n * 128], sc, ALU.mult)
                i += 1
                K.store(dr["wup_bf"][pi * 2 + isup, :, kc * 768:kc * 768 + n * 128], ob[:, 0:n * 128])
                yield


def phaseF_alloc(K, G, dr, wup_done=False):
    S = NS()
    S.Wo = K.sb("Wo", [128, 8, D], BF16)
    S.Wd = K.sb("Wd", [128, 22, D], BF16)
    mark = K.top
    stage = Ring(K, "wstF", 2, [128, 1792])
    load_weight(K, S.Wo, dr["w_out"], 1024, D, stage)
    load_weight(K, S.Wd, dr["w_down"], DFF, D, stage)
    if not wup_done:
        ost = Ring(K, "wupo", 2, [128, 768], BF16)
        for _ in wup_convert(K, G, dr, stage, ost):
            pass
    K.P.barrier()
    K.recycle()
    K.top = mark
    S.wb = Ring(K, "wb", 3, [128, 8, 768], BF16)
    S.ymT = K.sb("ymTF", [128, 8, 512], BF16)
    S.x = Ring(K, "xF", 2, [128, D])
    S.xmid = K.sb("xmid", [128, 4, D])
    S.ss = Ring(K, "ssF", 4, [128, 4])
    S.h2b = K.sb("h2b", [128, D], BF16)
    S.h2T = K.sb("h2T", [128, 8, 512], BF16)
    S.uS = Ring(K, "uS", 3, [128, 520])
    S.ct = Ring(K, "ct", 3, [128, 512])
    S.sg = K.sb("sg", [128, 6, 512], BF16)
    S.mT = K.sb("mT", [128, 22, 512], BF16)
    S.cvo = K.sb("cvo", [128, 128])
    return S


def phaseF_mt(K, G, S, dr, ucarry, ym_ap, x_ap, y_ap, N):
    cols = G.cols
    TPt = min(N, 128)
    NTT = max(1, N // 128)
    K.load(S.ymT[:, :, 0:N], ym_ap)
    xm_tiles = []
    for i in range(NTT):
        xt = S.x.next()
        K.load(xt[0:TPt], x_ap[i * 128:i * 128 + TPt, :])
        xm = S.xmid[0:TPt, i].alias(f"xmid{i}")
        xm_tiles.append(xm)
        for half in range(2):
            hs = slice(half * 512, (half + 1) * 512)
            bk = K.bank()
            for kc in range(8):
                K.mm(bk[0:TPt, :], S.ymT[:, kc, i * 128:i * 128 + TPt], S.Wo[:, kc, hs], start=(kc == 0), stop=(kc == 7))
            K.tt(DVE, xm[:, hs], xt[0:TPt, hs], bk[0:TPt, :], ALU.add)
        ss = S.ss.next()
        K.act(G.junk.alias("j")[0:TPt], xm, AF.Square, accum=ss[0:TPt, 0:1])
        K.act(ss[0:TPt, 1:2], ss[0:TPt, 0:1], AF.Sqrt, scale=1.0 / D, bias=G.epsc[0:TPt, 0:1])
        K.recip(ss[0:TPt, 2:3], ss[0:TPt, 1:2])
        K.act(S.h2b[0:TPt], xm, AF.Copy, scale=ss[0:TPt, 2:3])
        bk = K.bank()
        bkb = bk.bc(BF16)
        for kc in range(8):
            K.tr(bkb[:, kc * 128:kc * 128 + TPt], S.h2b[0:TPt, kc * 128:(kc + 1) * 128], G.identb[0:TPt, 0:TPt])
        K.cp(DVE, S.h2T[:, :, i * 128:i * 128 + TPt], bkb.rr("p (k t) -> p k t", t=128)[:, :, 0:TPt])
    for pi, (g0, n) in enumerate(FF_PIECES):
        for isup in range(2):
            wb = S.wb.next()
            pending = None
            K.load(wb[:, :, 0:n * 128],
                   dr["wup_bf"][pi * 2 + isup].rearrange("p (k c) -> p k c", c=768)[:, :, 0:n * 128], eng=SP)
            for f in range(n):
                fc = isup * 22 + g0 + f
                bk = K.bank()
                for kc in range(8):
                    K.mm(bk[:, 0:N], wb[:, kc, f * 128:(f + 1) * 128], S.h2T[:, kc, 0:N], start=(kc == 0), stop=(kc == 7))
                uS = S.uS.next()
                K.cp(POOL, uS[:, 0:2], ucarry[:, 2 * fc:2 * fc + 2])
                K.cp(ACT, uS[:, 2:N + 2], bk[:, 0:N])
                K.cp(POOL, ucarry[:, 2 * fc:2 * fc + 2], uS[:, N:N + 2])
                ct = S.ct.next()[:, 0:N]
                if (not isup) and f % 2 == 1:
                    K.ts(DVE, ct, uS[:, 2:N + 2], cols[:, C_CW + 88 + fc:C_CW + 88 + fc + 1], ALU.mult,
                         cols[:, C_CB + fc:C_CB + fc + 1], ALU.add)
                else:
                    K.act(ct, uS[:, 2:N + 2], AF.Identity, scale=cols[:, C_CW + 88 + fc:C_CW + 88 + fc + 1],
                          bias=cols[:, C_CB + fc:C_CB + fc + 1])
                K.stt(ct, uS[:, 1:N + 1], cols[:, C_CW + 44 + fc:C_CW + 44 + fc + 1], ct, ALU.mult, ALU.add)
                K.stt(ct, uS[:, 0:N], cols[:, C_CW + fc:C_CW + fc + 1], ct, ALU.mult, ALU.add)
                if pending is not None:
                    pending()

                def fin(ct=ct, f=f, isup=isup, g0=g0):
                    if not isup:
                        K.act(S.sg[:, f, 0:N], ct, AF.Silu)
                    else:
                        K.tt(POOL, S.mT[:, g0 + f, 0:N], ct, S.sg[:, f, 0:N], ALU.mult)
                pending = fin
            if pending is not None:
                pending()
                pending = None
    for i in range(NTT):
        xm = xm_tiles[i]
        for half in range(2):
            hs = slice(half * 512, (half + 1) * 512)
            bk = K.bank()
            for f in range(22):
                K.mm(bk[0:TPt, :], S.mT[:, f, i * 128:i * 128 + TPt], S.Wd[:, f, hs], start=(f == 0), stop=(f == 21))
            K.tt(DVE, xm[:, hs], xm[:, hs], bk[0:TPt, :], ALU.add)
        K.store(y_ap[i * 128:i * 128 + TPt, :], xm)


def conv_out(K, G, S, ucarry, out_ap):
    bk = K.bank()
    K.mm(bk[0:88, 0:128], ucarry, G.ident)
    K.cp(ACT, S.cvo[0:88], bk[0:88, 0:128])
    K.store(out_ap, S.cvo[0:88])
```
